# Optimizing a Trainium2 kernel written in Bass

```python
import math
import jax, jax.numpy as jnp
from jax import lax
import numpy as np

D_MODEL = 2048
BATCH = 4
SEQ = 8192
DEPTH = 1

CHUNK = 64
Q_BLOCK = 128
ATT_HEADS = 4
ATT_HEAD_DIM = 128
ATT_V_DIM = 2 * ATT_HEAD_DIM
ATT_WIDTH = ATT_HEADS * ATT_V_DIM
ROPE_THETA = 10000.0
CONV_CH = D_MODEL - ATT_WIDTH
CONV_WIDTH = 31
QK_COLS = ATT_HEADS * 2 * ATT_HEAD_DIM
IN_COLS = 2 * QK_COLS + ATT_WIDTH + 2 * CONV_CH
MIX_WIDTH = ATT_WIDTH + CONV_CH
FFN_HIDDEN = int(math.ceil(8 * D_MODEL / 3 / 256) * 256)
EPS = 1e-6
LN_EPS = 1e-5

kernel_name = "hybrid_diffattn_conformerconv_block"


def rms_norm(x, g, eps=EPS):
    xf = x.astype(jnp.float32)
    y = xf * lax.rsqrt(jnp.mean(xf * xf, axis=-1, keepdims=True) + eps)
    return (y * g.astype(jnp.float32)).astype(x.dtype)


def layer_norm(x, g, b, eps=LN_EPS):
    xf = x.astype(jnp.float32)
    mu = jnp.mean(xf, axis=-1, keepdims=True)
    var = jnp.mean(jnp.square(xf - mu), axis=-1, keepdims=True)
    y = (xf - mu) * lax.rsqrt(var + eps)
    return (y * g.astype(jnp.float32) + b.astype(jnp.float32)).astype(x.dtype)


def rope_tables(seq, dim):
    inv_freq = ROPE_THETA ** (-jnp.arange(0, dim, 2, dtype=jnp.float32) / dim)
    ang = jnp.arange(seq, dtype=jnp.float32)[:, None] * inv_freq[None, :]
    ang = jnp.concatenate([ang, ang], axis=-1)
    return jnp.cos(ang), jnp.sin(ang)


def apply_rope(t, cos, sin):
    half = t.shape[-1] // 2
    t1, t2 = t[..., :half], t[..., half:]
    rot = jnp.concatenate([-t2, t1], axis=-1)
    c = cos[None, :, None, None, :].astype(t.dtype)
    s = sin[None, :, None, None, :].astype(t.dtype)
    return t * c + rot * s


def diff_attention(q, k, v, lam):
    B, S, H, _, d = q.shape
    E = v.shape[-1]
    nb = S // Q_BLOCK
    scale = 1.0 / math.sqrt(d)
    qb = q.reshape(B, nb, Q_BLOCK, H, 2, d).transpose(1, 0, 2, 3, 4, 5)
    key_chunk = jnp.arange(S) // CHUNK

    def one_block(args):
        q_blk, i = args
        q_chunk = (i * Q_BLOCK + jnp.arange(Q_BLOCK)) // CHUNK
        s = jnp.einsum('bqhcd,bkhcd->bhcqk', q_blk, k).astype(jnp.float32) * scale
        mask = key_chunk[None, :] <= q_chunk[:, None]
        s = jnp.where(mask[None, None, None], s, -jnp.inf)
        p = jax.nn.softmax(s, axis=-1)
        w = p[:, :, 0] - lam * p[:, :, 1]
        return jnp.einsum('bhqk,bkhe->bqhe', w.astype(v.dtype), v)

    out = lax.map(one_block, (qb, jnp.arange(nb)))
    return out.transpose(1, 0, 2, 3, 4).reshape(B, S, H, E)


def causal_depthwise_conv(u, w, b):
    y = lax.conv_general_dilated(
        u, w.astype(u.dtype), window_strides=(1,), padding=[(CONV_WIDTH - 1, 0)],
        dimension_numbers=('NWC', 'WIO', 'NWC'), feature_group_count=u.shape[-1])
    return y + b.astype(u.dtype)


def setup_inputs(seed: int = 0) -> dict:
    key = jax.random.key(seed)
    ks = jax.random.split(key, 20)
    f32 = jnp.float32
    L = DEPTH
    d = ATT_HEAD_DIM
    nrm = lambda k, shape, s: jax.random.normal(k, shape, f32) * s
    return {
        "x": jax.random.normal(ks[0], (BATCH, SEQ, D_MODEL), f32),
        "norm1_g": 1.0 + nrm(ks[1], (L, D_MODEL), 0.02),
        "w_in": nrm(ks[2], (L, D_MODEL, IN_COLS), D_MODEL ** -0.5),
        "q_norm_g": 1.0 + nrm(ks[3], (L, d), 0.02),
        "k_norm_g": 1.0 + nrm(ks[4], (L, d), 0.02),
        "lambda_q1": nrm(ks[5], (L, d), 0.1),
        "lambda_k1": nrm(ks[6], (L, d), 0.1),
        "lambda_q2": nrm(ks[7], (L, d), 0.1),
        "lambda_k2": nrm(ks[8], (L, d), 0.1),
        "subln_g": 1.0 + nrm(ks[9], (L, ATT_V_DIM), 0.02),
        "conv_w": nrm(ks[10], (L, CONV_WIDTH, 1, CONV_CH), CONV_WIDTH ** -0.5),
        "conv_b": nrm(ks[11], (L, CONV_CH), 0.02),
        "conv_ln_g": 1.0 + nrm(ks[12], (L, CONV_CH), 0.02),
        "conv_ln_b": nrm(ks[13], (L, CONV_CH), 0.02),
        "w_out": nrm(ks[14], (L, MIX_WIDTH, D_MODEL), MIX_WIDTH ** -0.5),
        "norm2_g": 1.0 + nrm(ks[15], (L, D_MODEL), 0.02),
        "w_gate": nrm(ks[16], (L, D_MODEL, FFN_HIDDEN), D_MODEL ** -0.5),
        "w_up": nrm(ks[17], (L, D_MODEL, FFN_HIDDEN), D_MODEL ** -0.5),
        "w_down": nrm(ks[18], (L, FFN_HIDDEN, D_MODEL), FFN_HIDDEN ** -0.5),
    }


def reference(x, norm1_g, w_in, q_norm_g, k_norm_g, lambda_q1, lambda_k1,
              lambda_q2, lambda_k2, subln_g, conv_w, conv_b, conv_ln_g,
              conv_ln_b, w_out, norm2_g, w_gate, w_up, w_down):
    B, S, D = x.shape
    H, d = ATT_HEADS, ATT_HEAD_DIM
    cos, sin = rope_tables(S, d)
    h = x
    for l in range(DEPTH):
        lambda_init = 0.8 - 0.6 * math.exp(-0.3 * l)
        xn = rms_norm(h, norm1_g[l])
        proj = jnp.einsum('bsd,dn->bsn', xn, w_in[l].astype(xn.dtype))
        q, k, v, ga, gg = jnp.split(
            proj, np.cumsum([QK_COLS, QK_COLS, ATT_WIDTH, CONV_CH]).tolist(), axis=-1)
        q = q.reshape(B, S, H, 2, d)
        k = k.reshape(B, S, H, 2, d)
        v = v.reshape(B, S, H, ATT_V_DIM)
        q = apply_rope(rms_norm(q, q_norm_g[l]), cos, sin)
        k = apply_rope(rms_norm(k, k_norm_g[l]), cos, sin)
        lam = (jnp.exp(jnp.sum(lambda_q1[l].astype(jnp.float32) * lambda_k1[l].astype(jnp.float32)))
               - jnp.exp(jnp.sum(lambda_q2[l].astype(jnp.float32) * lambda_k2[l].astype(jnp.float32)))
               + lambda_init)
        att = diff_attention(q, k, v, lam)
        att = rms_norm(att, subln_g[l]) * (1.0 - lambda_init)
        att = att.reshape(B, S, ATT_WIDTH)
        u = ga * jax.nn.sigmoid(gg)
        u = causal_depthwise_conv(u, conv_w[l], conv_b[l])
        u = jax.nn.silu(layer_norm(u, conv_ln_g[l], conv_ln_b[l]))
        mixed = jnp.concatenate([att, u], axis=-1)
        h = h + jnp.einsum('bsm,md->bsd', mixed, w_out[l].astype(mixed.dtype))
        hn = rms_norm(h, norm2_g[l])
        a = jnp.einsum('bsd,df->bsf', hn, w_gate[l].astype(hn.dtype))
        bu = jnp.einsum('bsd,df->bsf', hn, w_up[l].astype(hn.dtype))
        h = h + jnp.einsum('bsf,fd->bsd', jax.nn.silu(a) * bu, w_down[l].astype(hn.dtype))
    return h
```

```python
import math
from contextlib import ExitStack

import numpy as np
import concourse.bass as bass
import concourse.mybir as mybir
from concourse.bass_utils import run_bass_kernel_spmd

F32 = mybir.dt.float32
BF16 = mybir.dt.bfloat16
ALU = mybir.AluOpType
AF = mybir.ActivationFunctionType
AX = mybir.AxisListType

EPS = 1e-6
LN_EPS = 1e-5
CW = 31
LAMBDA_INIT = 0.8 - 0.6 * math.exp(-0.3 * 0)
NEG_BIG = -30000.0
SLOT = 4096


class Cfg:
    def __init__(s, D=2048, H=4, F=5632, S=8192, B=4, R=6, KR=4):
        s.D, s.H, s.F, s.S, s.B = D, H, F, S, B
        s.DC = D // 128
        s.ATT = H * 256
        s.AC = s.ATT // 128
        s.C = D - s.ATT
        s.CC = s.C // 128
        s.FC = F // 128
        s.NT = S // 512
        s.NOWN = s.NT // 2
        s.KW = H * 256
        s.INC = 3 * s.KW + 2 * s.C
        s.K0 = s.KW
        s.V0 = 2 * s.KW
        s.GA0 = 3 * s.KW
        s.GG0 = 3 * s.KW + s.C
        s.NDG = D // 512
        s.R = R
        s.KR = KR
        s.NCORES = 2 * B


def own_tile(p, j):
    return 2 * j + ((j + p) % 2)


class T:
    __slots__ = ("name", "lw", "rd")

    def __init__(self, name):
        self.name = name
        self.lw = None
        self.rd = []


class Op:
    __slots__ = ("eng", "fn", "deps", "signal", "dma", "sem", "val", "sigval")

    def __init__(self, eng, fn, dma):
        self.eng = eng
        self.fn = fn
        self.deps = []
        self.signal = False
        self.dma = dma
        self.sem = None
        self.val = 0
        self.sigval = 0


class _Rec:
    def __init__(self):
        self.calls = []

    def __getattr__(self, name):
        def f(*a, **k):
            self.calls.append((name, a, k))
            return self
        return f


class Prog:
    ENGS = ("pe", "act", "dve", "pool", "sp")
    NDS = 12

    def __init__(self):
        self.ops = {e: [] for e in self.ENGS}
        self.dma_cnt = {"pool": 0, "sp": 0}
        self.dma_last = {}

    def op(self, eng, fn, r=(), w=(), dma=False):
        rec = _Rec()
        fn(rec)
        assert len(rec.calls) == 1
        o = Op(eng, rec.calls[0], dma)
        deps = set()
        for t in r:
            if t.lw is not None:
                deps.add(t.lw)
        for t in w:
            if t.lw is not None:
                deps.add(t.lw)
            deps.update(t.rd)
        if dma:
            k = self.dma_cnt[eng]
            self.dma_cnt[eng] = k + 1
            o.sem = (eng, k % self.NDS)
            o.val = 16 * (k // self.NDS + 1)
            prev = self.dma_last.get(o.sem)
            if prev is not None:
                deps.add(prev)
            self.dma_last[o.sem] = o
        for d in deps:
            if d.eng == "pe" and eng == "pe" and not d.dma:
                continue
            d.signal = True
            o.deps.append(d)
        for t in r:
            t.rd.append(o)
        for t in w:
            t.lw = o
            t.rd = []
        self.ops[eng].append(o)
        return o

    def emit(self, nc, es):
        csem = {e: es.enter_context(nc.semaphore("c_" + e)) for e in ("pe", "act", "dve", "pool")}
        dsem = {}
        for q in ("pool", "sp"):
            for i in range(self.NDS):
                dsem[(q, i)] = es.enter_context(nc.semaphore("d_%s%d" % (q, i)))
        for e in self.ENGS:
            c = 0
            for o in self.ops[e]:
                if o.signal and not o.dma:
                    c += 1
                    o.sigval = c
        block = es.enter_context(nc.Block())

        def run(ename, eng):
            waited = {}
            for o in self.ops[ename]:
                for d in o.deps:
                    if d.dma:
                        sem, val, key = dsem[d.sem], d.val, d.sem
                    else:
                        sem, val, key = csem[d.eng], d.sigval, d.eng
                    if waited.get(key, 0) >= val:
                        continue
                    waited[key] = val
                    eng.wait_ge(sem, val)
                name, a, k = o.fn
                ins = getattr(eng, name)(*a, **k)
                if o.dma:
                    ins.then_inc(dsem[o.sem], 16)
                elif o.signal:
                    assert ins is not None
                    ins.then_inc(csem[ename], 1)

        @block.tensor
        def _(e):
            run("pe", e)

        @block.scalar
        def _(e):
            run("act", e)

        @block.vector
        def _(e):
            run("dve", e)

        @block.gpsimd
        def _(e):
            run("pool", e)

        @block.sync
        def _(e):
            run("sp", e)


def build(cfg, debug=False):
    c = cfg
    D, H, DC, AC, CC, FC, NT, NOWN = c.D, c.H, c.DC, c.AC, c.CC, c.FC, c.NT, c.NOWN
    nc = bass.Bass("TRN2", target_bir_lowering=False)
    P = Prog()
    es = ExitStack()
    E = es.enter_context

    def din(name, shape, dt=F32):
        return nc.dram_tensor(name, list(shape), dt, kind="ExternalInput").ap()

    xf = din("xf", [c.S, D])
    xo = din("xo", [NOWN, 544, D])
    w_in = din("w_in", [D, c.INC])
    w_out = din("w_out", [D, D])
    w_gate = din("w_gate", [D, c.F])
    w_up = din("w_up", [D, c.F])
    w_down = din("w_down", [c.F, D])
    g1_d = din("norm1_g", [128, DC])
    g2_d = din("norm2_g", [128, DC])
    qg_d = din("q_norm_g", [128, 1])
    kg_d = din("k_norm_g", [128, 1])
    lq1_d = din("lambda_q1", [128])
    lk1_d = din("lambda_k1", [128])
    lq2_d = din("lambda_q2", [128])
    lk2_d = din("lambda_k2", [128])
    sg_d = din("subln_g", [128, 2])
    cw_d = din("conv_w", [128, CC * CW])
    cb_d = din("conv_b", [128, CC])
    lng_d = din("conv_ln_g", [128, CC])
    lnb_d = din("conv_ln_b", [128, CC])
    cosk_d = din("cosk", [128, c.S])
    sink_d = din("sink", [128, c.S])
    cosq_d = din("cosq", [NOWN, 128, 512])
    sinq_d = din("sinq", [NOWN, 128, 512])
    ident_d = din("ident", [128, 128])
    ones_d = din("ones", [128, 128])
    perm_d = din("perm", [128, 128])
    mk_d = din("mk", [16, 1024])
    mq_d = din("mq", [16, NOWN * 512])
    y_d = nc.dram_tensor("y", [NOWN * 512, D], F32, kind="ExternalOutput").ap()

    NKB = NT
    skind = "ExternalOutput" if debug else "Internal"
    kts = nc.dram_tensor("kts", [H * NKB, 128, 1024], BF16, kind=skind).ap()
    vs = nc.dram_tensor("vs", [H * NKB, 128, 1024], BF16, kind=skind).ap()
    dbg_ts = []

    def dbg(name, ap, tiles, shape, dt):
        if not debug:
            return
        d = nc.dram_tensor("dbg_" + name, list(shape), dt, kind="ExternalOutput").ap()
        t_ = T("dbg_" + name)
        dbg_ts.append(t_)
        P.op("pool", lambda e: e.dma_start(out=d, in_=ap), r=list(tiles), w=[t_], dma=True)

    kts_t = [T("kts%d" % i) for i in range(H * NKB)]
    vs_t = [T("vs%d" % i) for i in range(H * NKB)]

    win_v = w_in.rearrange("(c p) n -> p c n", p=128)
    wout_v = w_out.rearrange("(c p) n -> p c n", p=128)
    wg_v = w_gate.rearrange("(c p) n -> p c n", p=128)
    wu_v = w_up.rearrange("(c p) n -> p c n", p=128)
    wd_v = w_down.rearrange("(f p) n -> p f n", p=128)
    units = []
    for hh in range(2 * H):
        units.append([(("q", hh), win_v[:, :, hh * 128:(hh + 1) * 128], DC, 128)])
    for cc in range(CC):
        units.append([(("ga", cc), win_v[:, :, c.GA0 + cc * 128:c.GA0 + (cc + 1) * 128], DC, 128),
                      (("gg", cc), win_v[:, :, c.GG0 + cc * 128:c.GG0 + (cc + 1) * 128], DC, 128)])
    for dg in range(c.NDG):
        for half in range(2):
            units.append([(("o", dg, half), wout_v[:, half * (DC // 2):(half + 1) * (DC // 2), dg * 512:(dg + 1) * 512], DC // 2, 512)])
    for fc in range(FC):
        units.append([(("g", fc), wg_v[:, :, fc * 128:(fc + 1) * 128], DC, 128),
                      (("u", fc), wu_v[:, :, fc * 128:(fc + 1) * 128], DC, 128)])
    dblocks = []
    for dg in range(c.NDG):
        f0 = 0
        while f0 < FC:
            nf = min(8, FC - f0)
            units.append([(("d", dg, f0), wd_v[:, f0:f0 + nf, dg * 512:(dg + 1) * 512], nf, 512)])
            dblocks.append((dg, f0, nf))
            f0 += nf
    slots = []
    windex = {}
    cur, used = [], 0
    for u in units:
        sz = sum(a * b for (_, _, a, b) in u)
        if used + sz > SLOT and cur:
            slots.append(cur)
            cur, used = [], 0
        for (key, src, a, b) in u:
            cur.append((key, src, a, b, used))
            windex[key] = (len(slots), used, a, b)
            used += a * b
    if cur:
        slots.append(cur)
    NSLOT = len(slots)
    slot_used = [max(off + a * b for (_, _, a, b, off) in s) for s in slots]
    wscr = nc.dram_tensor("wscr", [NSLOT, 128, SLOT], BF16, kind=skind).ap()
    wscr_t = [T("wscr%d" % i) for i in range(NSLOT)]

    def sb(name, shape, dt):
        return E(nc.sbuf_tensor("s_" + name, list(shape), dt))

    GR = 512
    A1 = 4 * D * 2
    YC0 = A1
    QT0 = YC0 + CC * 1024
    KV0 = QT0 + 2 * H * 512
    KVSZ = 2048
    main_sz = max(A1 + FC * 512, KV0 + c.KR * KVSZ)
    WK1 = DC * 2 * c.KW
    VST0 = WK1 + 2048
    pa_sz = VST0 + 4 * c.KW
    AR = ((max(main_sz, pa_sz) + GR - 1) // GR) * GR
    assert CC * 1536 <= A1
    ar = sb("arena", [128, AR], BF16)
    gr = [T("gr%d" % i) for i in range(AR // GR)]

    def G(off, n):
        return gr[off // GR:(off + n - 1) // GR + 1]

    def hT(q, dg=None):
        if dg is None:
            return ar[:, q * D * 2:(q + 1) * D * 2].bitcast(F32), G(q * D * 2, D * 2)
        o = (q * D + dg * 512) * 2
        return ar[:, o:o + 1024].bitcast(F32), G(o, 1024)

    def uext(cc):
        o = cc * 1536
        return ar[:, o:o + 1088].bitcast(F32), G(o, 1088)

    def actT(fc):
        o = A1 + fc * 512
        return ar[:, o:o + 512], G(o, 512)

    def ycv(cc):
        o = YC0 + cc * 1024
        return ar[:, o:o + 1024].bitcast(F32), G(o, 1024)

    def QT(hh):
        o = QT0 + hh * 512
        return ar[:, o:o + 512], G(o, 512)

    def ktr(r_):
        o = KV0 + r_ * KVSZ
        return ar[:, o:o + 1024].rearrange("p (c k) -> p c k", c=2), G(o, 1024)

    def vr(r_):
        o = KV0 + r_ * KVSZ + 1024
        return ar[:, o:o + 1024].rearrange("p (k e) -> p k e", k=4), G(o, 1024)

    wkv_ap = ar[:, 0:WK1].rearrange("p (c n) -> p c n", c=DC)

    def wkvG(ch, col, n):
        return G(ch * 2 * c.KW + col, n)

    def kst(i):
        o = WK1 + i * 1024
        return ar[:, o:o + 1024].rearrange("p (c k) -> p c k", c=2), G(o, 1024)

    vst = ar[:, VST0:VST0 + 4 * c.KW].rearrange("p (k e) -> p k e", k=4)
    vst_g = G(VST0, 4 * c.KW)

    ring = [sb("ring%d" % i, [128, SLOT], BF16) for i in range(c.R)]
    ring_t = [T("ring%d" % i) for i in range(c.R)]
    NXS = 2
    xs = [sb("xs%d" % i, [128, D], F32) for i in range(NXS)]
    xs_t = [T("xs%d" % i) for i in range(NXS)]
    xn = [sb("xn%d" % i, [128, D], BF16) for i in range(2)]
    xn_t = [T("xn%d" % i) for i in range(2)]
    st = sb("st", [128, 16], F32)
    st_t = [T("st%d" % i) for i in range(16)]
    XT = sb("XT", [128, DC, 544], BF16)
    XT_t = [T("XT%d" % i) for i in range(DC)]
    XTh_t = T("XTh")
    NPT = 3
    pt = [sb("pt%d" % i, [128, 512], BF16) for i in range(NPT)]
    pt_t = [T("pt%d" % i) for i in range(NPT)]
    NTMP = 8
    tmp = [sb("tmp%d" % i, [128, 512], F32) for i in range(NTMP)]
    tmp_t = [T("tmp%d" % i) for i in range(NTMP)]
    NTB = 6
    tb = [sb("tb%d" % i, [128, 512], BF16) for i in range(NTB)]
    tb_t = [T("tb%d" % i) for i in range(NTB)]
    cosb = sb("cosb", [128, 512], F32)
    sinb = sb("sinb", [128, 512], F32)
    cs_t = T("cs")
    g1T = sb("g1T", [128, DC], F32)
    g2T = sb("g2T", [128, DC], F32)
    gsbc = sb("gsbc", [128, 2], F32)
    qgc = sb("qgc", [128, 1], F32)
    kgc = sb("kgc", [128, 1], F32)
    cwt = sb("cwt", [128, CC * CW], F32)
    cbt = sb("cbt", [128, CC], F32)
    lngt = sb("lngt", [128, CC], F32)
    lnbt = sb("lnbt", [128, CC], F32)
    ident = sb("ident", [128, 128], BF16)
    ones = sb("ones", [128, 128], BF16)
    perm = sb("perm", [128, 128], BF16)
    mk = sb("mk", [16, 1024], BF16)
    mqb = sb("mqb", [16, 512], BF16)
    mqb_t = T("mqb")
    const_t = T("const")
    lam_t = T("lam")
    epsc = sb("epsc", [128, 2], F32)
    eps_t = T("eps")
    psf = [E(nc.psum_tensor("psf%d" % i, [128, 512], F32)) for i in range(8)]
    psf_t = [T("psf%d" % i) for i in range(8)]
    psT_l = [psf[7][:].bitcast(BF16), psf[6][:].bitcast(BF16)]
    psT_tl = [psf_t[7], psf_t[6]]
    psT_i = {"i": 0}

    def dma(q, out, in_, r=(), w=()):
        return P.op(q, lambda e: e.dma_start(out=out, in_=in_), r=r, w=w, dma=True)

    rr = {"tmp": 0, "tb": 0, "pt": 0, "xs": 0, "xn": 0, "st": 0}

    def nxt(kind, n):
        i = rr[kind]
        rr[kind] = (i + 1) % n
        return i

    def new_tmp():
        i = nxt("tmp", NTMP)
        return tmp[i], tmp_t[i]

    def new_tb():
        i = nxt("tb", NTB)
        return tb[i], tb_t[i]

    cl = [
        (g1T[:], g1_d), (g2T[:], g2_d),
        (gsbc[:], sg_d), (qgc[:], qg_d), (kgc[:], kg_d),
        (cwt[:], cw_d), (cbt[:], cb_d), (lngt[:], lng_d), (lnbt[:], lnb_d),
        (ident[:], ident_d), (ones[:], ones_d), (perm[:], perm_d), (mk[:], mk_d),
    ]
    for (o_, i_) in cl:
        dma("pool", o_, i_, w=[const_t])
    P.op("dve", lambda e: e.memset(epsc[:, 0:1], EPS), w=[eps_t])
    P.op("dve", lambda e: e.memset(epsc[:, 1:2], LN_EPS), w=[eps_t])
    P.op("dve", lambda e: e.tensor_scalar(out=gsbc[:], in0=gsbc[:], scalar1=1.0 - LAMBDA_INIT, scalar2=None,
                                          op0=ALU.mult), r=[const_t], w=[const_t])
    LS = 8
    for k, (da, db) in enumerate(((lq1_d, lk1_d), (lq2_d, lk2_d))):
        ta, ta_t = new_tmp()
        tb2, tb2_t = new_tmp()
        dma("pool", ta[:, 0:128], da.partition_broadcast(128), w=[ta_t])
        dma("pool", tb2[:, 0:128], db.partition_broadcast(128), w=[tb2_t])
        P.op("dve", lambda e, ta=ta, tb2=tb2: e.tensor_tensor(out=ta[:, 128:256], in0=ta[:, 0:128], in1=tb2[:, 0:128], op=ALU.mult),
             r=[ta_t, tb2_t], w=[ta_t])
        P.op("dve", lambda e, k=k, ta=ta: e.reduce_sum(out=st[:, LS + k:LS + k + 1], in_=ta[:, 128:256], axis=AX.X),
             r=[ta_t], w=[st_t[LS + k]])
        P.op("act", lambda e, k=k: e.activation(out=st[:, LS + 2 + k:LS + 3 + k], in_=st[:, LS + k:LS + k + 1], func=AF.Exp),
             r=[st_t[LS + k]], w=[st_t[LS + 2 + k]])
    P.op("dve", lambda e: e.tensor_tensor(out=st[:, LS + 4:LS + 5], in0=st[:, LS + 3:LS + 4], in1=st[:, LS + 2:LS + 3],
                                          op=ALU.subtract), r=[st_t[LS + 2], st_t[LS + 3]], w=[st_t[LS + 4]])
    P.op("dve", lambda e: e.tensor_scalar(out=st[:, LS + 5:LS + 6], in0=st[:, LS + 4:LS + 5], scalar1=-LAMBDA_INIT,
                                          scalar2=None, op0=ALU.add), r=[st_t[LS + 4]], w=[lam_t])
    nlam = st[:, LS + 5:LS + 6]

    def rstd_from_ss(ss_ap, ss_t, out_ap, out_t, n, eps):
        np_ = ss_ap.shape[0]
        bcol = epsc[:np_, 0:1] if eps == EPS else epsc[:np_, 1:2]
        P.op("act", lambda e: e.activation(out=out_ap, in_=ss_ap, func=AF.Ln, scale=1.0 / n, bias=bcol),
             r=[ss_t, eps_t], w=[out_t])
        P.op("act", lambda e: e.activation(out=out_ap, in_=out_ap, func=AF.Exp, scale=-0.5),
             r=[out_t], w=[out_t])

    def norm_transpose(src_ap, src_ts, np_, gT, col0, halo=False):
        si = nxt("st", 4)
        ssc, ssc_t = st[:np_, si:si + 1], st_t[si]
        rsc, rsc_t = st[:np_, 4 + si:5 + si], st_t[4 + si]
        xi = nxt("xn", 2)
        P.op("act", lambda e: e.activation(out=xn[xi][:np_, :], in_=src_ap, func=AF.Square, accum_out=ssc),
             r=list(src_ts), w=[xn_t[xi], ssc_t])
        rstd_from_ss(ssc, ssc_t, rsc, rsc_t, D, EPS)
        P.op("act", lambda e: e.activation(out=xn[xi][:np_, :], in_=src_ap, func=AF.Copy, scale=rsc),
             r=list(src_ts) + [rsc_t], w=[xn_t[xi]])
        for c0 in range(0, DC, 8):
            n = min(8, DC - c0)
            ti = psT_i["i"] % 2
            psT_i["i"] += 1
            psT, psT_t = psT_l[ti], psT_tl[ti]
            for k in range(n):
                P.op("pe", lambda e: e.transpose(out=psT[:, k * np_:(k + 1) * np_],
                                                 in_=xn[xi][:np_, (c0 + k) * 128:(c0 + k + 1) * 128],
                                                 identity=ident[:np_, :np_]),
                     r=[xn_t[xi], const_t], w=[psT_t])
            P.op("dve", lambda e: e.tensor_tensor(
                out=XT[:, c0:c0 + n, col0:col0 + np_], in0=psT[:, 0:n * np_].rearrange("p (c t) -> p c t", c=n),
                in1=gT[:, c0:c0 + n].unsqueeze(2).to_broadcast([128, n, np_]), op=ALU.mult),
                 r=[psT_t, const_t], w=([XTh_t] if halo else XT_t[c0:c0 + n]))

    sbank = {"i": 0}

    def stream_bank(banks):
        i = sbank["i"]
        sbank["i"] = i + 1
        b = banks[i % len(banks)]
        return psf[b], psf_t[b]

    def qk_part1(ps, ps_t, gcol):
        sq, sq_t = new_tb()
        kg, kg_t = new_tb()
        P.op("act", lambda e: e.activation(out=sq[:], in_=ps[:], func=AF.Square), r=[ps_t], w=[sq_t])
        P.op("act", lambda e: e.activation(out=kg[:], in_=ps[:], func=AF.Copy, scale=gcol), r=[ps_t, const_t], w=[kg_t])
        return (sq, sq_t, kg, kg_t)

    def qk_part2(st1, out_ap, out_ts, banks):
        sq, sq_t, kg, kg_t = st1
        pa, pa_t = stream_bank(banks)
        P.op("pe", lambda e: e.matmul(pa[:], lhsT=ones[:], rhs=sq[:], start=True, stop=True), r=[sq_t, const_t], w=[pa_t])
        pb, pb_t = stream_bank(banks)
        P.op("pe", lambda e: e.matmul(pb[:], lhsT=perm[:], rhs=kg[:], start=True, stop=True), r=[kg_t, const_t], w=[pb_t])
        rs, rs_t = new_tmp()
        rstd_from_ss(pa[:], pa_t, rs[:], rs_t, 128, EPS)
        t1, t1_t = new_tmp()
        t2, t2_t = new_tmp()
        P.op("dve", lambda e: e.tensor_tensor(out=t1[:], in0=kg[:], in1=cosb[:], op=ALU.mult), r=[kg_t, cs_t], w=[t1_t])
        P.op("dve", lambda e: e.tensor_tensor(out=t2[:], in0=pb[:], in1=sinb[:], op=ALU.mult), r=[pb_t, cs_t], w=[t2_t])
        P.op("dve", lambda e: e.tensor_tensor(out=t1[:], in0=t1[:], in1=t2[:], op=ALU.add), r=[t1_t, t2_t], w=[t1_t])
        P.op("dve", lambda e: e.tensor_tensor(out=out_ap, in0=t1[:], in1=rs[:], op=ALU.mult), r=[t1_t, rs_t], w=list(out_ts))

    for ch in range(DC):
        dma("pool", wkv_ap[:, ch, :], win_v[:, ch, c.K0:c.K0 + 2 * c.KW], w=wkvG(ch, 0, 2 * c.KW))

    def prep_slot(n):
        rb, rb_t = ring[n % c.R], ring_t[n % c.R]
        for (key, src, a, b, off) in slots[n]:
            dma("pool", rb[:, off:off + a * b].rearrange("p (a b) -> p a b", a=a), src, w=[rb_t])
        dma("sp", wscr[n][:, 0:slot_used[n]], rb[:, 0:slot_used[n]], r=[rb_t], w=[wscr_t[n]])

    prep_next = {"n": 0}

    def prep_some(k):
        for _ in range(k):
            if prep_next["n"] < NSLOT:
                prep_slot(prep_next["n"])
                prep_next["n"] += 1

    A_BANKS = [0, 1, 2, 3, 4, 5]
    prep_per_tile = (NSLOT + NT - 1) // NT
    for t in range(NT):
        dma("sp", cosb[:], cosk_d[:, t * 512:(t + 1) * 512], w=[cs_t])
        dma("sp", sinb[:], sink_d[:, t * 512:(t + 1) * 512], w=[cs_t])
        for sub in range(4):
            xi = nxt("xs", NXS)
            dma("sp", xs[xi][:], xf[t * 512 + sub * 128:t * 512 + (sub + 1) * 128, :], w=[xs_t[xi]])
            norm_transpose(xs[xi][:], [xs_t[xi]], 128, g1T, 32 + sub * 128)
        pend = None
        for hh in range(2 * H + 1):
            cur = None
            if hh < 2 * H:
                ps, ps_t = stream_bank(A_BANKS)
                for ch in range(DC):
                    P.op("pe", lambda e: e.matmul(ps[:], lhsT=wkv_ap[:, ch, hh * 128:(hh + 1) * 128],
                                                  rhs=XT[:, ch, 32:544], start=(ch == 0), stop=(ch == DC - 1)),
                         r=[XT_t[ch]] + wkvG(ch, hh * 128, 128), w=[ps_t])
                cur = (hh, qk_part1(ps, ps_t, kgc[:, 0:1]))
            if pend is not None:
                ph_, st1 = pend
                h, cpart = ph_ // 2, ph_ % 2
                ki = (t * H + h) % 2
                ks_ap, ks_g = kst(ki)
                qk_part2(st1, ks_ap[:, cpart, :], ks_g, A_BANKS)
                if cpart == 1:
                    dma("sp", kts[h * NKB + t].rearrange("p (c k) -> p c k", c=2), ks_ap, r=ks_g, w=[kts_t[h * NKB + t]])
            pend = cur
        VW = min(512, c.KW)
        for sub in range(4):
            for half in range(c.KW // VW):
                ps, ps_t = stream_bank(A_BANKS)
                for ch in range(DC):
                    P.op("pe", lambda e, ps=ps, ch=ch, sub=sub, half=half: e.matmul(
                        ps[:, 0:VW], lhsT=XT[:, ch, 32 + sub * 128:32 + (sub + 1) * 128],
                        rhs=wkv_ap[:, ch, c.KW + half * VW:c.KW + (half + 1) * VW], start=(ch == 0), stop=(ch == DC - 1)),
                         r=[XT_t[ch]] + wkvG(ch, c.KW + half * VW, VW), w=[ps_t])
                P.op("act", lambda e, ps=ps, sub=sub, half=half: e.activation(out=vst[:, sub, half * VW:(half + 1) * VW],
                                                                               in_=ps[:, 0:VW], func=AF.Copy),
                     r=[ps_t], w=vst_g)
        for h in range(H):
            dma("sp", vs[h * NKB + t].rearrange("p (k e) -> p k e", k=4), vst[:, :, h * 256:(h + 1) * 256],
                r=vst_g, w=[vs_t[h * NKB + t]])

    wnext = {"g": 0}
    TOT = NOWN * NSLOT

    def wload(gi):
        n = gi % NSLOT
        rb, rb_t = ring[gi % c.R], ring_t[gi % c.R]
        if gi < NSLOT:
            for (key, src, a, b, off) in slots[n]:
                dma("pool", rb[:, off:off + a * b].rearrange("p (a b) -> p a b", a=a), src, w=[rb_t])
            dma("sp", wscr[n][:, 0:slot_used[n]], rb[:, 0:slot_used[n]], r=[rb_t], w=[wscr_t[n]])
        else:
            dma("sp", rb[:, 0:slot_used[n]], wscr[n][:, 0:slot_used[n]], r=[wscr_t[n]], w=[rb_t])

    def wblock(j, key):
        n, off, a, b = windex[key]
        gi = j * NSLOT + n
        while wnext["g"] <= min(gi + c.R - 1, TOT - 1):
            wload(wnext["g"])
            wnext["g"] += 1
        return ring[gi % c.R][:, off:off + a * b].rearrange("p (a b) -> p a b", a=a), ring_t[gi % c.R]

    kvseq = []
    for j in range(NOWN):
        for h in range(H):
            for kb in range(2 * j + 2):
                kvseq.append((j, h, kb))
    kvpos = {k: i for i, k in enumerate(kvseq)}
    kvnext = {"g": 0}

    def kvload(gi):
        (j, h, kb) = kvseq[gi]
        r_ = gi % c.KR
        k_ap, k_g = ktr(r_)
        v_ap, v_g = vr(r_)
        dma("pool", k_ap, kts[h * NKB + kb].rearrange("p (c k) -> p c k", c=2), r=[kts_t[h * NKB + kb]], w=k_g)
        dma("pool", v_ap, vs[h * NKB + kb].rearrange("p (k e) -> p k e", k=4), r=[vs_t[h * NKB + kb]], w=v_g)

    def kvblock(j, h, kb):
        gi = kvpos[(j, h, kb)]
        last_j = kvpos[(j, H - 1, 2 * j + 1)]
        while kvnext["g"] <= min(gi + c.KR - 2, last_j):
            kvload(kvnext["g"])
            kvnext["g"] += 1
        r_ = gi % c.KR
        k_ap, k_g = ktr(r_)
        v_ap, v_g = vr(r_)
        return k_ap, v_ap, k_g + v_g

    S_BANKS = [4, 5, 6]
    Q_BANKS = [0, 1, 2, 3, 4, 5]
    ESCALE = 1.0 / math.sqrt(128.0)
    out_ts = []
    for j in range(NOWN):
        dma("sp", cosb[:], cosq_d[j], w=[cs_t])
        dma("sp", sinb[:], sinq_d[j], w=[cs_t])
        dma("pool", mqb[:], mq_d[:, j * 512:(j + 1) * 512], w=[mqb_t])
        xi = nxt("xs", NXS)
        dma("sp", xs[xi][0:32, :], xo[j, 0:32, :], w=[xs_t[xi]])
        norm_transpose(xs[xi][0:32, :], [xs_t[xi]], 32, g1T, 0, halo=True)
        for sub in range(4):
            xi = nxt("xs", NXS)
            dma("sp", xs[xi][:], xo[j, 32 + sub * 128:32 + (sub + 1) * 128, :], w=[xs_t[xi]])
            norm_transpose(xs[xi][:], [xs_t[xi]], 128, g1T, 32 + sub * 128)
        dbg("xnT%d" % j, XT[:], XT_t + [XTh_t], [128, DC, 544], BF16)
        pend = None
        for hh in range(2 * H + 1):
            cur = None
            if hh < 2 * H:
                wq, wq_t = wblock(j, ("q", hh))
                ps, ps_t = stream_bank(Q_BANKS)
                for ch in range(DC):
                    P.op("pe", lambda e: e.matmul(ps[:], lhsT=wq[:, ch, :], rhs=XT[:, ch, 32:544],
                                                  start=(ch == 0), stop=(ch == DC - 1)),
                         r=[XT_t[ch], wq_t], w=[ps_t])
                cur = (hh, qk_part1(ps, ps_t, qgc[:, 0:1]))
            if pend is not None:
                ph_, st1 = pend
                q_ap, q_g = QT(ph_)
                qk_part2(st1, q_ap, q_g, Q_BANKS)
            pend = cur
        def conv_chunk(cc):
            u_ap, u_g = uext(cc)
            y_ap, y_g = ycv(cc)
            P.op("dve", lambda e: e.tensor_scalar(
                out=y_ap, in0=u_ap[:, 2:514], scalar1=cwt[:, cc * CW:cc * CW + 1], scalar2=cbt[:, cc:cc + 1],
                op0=ALU.mult, op1=ALU.add), r=u_g + [const_t], w=y_g)
            for tap in range(1, CW):
                P.op("dve", lambda e: e.scalar_tensor_tensor(
                    out=y_ap, in0=u_ap[:, 2 + tap:514 + tap], scalar=cwt[:, cc * CW + tap:cc * CW + tap + 1],
                    in1=y_ap, op0=ALU.mult, op1=ALU.add), r=u_g + y_g, w=y_g)

        for cc in range(CC):
            wa, wa_t = wblock(j, ("ga", cc))
            wg_, wg_t = wblock(j, ("gg", cc))
            pa, pa_t = stream_bank(Q_BANKS)
            pg, pg_t = stream_bank(Q_BANKS)
            ph, ph_t = stream_bank(Q_BANKS)
            for (w_, w_t, po, po_t, col) in ((wa, wa_t, pa, pa_t, 0), (wg_, wg_t, pg, pg_t, 32)):
                for ch in range(DC):
                    P.op("pe", lambda e, po=po, ch=ch, w_=w_: e.matmul(po[:], lhsT=w_[:, ch, :], rhs=XT[:, ch, 32:544],
                                                                       start=(ch == 0), stop=(ch == DC - 1)),
                         r=[XT_t[ch], w_t], w=[po_t])
                for ch in range(DC):
                    P.op("pe", lambda e, ch=ch, w_=w_, col=col, ph=ph: e.matmul(ph[:, col:col + 32], lhsT=w_[:, ch, :], rhs=XT[:, ch, 0:32],
                                                                               start=(ch == 0), stop=(ch == DC - 1)),
                         r=[XTh_t, w_t], w=[ph_t])
            u_ap, u_g = uext(cc)
            sg, sg_t = new_tmp()
            P.op("act", lambda e, sg=sg, pg=pg: e.activation(out=sg[:], in_=pg[:], func=AF.Sigmoid), r=[pg_t], w=[sg_t])
            P.op("dve", lambda e, sg=sg, pa=pa, u_ap=u_ap: e.tensor_tensor(out=u_ap[:, 32:544], in0=pa[:], in1=sg[:], op=ALU.mult),
                 r=[pa_t, sg_t], w=u_g)
            sh, sh_t = new_tmp()
            P.op("act", lambda e, sh=sh, ph=ph: e.activation(out=sh[:, 0:32], in_=ph[:, 32:64], func=AF.Sigmoid), r=[ph_t], w=[sh_t])
            P.op("dve", lambda e, sh=sh, ph=ph, u_ap=u_ap: e.tensor_tensor(out=u_ap[:, 0:32], in0=ph[:, 0:32], in1=sh[:, 0:32], op=ALU.mult),
                 r=[ph_t, sh_t], w=u_g)
            conv_chunk(cc)
        dbg("QT%d" % j, ar[:, QT0:QT0 + 2 * H * 512], G(QT0, 2 * H * 512), [128, 2 * H * 512], BF16)
        dbg("uext%d" % j, ar[:, 0:CC * 1536], G(0, CC * 1536), [128, CC * 1536], BF16)
        for h in range(H):
            nkb = 2 * j + 2
            its = [(kb, kti, cpart) for kb in range(nkb) for kti in range(4) for cpart in range(2)]
            blk = {}

            def emit_score(idx):
                kb, kti, cpart = its[idx]
                if kb not in blk:
                    blk[kb] = kvblock(j, h, kb)
                kt_, v_, kv_g = blk[kb]
                masked = kb >= 2 * j
                sbk, sbk_t = psf[6 + idx % 2], psf_t[6 + idx % 2]
                q_ap, q_g = QT(2 * h + cpart)
                P.op("pe", lambda e: e.matmul(sbk[:], lhsT=kt_[:, cpart, kti * 128:(kti + 1) * 128], rhs=q_ap,
                                              start=True, stop=(not masked)), r=kv_g + q_g, w=[sbk_t])
                if masked:
                    wdx = (kb - 2 * j) * 4 + kti
                    P.op("pe", lambda e: e.matmul(sbk[:], lhsT=mk[:, wdx * 128:(wdx + 1) * 128], rhs=mqb[:],
                                                  start=False, stop=True), r=[const_t, mqb_t], w=[sbk_t])
                pi = idx % NPT
                P.op("act", lambda e: e.activation(out=pt[pi][:], in_=sbk[:], func=AF.Exp, scale=ESCALE),
                     r=[sbk_t], w=[pt_t[pi]])

            def emit_pv(idx):
                kb, kti, cpart = its[idx]
                kt_, v_, kv_g = blk[kb]
                pi = idx % NPT
                first = (kb == 0 and kti == 0)
                last = (kb == nkb - 1) and (kti == 3)
                for ec in range(2):
                    P.op("pe", lambda e: e.matmul(psf[2 * cpart + ec][:], lhsT=v_[:, kti, ec * 128:(ec + 1) * 128], rhs=pt[pi][:],
                                                  start=first, stop=last), r=[pt_t[pi]] + kv_g, w=[psf_t[2 * cpart + ec]])
                P.op("pe", lambda e: e.matmul(psf[4 + cpart][:], lhsT=ones[:], rhs=pt[pi][:], start=first, stop=last),
                     r=[pt_t[pi], const_t], w=[psf_t[4 + cpart]])

            emit_score(0)
            for idx in range(len(its)):
                if idx + 1 < len(its):
                    emit_score(idx + 1)
                emit_pv(idx)
            r1, r1_t = new_tmp()
            r2, r2_t = new_tmp()
            P.op("dve", lambda e: e.reciprocal(out=r1[:], in_=psf[4][:]), r=[psf_t[4]], w=[r1_t])
            P.op("dve", lambda e: e.reciprocal(out=r2[:], in_=psf[5][:]), r=[psf_t[5]], w=[r2_t])
            P.op("dve", lambda e: e.tensor_scalar(out=r2[:], in0=r2[:], scalar1=nlam, scalar2=None, op0=ALU.mult),
                 r=[r2_t, lam_t], w=[r2_t])
            atts = []
            for ec in range(2):
                a1_, a1_t = new_tmp()
                a2_, a2_t = new_tmp()
                P.op("dve", lambda e: e.tensor_tensor(out=a1_[:], in0=psf[ec][:], in1=r1[:], op=ALU.mult), r=[psf_t[ec], r1_t], w=[a1_t])
                P.op("dve", lambda e: e.tensor_tensor(out=a2_[:], in0=psf[2 + ec][:], in1=r2[:], op=ALU.mult), r=[psf_t[2 + ec], r2_t], w=[a2_t])
                P.op("dve", lambda e: e.tensor_tensor(out=a1_[:], in0=a1_[:], in1=a2_[:], op=ALU.add), r=[a1_t, a2_t], w=[a1_t])
                sq_, sq_t = new_tb()
                P.op("act", lambda e: e.activation(out=sq_[:], in_=a1_[:], func=AF.Square), r=[a1_t], w=[sq_t])
                P.op("pe", lambda e: e.matmul(psf[4][:], lhsT=ones[:], rhs=sq_[:], start=(ec == 0), stop=(ec == 1)),
                     r=[sq_t, const_t], w=[psf_t[4]])
                atts.append((a1_, a1_t))
            rs_, rs_t = new_tmp()
            rstd_from_ss(psf[4][:], psf_t[4], rs_[:], rs_t, 256, EPS)
            for ec in range(2):
                a1_, a1_t = atts[ec]
                P.op("dve", lambda e: e.scalar_tensor_tensor(out=XT[:, 2 * h + ec, 32:544], in0=a1_[:], scalar=gsbc[:, ec:ec + 1],
                                                             in1=rs_[:], op0=ALU.mult, op1=ALU.mult),
                     r=[a1_t, rs_t, const_t], w=[XT_t[2 * h + ec]])
        psum_s, psum_s_t = psf[0], psf_t[0]
        psum_q, psum_q_t = psf[1], psf_t[1]
        for cc in range(CC):
            y_ap, y_g = ycv(cc)
            yb, yb_t = new_tb()
            y2, y2_t = new_tb()
            P.op("act", lambda e: e.activation(out=yb[:], in_=y_ap, func=AF.Copy), r=y_g, w=[yb_t])
            P.op("act", lambda e: e.activation(out=y2[:], in_=y_ap, func=AF.Square), r=y_g, w=[y2_t])
            P.op("pe", lambda e: e.matmul(psum_s[:], lhsT=ones[:], rhs=yb[:], start=(cc == 0), stop=(cc == CC - 1)),
                 r=[yb_t, const_t], w=[psum_s_t])
            P.op("pe", lambda e: e.matmul(psum_q[:], lhsT=ones[:], rhs=y2[:], start=(cc == 0), stop=(cc == CC - 1)),
                 r=[y2_t, const_t], w=[psum_q_t])
        mean, mean_t = new_tmp()
        msq, msq_t = new_tmp()
        rsd, rsd_t = new_tmp()
        P.op("dve", lambda e: e.tensor_scalar(out=mean[:], in0=psum_s[:], scalar1=1.0 / c.C, scalar2=None, op0=ALU.mult),
             r=[psum_s_t], w=[mean_t])
        P.op("dve", lambda e: e.tensor_tensor(out=msq[:], in0=mean[:], in1=mean[:], op=ALU.mult), r=[mean_t], w=[msq_t])
        P.op("dve", lambda e: e.scalar_tensor_tensor(out=msq[:], in0=psum_q[:], scalar=1.0 / c.C, in1=msq[:], op0=ALU.mult,
                                                     op1=ALU.subtract), r=[psum_q_t, msq_t], w=[msq_t])
        rstd_from_ss(msq[:], msq_t, rsd[:], rsd_t, 1.0, LN_EPS)
        for cc in range(CC):
            y_ap, y_g = ycv(cc)
            P.op("dve", lambda e: e.tensor_tensor(out=y_ap, in0=y_ap, in1=mean[:], op=ALU.subtract),
                 r=y_g + [mean_t], w=y_g)
            P.op("dve", lambda e: e.tensor_tensor(out=y_ap, in0=y_ap, in1=rsd[:], op=ALU.mult),
                 r=y_g + [rsd_t], w=y_g)
            P.op("act", lambda e: e.activation(out=XT[:, AC + cc, 32:544], in_=y_ap, func=AF.Silu,
                                               scale=lngt[:, cc:cc + 1], bias=lnbt[:, cc:cc + 1]),
                 r=y_g + [const_t], w=[XT_t[AC + cc]])
        dbg("mixT%d" % j, XT[:], XT_t + [XTh_t], [128, DC, 544], BF16)
        for q in range(4):
            h_ap, h_g = hT(q)
            dma("sp", h_ap, xo[j, 32 + q * 128:32 + (q + 1) * 128, :], w=h_g)
        for dg in range(c.NDG):
            for half in range(2):
                wo, wo_t = wblock(j, ("o", dg, half))
                for q in range(4):
                    pb_, pb_t = psf[q], psf_t[q]
                    for m in range(DC // 2):
                        mc = half * (DC // 2) + m
                        P.op("pe", lambda e, pb_=pb_, mc=mc, m=m, q=q, wo=wo: e.matmul(
                            pb_[:], lhsT=XT[:, mc, 32 + q * 128:32 + (q + 1) * 128], rhs=wo[:, m, :],
                            start=(m == 0), stop=(m == DC // 2 - 1)), r=[XT_t[mc], wo_t], w=[pb_t])
                    h_ap, h_g = hT(q, dg)
                    P.op("dve", lambda e, pb_=pb_, h_ap=h_ap: e.tensor_tensor(out=h_ap, in0=pb_[:], in1=h_ap, op=ALU.add),
                         r=[pb_t] + h_g, w=h_g)
        dbg("h%d" % j, ar[:, 0:A1], G(0, A1), [128, A1], BF16)
        for q in range(4):
            h_ap, h_g = hT(q)
            norm_transpose(h_ap, h_g, 128, g2T, 32 + q * 128)
        F_PAIRS = [(0, 1), (2, 3), (4, 5)]
        for fc in range(FC):
            wg2, wg2_t = wblock(j, ("g", fc))
            wu2, wu2_t = wblock(j, ("u", fc))
            ba, bb = F_PAIRS[fc % 3]
            for (w_, w_t, b_) in ((wg2, wg2_t, ba), (wu2, wu2_t, bb)):
                for ch in range(DC):
                    P.op("pe", lambda e, b_=b_, ch=ch, w_=w_: e.matmul(psf[b_][:], lhsT=w_[:, ch, :], rhs=XT[:, ch, 32:544],
                                                                       start=(ch == 0), stop=(ch == DC - 1)),
                         r=[XT_t[ch], w_t], w=[psf_t[b_]])
            sl, sl_t = new_tmp()
            a_ap, a_g = actT(fc)
            P.op("act", lambda e, sl=sl, ba=ba: e.activation(out=sl[:], in_=psf[ba][:], func=AF.Silu), r=[psf_t[ba]], w=[sl_t])
            P.op("dve", lambda e, sl=sl, bb=bb, a_ap=a_ap: e.tensor_tensor(out=a_ap, in0=psf[bb][:], in1=sl[:], op=ALU.mult),
                 r=[psf_t[bb], sl_t], w=a_g)
        dbg("actT%d" % j, ar[:, A1:A1 + FC * 512], G(A1, FC * 512), [128, FC * 512], BF16)
        for dg in range(c.NDG):
            for (dg2, f0, nf) in dblocks:
                if dg2 != dg:
                    continue
                wd, wd_t = wblock(j, ("d", dg, f0))
                for f in range(nf):
                    fc = f0 + f
                    a_ap, a_g = actT(fc)
                    for q in range(4):
                        P.op("pe", lambda e, q=q, fc=fc, f=f, wd=wd, a_ap=a_ap: e.matmul(
                            psf[q][:], lhsT=a_ap[:, q * 128:(q + 1) * 128], rhs=wd[:, f, :],
                            start=(fc == 0), stop=(fc == FC - 1)), r=a_g + [wd_t], w=[psf_t[q]])
            for q in range(4):
                h_ap, h_g = hT(q, dg)
                P.op("dve", lambda e, q=q, h_ap=h_ap: e.tensor_tensor(out=h_ap, in0=psf[q][:], in1=h_ap, op=ALU.add),
                     r=[psf_t[q]] + h_g, w=h_g)
        for q in range(4):
            h_ap, h_g = hT(q)
            ot = T("y%d_%d" % (j, q))
            out_ts.append(ot)
            dma("sp", y_d[j * 512 + q * 128:j * 512 + (q + 1) * 128, :], h_ap, r=h_g, w=[ot])
    P.op("pool", lambda e: e.engine_nop(), r=out_ts + dbg_ts)
    P.emit(nc, es)
    es.close()
    return nc


def _tables(cfg):
    S = cfg.S
    inv = (np.float32(10000.0) ** (-(np.arange(0, 128, 2, dtype=np.float32) / np.float32(128)))).astype(np.float32)
    ang = np.arange(S, dtype=np.float32)[:, None] * inv[None, :]
    ang = np.concatenate([ang, ang], axis=-1)
    cosT = np.ascontiguousarray(np.cos(ang).astype(np.float32).T)
    sinT = np.ascontiguousarray(np.sin(ang).astype(np.float32).T)
    ident = np.eye(128, dtype=np.float32)
    ones = np.ones((128, 128), dtype=np.float32)
    perm = np.zeros((128, 128), dtype=np.float32)
    for d in range(64):
        perm[d + 64, d] = -1.0
        perm[d, d + 64] = 1.0
    mk = np.zeros((16, 1024), dtype=np.float32)
    for i in range(1024):
        mk[i // 64, i] = 1.0
    return cosT, sinT, ident, ones, perm, mk


def make_in_maps(cfg, inputs):
    c = cfg
    f32 = np.float32
    x = np.asarray(inputs["x"], dtype=f32)
    cosT, sinT, ident, ones, perm, mk = _tables(c)

    def colT(v, n):
        return np.ascontiguousarray(np.asarray(v, dtype=f32).reshape(n, 128).T)

    shared = {
        "w_in": np.ascontiguousarray(np.asarray(inputs["w_in"], dtype=f32)[0]),
        "w_out": np.ascontiguousarray(np.asarray(inputs["w_out"], dtype=f32)[0]),
        "w_gate": np.ascontiguousarray(np.asarray(inputs["w_gate"], dtype=f32)[0]),
        "w_up": np.ascontiguousarray(np.asarray(inputs["w_up"], dtype=f32)[0]),
        "w_down": np.ascontiguousarray(np.asarray(inputs["w_down"], dtype=f32)[0]),
        "norm1_g": colT(inputs["norm1_g"][0], c.DC),
        "norm2_g": colT(inputs["norm2_g"][0], c.DC),
        "q_norm_g": np.ascontiguousarray(np.asarray(inputs["q_norm_g"], dtype=f32)[0].reshape(128, 1)),
        "k_norm_g": np.ascontiguousarray(np.asarray(inputs["k_norm_g"], dtype=f32)[0].reshape(128, 1)),
        "lambda_q1": np.ascontiguousarray(np.asarray(inputs["lambda_q1"], dtype=f32)[0]),
        "lambda_k1": np.ascontiguousarray(np.asarray(inputs["lambda_k1"], dtype=f32)[0]),
        "lambda_q2": np.ascontiguousarray(np.asarray(inputs["lambda_q2"], dtype=f32)[0]),
        "lambda_k2": np.ascontiguousarray(np.asarray(inputs["lambda_k2"], dtype=f32)[0]),
        "subln_g": colT(inputs["subln_g"][0], 2),
        "conv_w": np.ascontiguousarray(np.asarray(inputs["conv_w"], dtype=f32)[0, :, 0, :].T.reshape(c.CC, 128, CW)
                                       .transpose(1, 0, 2).reshape(128, c.CC * CW)),
        "conv_b": colT(inputs["conv_b"][0], c.CC),
        "conv_ln_g": colT(inputs["conv_ln_g"][0], c.CC),
        "conv_ln_b": colT(inputs["conv_ln_b"][0], c.CC),
        "cosk": cosT, "sink": sinT, "ident": ident, "ones": ones, "perm": perm, "mk": mk,
    }
    maps = []
    for core in range(c.NCORES):
        b, p = core // 2, core % 2
        xo = np.zeros((c.NOWN, 544, c.D), dtype=f32)
        cosq = np.zeros((c.NOWN, 128, 512), dtype=f32)
        sinq = np.zeros((c.NOWN, 128, 512), dtype=f32)
        mq = np.zeros((16, c.NOWN * 512), dtype=f32)
        for j in range(c.NOWN):
            g = own_tile(p, j)
            xo[j, 32:] = x[b, g * 512:(g + 1) * 512]
            if g > 0:
                xo[j, :32] = x[b, g * 512 - 32:g * 512]
            cosq[j] = cosT[:, g * 512:(g + 1) * 512]
            sinq[j] = sinT[:, g * 512:(g + 1) * 512]
            for s in range(2):
                qc = (g - 2 * j) * 8 + s * 4 + np.arange(256) // 64
                for r_ in range(16):
                    mq[r_, j * 512 + s * 256:j * 512 + (s + 1) * 256] = np.where(qc < r_, NEG_BIG, 0.0)
        m = dict(shared)
        m["xf"] = np.ascontiguousarray(x[b])
        m["xo"] = xo
        m["cosq"] = cosq
        m["sinq"] = sinq
        m["mq"] = mq
        maps.append(m)
    return maps


def gather_out(cfg, results, dtype=np.float32):
    c = cfg
    out = np.zeros((c.B, c.S, c.D), dtype=dtype)
    for core in range(c.NCORES):
        b, p = core // 2, core % 2
        y = np.asarray(results[core]["y"])
        for j in range(c.NOWN):
            g = own_tile(p, j)
            out[b, g * 512:(g + 1) * 512] = y[j * 512:(j + 1) * 512]
    return out


_NC_CACHE = {}


def run_cfg(cfg, inputs):
    key = (cfg.D, cfg.H, cfg.F, cfg.S, cfg.B)
    if key not in _NC_CACHE:
        _NC_CACHE[key] = build(cfg)
    nc = _NC_CACHE[key]
    maps = make_in_maps(cfg, inputs)
    res = run_bass_kernel_spmd(nc, maps, core_ids=list(range(cfg.NCORES)))
    return gather_out(cfg, res.results)


def kernel(**inputs):
    cfg = Cfg()
    return run_cfg(cfg, inputs)
```

```python
import math
from contextlib import ExitStack

import numpy as np
import concourse.bass as bass
import concourse.mybir as mybir
from concourse.bass_utils import run_bass_kernel_spmd

F32 = mybir.dt.float32
BF16 = mybir.dt.bfloat16
ALU = mybir.AluOpType
AF = mybir.ActivationFunctionType
AX = mybir.AxisListType

EPS = 1e-6
LN_EPS = 1e-5
CW = 31
LAMBDA_INIT = 0.8 - 0.6 * math.exp(-0.3 * 0)
NEG_BIG = -30000.0
SLOT = 4096


class Cfg:
    def __init__(s, D=2048, H=4, F=5632, S=8192, B=4, R=6, KR=4):
        s.D, s.H, s.F, s.S, s.B = D, H, F, S, B
        s.DC = D // 128
        s.ATT = H * 256
        s.AC = s.ATT // 128
        s.C = D - s.ATT
        s.CC = s.C // 128
        s.FC = F // 128
        s.NT = S // 512
        s.NOWN = s.NT // 2
        s.KW = H * 256
        s.INC = 3 * s.KW + 2 * s.C
        s.K0 = s.KW
        s.V0 = 2 * s.KW
        s.GA0 = 3 * s.KW
        s.GG0 = 3 * s.KW + s.C
        s.NDG = D // 512
        s.R = R
        s.KR = KR
        s.NCORES = 2 * B


def own_tile(p, j):
    return 2 * j + ((j + p) % 2)


class T:
    __slots__ = ("name", "lw", "rd")

    def __init__(self, name):
        self.name = name
        self.lw = None
        self.rd = []


class Op:
    __slots__ = ("eng", "fn", "deps", "signal", "dma", "sem", "val", "sigval")

    def __init__(self, eng, fn, dma):
        self.eng = eng
        self.fn = fn
        self.deps = []
        self.signal = False
        self.dma = dma
        self.sem = None
        self.val = 0
        self.sigval = 0


class _Rec:
    def __init__(self):
        self.calls = []

    def __getattr__(self, name):
        def f(*a, **k):
            self.calls.append((name, a, k))
            return self
        return f


class Prog:
    ENGS = ("pe", "act", "dve", "pool", "sp")
    NDS = 12

    def __init__(self):
        self.ops = {e: [] for e in self.ENGS}
        self.dma_cnt = {"pool": 0, "sp": 0}
        self.dma_last = {}

    def op(self, eng, fn, r=(), w=(), dma=False):
        rec = _Rec()
        fn(rec)
        assert len(rec.calls) == 1
        o = Op(eng, rec.calls[0], dma)
        deps = set()
        for t in r:
            if t.lw is not None:
                deps.add(t.lw)
        for t in w:
            if t.lw is not None:
                deps.add(t.lw)
            deps.update(t.rd)
        if dma:
            k = self.dma_cnt[eng]
            self.dma_cnt[eng] = k + 1
            o.sem = (eng, k % self.NDS)
            o.val = 16 * (k // self.NDS + 1)
            prev = self.dma_last.get(o.sem)
            if prev is not None:
                deps.add(prev)
            self.dma_last[o.sem] = o
        for d in deps:
            if d.eng == "pe" and eng == "pe" and not d.dma:
                continue
            d.signal = True
            o.deps.append(d)
        for t in r:
            t.rd.append(o)
        for t in w:
            t.lw = o
            t.rd = []
        self.ops[eng].append(o)
        return o

    def emit(self, nc, es):
        csem = {e: es.enter_context(nc.semaphore("c_" + e)) for e in ("pe", "act", "dve", "pool")}
        dsem = {}
        for q in ("pool", "sp"):
            for i in range(self.NDS):
                dsem[(q, i)] = es.enter_context(nc.semaphore("d_%s%d" % (q, i)))
        for e in self.ENGS:
            c = 0
            for o in self.ops[e]:
                if o.signal and not o.dma:
                    c += 1
                    o.sigval = c
        block = es.enter_context(nc.Block())

        def run(ename, eng):
            waited = {}
            for o in self.ops[ename]:
                for d in o.deps:
                    if d.dma:
                        sem, val, key = dsem[d.sem], d.val, d.sem
                    else:
                        sem, val, key = csem[d.eng], d.sigval, d.eng
                    if waited.get(key, 0) >= val:
                        continue
                    waited[key] = val
                    eng.wait_ge(sem, val)
                name, a, k = o.fn
                ins = getattr(eng, name)(*a, **k)
                if o.dma:
                    ins.then_inc(dsem[o.sem], 16)
                elif o.signal:
                    assert ins is not None
                    ins.then_inc(csem[ename], 1)

        @block.tensor
        def _(e):
            run("pe", e)

        @block.scalar
        def _(e):
            run("act", e)

        @block.vector
        def _(e):
            run("dve", e)

        @block.gpsimd
        def _(e):
            run("pool", e)

        @block.sync
        def _(e):
            run("sp", e)


def build(cfg, debug=False):
    c = cfg
    D, H, DC, AC, CC, FC, NT, NOWN = c.D, c.H, c.DC, c.AC, c.CC, c.FC, c.NT, c.NOWN
    nc = bass.Bass("TRN2", target_bir_lowering=False)
    P = Prog()
    es = ExitStack()
    E = es.enter_context

    def din(name, shape, dt=F32):
        return nc.dram_tensor(name, list(shape), dt, kind="ExternalInput").ap()

    xf = din("xf", [c.S, D])
    xo = din("xo", [NOWN, 544, D])
    w_in = din("w_in", [D, c.INC])
    w_out = din("w_out", [D, D])
    w_gate = din("w_gate", [D, c.F])
    w_up = din("w_up", [D, c.F])
    w_down = din("w_down", [c.F, D])
    g1_d = din("norm1_g", [128, DC])
    g2_d = din("norm2_g", [128, DC])
    qg_d = din("q_norm_g", [128, 1])
    kg_d = din("k_norm_g", [128, 1])
    lq1_d = din("lambda_q1", [128])
    lk1_d = din("lambda_k1", [128])
    lq2_d = din("lambda_q2", [128])
    lk2_d = din("lambda_k2", [128])
    sg_d = din("subln_g", [128, 2])
    cw_d = din("conv_w", [128, CC * CW])
    cb_d = din("conv_b", [128, CC])
    lng_d = din("conv_ln_g", [128, CC])
    lnb_d = din("conv_ln_b", [128, CC])
    cosk_d = din("cosk", [128, c.S])
    sink_d = din("sink", [128, c.S])
    cosq_d = din("cosq", [NOWN, 128, 512])
    sinq_d = din("sinq", [NOWN, 128, 512])
    ident_d = din("ident", [128, 128])
    ones_d = din("ones", [128, 128])
    perm_d = din("perm", [128, 128])
    mk_d = din("mk", [16, 1024])
    mq_d = din("mq", [16, NOWN * 512])
    y_d = nc.dram_tensor("y", [NOWN * 512, D], F32, kind="ExternalOutput").ap()

    NKB = NT
    skind = "ExternalOutput" if debug else "Internal"
    kts = nc.dram_tensor("kts", [H * NKB, 128, 1024], BF16, kind=skind).ap()
    vs = nc.dram_tensor("vs", [H * NKB, 128, 1024], BF16, kind=skind).ap()
    dbg_ts = []

    def dbg(name, ap, tiles, shape, dt):
        if not debug:
            return
        d = nc.dram_tensor("dbg_" + name, list(shape), dt, kind="ExternalOutput").ap()
        t_ = T("dbg_" + name)
        dbg_ts.append(t_)
        P.op("pool", lambda e: e.dma_start(out=d, in_=ap), r=list(tiles), w=[t_], dma=True)

    kts_t = [T("kts%d" % i) for i in range(H * NKB)]
    vs_t = [T("vs%d" % i) for i in range(H * NKB)]

    win_v = w_in.rearrange("(c p) n -> p c n", p=128)
    wout_v = w_out.rearrange("(c p) n -> p c n", p=128)
    wg_v = w_gate.rearrange("(c p) n -> p c n", p=128)
    wu_v = w_up.rearrange("(c p) n -> p c n", p=128)
    wd_v = w_down.rearrange("(f p) n -> p f n", p=128)
    units = []
    for hh in range(2 * H):
        units.append([(("q", hh), win_v[:, :, hh * 128:(hh + 1) * 128], DC, 128)])
    for cc in range(CC):
        units.append([(("ga", cc), win_v[:, :, c.GA0 + cc * 128:c.GA0 + (cc + 1) * 128], DC, 128),
                      (("gg", cc), win_v[:, :, c.GG0 + cc * 128:c.GG0 + (cc + 1) * 128], DC, 128)])
    for dg in range(c.NDG):
        for half in range(2):
            units.append([(("o", dg, half), wout_v[:, half * (DC // 2):(half + 1) * (DC // 2), dg * 512:(dg + 1) * 512], DC // 2, 512)])
    for fc in range(FC):
        units.append([(("g", fc), wg_v[:, :, fc * 128:(fc + 1) * 128], DC, 128),
                      (("u", fc), wu_v[:, :, fc * 128:(fc + 1) * 128], DC, 128)])
    dblocks = []
    for dg in range(c.NDG):
        f0 = 0
        while f0 < FC:
            nf = min(8, FC - f0)
            units.append([(("d", dg, f0), wd_v[:, f0:f0 + nf, dg * 512:(dg + 1) * 512], nf, 512)])
            dblocks.append((dg, f0, nf))
            f0 += nf
    slots = []
    windex = {}
    cur, used = [], 0
    for u in units:
        sz = sum(a * b for (_, _, a, b) in u)
        if used + sz > SLOT and cur:
            slots.append(cur)
            cur, used = [], 0
        for (key, src, a, b) in u:
            cur.append((key, src, a, b, used))
            windex[key] = (len(slots), used, a, b)
            used += a * b
    if cur:
        slots.append(cur)
    NSLOT = len(slots)
    slot_used = [max(off + a * b for (_, _, a, b, off) in s) for s in slots]
    wscr = nc.dram_tensor("wscr", [NSLOT, 128, SLOT], BF16, kind=skind).ap()
    wscr_t = [T("wscr%d" % i) for i in range(NSLOT)]

    def sb(name, shape, dt):
        return E(nc.sbuf_tensor("s_" + name, list(shape), dt))

    GR = 512
    A1 = 4 * D * 2
    YC0 = A1
    QT0 = YC0 + CC * 1024
    KV0 = QT0 + 2 * H * 512
    KVSZ = 2048
    main_sz = max(A1 + FC * 512, KV0 + c.KR * KVSZ)
    WK1 = DC * 2 * c.KW
    VST0 = WK1 + 2048
    pa_sz = VST0 + 4 * c.KW
    AR = ((max(main_sz, pa_sz) + GR - 1) // GR) * GR
    assert CC * 1536 <= A1
    ar = sb("arena", [128, AR], BF16)
    gr = [T("gr%d" % i) for i in range(AR // GR)]

    def G(off, n):
        return gr[off // GR:(off + n - 1) // GR + 1]

    def hT(q, dg=None):
        if dg is None:
            return ar[:, q * D * 2:(q + 1) * D * 2].bitcast(F32), G(q * D * 2, D * 2)
        o = (q * D + dg * 512) * 2
        return ar[:, o:o + 1024].bitcast(F32), G(o, 1024)

    def uext(cc):
        o = cc * 1536
        return ar[:, o:o + 1088].bitcast(F32), G(o, 1088)

    def actT(fc):
        o = A1 + fc * 512
        return ar[:, o:o + 512], G(o, 512)

    def ycv(cc):
        o = YC0 + cc * 1024
        return ar[:, o:o + 1024].bitcast(F32), G(o, 1024)

    def QT(hh):
        o = QT0 + hh * 512
        return ar[:, o:o + 512], G(o, 512)

    def ktr(r_):
        o = KV0 + r_ * KVSZ
        return ar[:, o:o + 1024].rearrange("p (c k) -> p c k", c=2), G(o, 1024)

    def vr(r_):
        o = KV0 + r_ * KVSZ + 1024
        return ar[:, o:o + 1024].rearrange("p (k e) -> p k e", k=4), G(o, 1024)

    wkv_ap = ar[:, 0:WK1].rearrange("p (c n) -> p c n", c=DC)

    def wkvG(ch, col, n):
        return G(ch * 2 * c.KW + col, n)

    def kst(i):
        o = WK1 + i * 1024
        return ar[:, o:o + 1024].rearrange("p (c k) -> p c k", c=2), G(o, 1024)

    vst = ar[:, VST0:VST0 + 4 * c.KW].rearrange("p (k e) -> p k e", k=4)
    vst_g = G(VST0, 4 * c.KW)

    ring = [sb("ring%d" % i, [128, SLOT], BF16) for i in range(c.R)]
    ring_t = [T("ring%d" % i) for i in range(c.R)]
    NXS = 2
    xs = [sb("xs%d" % i, [128, D], F32) for i in range(NXS)]
    xs_t = [T("xs%d" % i) for i in range(NXS)]
    xn = [sb("xn%d" % i, [128, D], BF16) for i in range(2)]
    xn_t = [T("xn%d" % i) for i in range(2)]
    st = sb("st", [128, 16], F32)
    st_t = [T("st%d" % i) for i in range(16)]
    XT = sb("XT", [128, DC, 544], BF16)
    XT_t = [T("XT%d" % i) for i in range(DC)]
    XTh_t = T("XTh")
    NPT = 3
    pt = [sb("pt%d" % i, [128, 512], BF16) for i in range(NPT)]
    pt_t = [T("pt%d" % i) for i in range(NPT)]
    NTMP = 8
    tmp = [sb("tmp%d" % i, [128, 512], F32) for i in range(NTMP)]
    tmp_t = [T("tmp%d" % i) for i in range(NTMP)]
    NTB = 6
    tb = [sb("tb%d" % i, [128, 512], BF16) for i in range(NTB)]
    tb_t = [T("tb%d" % i) for i in range(NTB)]
    cosb = sb("cosb", [128, 512], F32)
    sinb = sb("sinb", [128, 512], F32)
    cs_t = T("cs")
    g1T = sb("g1T", [128, DC], F32)
    g2T = sb("g2T", [128, DC], F32)
    gsbc = sb("gsbc", [128, 2], F32)
    qgc = sb("qgc", [128, 1], F32)
    kgc = sb("kgc", [128, 1], F32)
    cwt = sb("cwt", [128, CC * CW], F32)
    cbt = sb("cbt", [128, CC], F32)
    lngt = sb("lngt", [128, CC], F32)
    lnbt = sb("lnbt", [128, CC], F32)
    ident = sb("ident", [128, 128], BF16)
    ones = sb("ones", [128, 128], BF16)
    perm = sb("perm", [128, 128], BF16)
    mk = sb("mk", [16, 1024], BF16)
    mqb = sb("mqb", [16, 512], BF16)
    mqb_t = T("mqb")
    const_t = T("const")
    lam_t = T("lam")
    epsc = sb("epsc", [128, 2], F32)
    eps_t = T("eps")
    psf = [E(nc.psum_tensor("psf%d" % i, [128, 512], F32)) for i in range(8)]
    psf_t = [T("psf%d" % i) for i in range(8)]
    psT_l = [psf[7][:].bitcast(BF16), psf[6][:].bitcast(BF16)]
    psT_tl = [psf_t[7], psf_t[6]]
    psT_i = {"i": 0}

    def dma(q, out, in_, r=(), w=()):
        return P.op(q, lambda e: e.dma_start(out=out, in_=in_), r=r, w=w, dma=True)

    rr = {"tmp": 0, "tb": 0, "pt": 0, "xs": 0, "xn": 0, "st": 0}

    def nxt(kind, n):
        i = rr[kind]
        rr[kind] = (i + 1) % n
        return i

    def new_tmp():
        i = nxt("tmp", NTMP)
        return tmp[i], tmp_t[i]

    def new_tb():
        i = nxt("tb", NTB)
        return tb[i], tb_t[i]

    cl = [
        (g1T[:], g1_d), (g2T[:], g2_d),
        (gsbc[:], sg_d), (qgc[:], qg_d), (kgc[:], kg_d),
        (cwt[:], cw_d), (cbt[:], cb_d), (lngt[:], lng_d), (lnbt[:], lnb_d),
        (ident[:], ident_d), (ones[:], ones_d), (perm[:], perm_d), (mk[:], mk_d),
    ]
    for (o_, i_) in cl:
        dma("pool", o_, i_, w=[const_t])
    P.op("dve", lambda e: e.memset(epsc[:, 0:1], EPS), w=[eps_t])
    P.op("dve", lambda e: e.memset(epsc[:, 1:2], LN_EPS), w=[eps_t])
    P.op("dve", lambda e: e.tensor_scalar(out=gsbc[:], in0=gsbc[:], scalar1=1.0 - LAMBDA_INIT, scalar2=None,
                                          op0=ALU.mult), r=[const_t], w=[const_t])
    LS = 8
    for k, (da, db) in enumerate(((lq1_d, lk1_d), (lq2_d, lk2_d))):
        ta, ta_t = new_tmp()
        tb2, tb2_t = new_tmp()
        dma("pool", ta[:, 0:128], da.partition_broadcast(128), w=[ta_t])
        dma("pool", tb2[:, 0:128], db.partition_broadcast(128), w=[tb2_t])
        P.op("dve", lambda e, ta=ta, tb2=tb2: e.tensor_tensor(out=ta[:, 128:256], in0=ta[:, 0:128], in1=tb2[:, 0:128], op=ALU.mult),
             r=[ta_t, tb2_t], w=[ta_t])
        P.op("dve", lambda e, k=k, ta=ta: e.reduce_sum(out=st[:, LS + k:LS + k + 1], in_=ta[:, 128:256], axis=AX.X),
             r=[ta_t], w=[st_t[LS + k]])
        P.op("act", lambda e, k=k: e.activation(out=st[:, LS + 2 + k:LS + 3 + k], in_=st[:, LS + k:LS + k + 1], func=AF.Exp),
             r=[st_t[LS + k]], w=[st_t[LS + 2 + k]])
    P.op("dve", lambda e: e.tensor_tensor(out=st[:, LS + 4:LS + 5], in0=st[:, LS + 3:LS + 4], in1=st[:, LS + 2:LS + 3],
                                          op=ALU.subtract), r=[st_t[LS + 2], st_t[LS + 3]], w=[st_t[LS + 4]])
    P.op("dve", lambda e: e.tensor_scalar(out=st[:, LS + 5:LS + 6], in0=st[:, LS + 4:LS + 5], scalar1=-LAMBDA_INIT,
                                          scalar2=None, op0=ALU.add), r=[st_t[LS + 4]], w=[lam_t])
    nlam = st[:, LS + 5:LS + 6]

    def rstd_from_ss(ss_ap, ss_t, out_ap, out_t, n, eps):
        np_ = ss_ap.shape[0]
        bcol = epsc[:np_, 0:1] if eps == EPS else epsc[:np_, 1:2]
        P.op("act", lambda e: e.activation(out=out_ap, in_=ss_ap, func=AF.Ln, scale=1.0 / n, bias=bcol),
             r=[ss_t, eps_t], w=[out_t])
        P.op("act", lambda e: e.activation(out=out_ap, in_=out_ap, func=AF.Exp, scale=-0.5),
             r=[out_t], w=[out_t])

    def norm_transpose(src_ap, src_ts, np_, gT, col0, halo=False):
        si = nxt("st", 4)
        ssc, ssc_t = st[:np_, si:si + 1], st_t[si]
        rsc, rsc_t = st[:np_, 4 + si:5 + si], st_t[4 + si]
        xi = nxt("xn", 2)
        P.op("act", lambda e: e.activation(out=xn[xi][:np_, :], in_=src_ap, func=AF.Square, accum_out=ssc),
             r=list(src_ts), w=[xn_t[xi], ssc_t])
        rstd_from_ss(ssc, ssc_t, rsc, rsc_t, D, EPS)
        P.op("act", lambda e: e.activation(out=xn[xi][:np_, :], in_=src_ap, func=AF.Copy, scale=rsc),
             r=list(src_ts) + [rsc_t], w=[xn_t[xi]])
        for c0 in range(0, DC, 8):
            n = min(8, DC - c0)
            ti = psT_i["i"] % 2
            psT_i["i"] += 1
            psT, psT_t = psT_l[ti], psT_tl[ti]
            for k in range(n):
                P.op("pe", lambda e: e.transpose(out=psT[:, k * np_:(k + 1) * np_],
                                                 in_=xn[xi][:np_, (c0 + k) * 128:(c0 + k + 1) * 128],
                                                 identity=ident[:np_, :np_]),
                     r=[xn_t[xi], const_t], w=[psT_t])
            P.op("dve", lambda e: e.tensor_tensor(
                out=XT[:, c0:c0 + n, col0:col0 + np_], in0=psT[:, 0:n * np_].rearrange("p (c t) -> p c t", c=n),
                in1=gT[:, c0:c0 + n].unsqueeze(2).to_broadcast([128, n, np_]), op=ALU.mult),
                 r=[psT_t, const_t], w=([XTh_t] if halo else XT_t[c0:c0 + n]))

    sbank = {"i": 0}

    def stream_bank(banks):
        i = sbank["i"]
        sbank["i"] = i + 1
        b = banks[i % len(banks)]
        return psf[b], psf_t[b]

    def qk_part1(ps, ps_t, gcol):
        sq, sq_t = new_tb()
        kg, kg_t = new_tb()
        P.op("act", lambda e: e.activation(out=sq[:], in_=ps[:], func=AF.Square), r=[ps_t], w=[sq_t])
        P.op("act", lambda e: e.activation(out=kg[:], in_=ps[:], func=AF.Copy, scale=gcol), r=[ps_t, const_t], w=[kg_t])
        return (sq, sq_t, kg, kg_t)

    def qk_part2(st1, out_ap, out_ts, banks):
        sq, sq_t, kg, kg_t = st1
        pa, pa_t = stream_bank(banks)
        P.op("pe", lambda e: e.matmul(pa[:], lhsT=ones[:], rhs=sq[:], start=True, stop=True), r=[sq_t, const_t], w=[pa_t])
        pb, pb_t = stream_bank(banks)
        P.op("pe", lambda e: e.matmul(pb[:], lhsT=perm[:], rhs=kg[:], start=True, stop=True), r=[kg_t, const_t], w=[pb_t])
        rs, rs_t = new_tmp()
        rstd_from_ss(pa[:], pa_t, rs[:], rs_t, 128, EPS)
        t1, t1_t = new_tmp()
        t2, t2_t = new_tmp()
        P.op("dve", lambda e: e.tensor_tensor(out=t1[:], in0=kg[:], in1=cosb[:], op=ALU.mult), r=[kg_t, cs_t], w=[t1_t])
        P.op("dve", lambda e: e.tensor_tensor(out=t2[:], in0=pb[:], in1=sinb[:], op=ALU.mult), r=[pb_t, cs_t], w=[t2_t])
        P.op("dve", lambda e: e.tensor_tensor(out=t1[:], in0=t1[:], in1=t2[:], op=ALU.add), r=[t1_t, t2_t], w=[t1_t])
        P.op("dve", lambda e: e.tensor_tensor(out=out_ap, in0=t1[:], in1=rs[:], op=ALU.mult), r=[t1_t, rs_t], w=list(out_ts))

    for ch in range(DC):
        dma("pool", wkv_ap[:, ch, :], win_v[:, ch, c.K0:c.K0 + 2 * c.KW], w=wkvG(ch, 0, 2 * c.KW))

    def prep_slot(n):
        rb, rb_t = ring[n % c.R], ring_t[n % c.R]
        for (key, src, a, b, off) in slots[n]:
            dma("pool", rb[:, off:off + a * b].rearrange("p (a b) -> p a b", a=a), src, w=[rb_t])
        dma("sp", wscr[n][:, 0:slot_used[n]], rb[:, 0:slot_used[n]], r=[rb_t], w=[wscr_t[n]])

    prep_next = {"n": 0}

    def prep_some(k):
        for _ in range(k):
            if prep_next["n"] < NSLOT:
                prep_slot(prep_next["n"])
                prep_next["n"] += 1

    A_BANKS = [0, 1, 2, 3, 4, 5]
    prep_per_tile = (NSLOT + NT - 1) // NT
    for t in range(NT):
        dma("sp", cosb[:], cosk_d[:, t * 512:(t + 1) * 512], w=[cs_t])
        dma("sp", sinb[:], sink_d[:, t * 512:(t + 1) * 512], w=[cs_t])
        for sub in range(4):
            xi = nxt("xs", NXS)
            dma("sp", xs[xi][:], xf[t * 512 + sub * 128:t * 512 + (sub + 1) * 128, :], w=[xs_t[xi]])
            norm_transpose(xs[xi][:], [xs_t[xi]], 128, g1T, 32 + sub * 128)
        pend = None
        for hh in range(2 * H + 1):
            cur = None
            if hh < 2 * H:
                ps, ps_t = stream_bank(A_BANKS)
                for ch in range(DC):
                    P.op("pe", lambda e: e.matmul(ps[:], lhsT=wkv_ap[:, ch, hh * 128:(hh + 1) * 128],
                                                  rhs=XT[:, ch, 32:544], start=(ch == 0), stop=(ch == DC - 1)),
                         r=[XT_t[ch]] + wkvG(ch, hh * 128, 128), w=[ps_t])
                cur = (hh, qk_part1(ps, ps_t, kgc[:, 0:1]))
            if pend is not None:
                ph_, st1 = pend
                h, cpart = ph_ // 2, ph_ % 2
                ki = (t * H + h) % 2
                ks_ap, ks_g = kst(ki)
                qk_part2(st1, ks_ap[:, cpart, :], ks_g, A_BANKS)
                if cpart == 1:
                    dma("sp", kts[h * NKB + t].rearrange("p (c k) -> p c k", c=2), ks_ap, r=ks_g, w=[kts_t[h * NKB + t]])
            pend = cur
        VW = min(512, c.KW)
        for sub in range(4):
            for half in range(c.KW // VW):
                ps, ps_t = stream_bank(A_BANKS)
                for ch in range(DC):
                    P.op("pe", lambda e, ps=ps, ch=ch, sub=sub, half=half: e.matmul(
                        ps[:, 0:VW], lhsT=XT[:, ch, 32 + sub * 128:32 + (sub + 1) * 128],
                        rhs=wkv_ap[:, ch, c.KW + half * VW:c.KW + (half + 1) * VW], start=(ch == 0), stop=(ch == DC - 1)),
                         r=[XT_t[ch]] + wkvG(ch, c.KW + half * VW, VW), w=[ps_t])
                P.op("act", lambda e, ps=ps, sub=sub, half=half: e.activation(out=vst[:, sub, half * VW:(half + 1) * VW],
                                                                               in_=ps[:, 0:VW], func=AF.Copy),
                     r=[ps_t], w=vst_g)
        for h in range(H):
            dma("sp", vs[h * NKB + t].rearrange("p (k e) -> p k e", k=4), vst[:, :, h * 256:(h + 1) * 256],
                r=vst_g, w=[vs_t[h * NKB + t]])

    wnext = {"g": 0}
    TOT = NOWN * NSLOT

    def wload(gi):
        n = gi % NSLOT
        rb, rb_t = ring[gi % c.R], ring_t[gi % c.R]
        if gi < NSLOT:
            for (key, src, a, b, off) in slots[n]:
                dma("pool", rb[:, off:off + a * b].rearrange("p (a b) -> p a b", a=a), src, w=[rb_t])
            dma("sp", wscr[n][:, 0:slot_used[n]], rb[:, 0:slot_used[n]], r=[rb_t], w=[wscr_t[n]])
        else:
            dma("sp", rb[:, 0:slot_used[n]], wscr[n][:, 0:slot_used[n]], r=[wscr_t[n]], w=[rb_t])

    def wblock(j, key):
        n, off, a, b = windex[key]
        gi = j * NSLOT + n
        while wnext["g"] <= min(gi + c.R - 1, TOT - 1):
            wload(wnext["g"])
            wnext["g"] += 1
        return ring[gi % c.R][:, off:off + a * b].rearrange("p (a b) -> p a b", a=a), ring_t[gi % c.R]

    kvseq = []
    for j in range(NOWN):
        for h in range(H):
            for kb in range(2 * j + 2):
                kvseq.append((j, h, kb))
    kvpos = {k: i for i, k in enumerate(kvseq)}
    kvnext = {"g": 0}

    def kvload(gi):
        (j, h, kb) = kvseq[gi]
        r_ = gi % c.KR
        k_ap, k_g = ktr(r_)
        v_ap, v_g = vr(r_)
        dma("pool", k_ap, kts[h * NKB + kb].rearrange("p (c k) -> p c k", c=2), r=[kts_t[h * NKB + kb]], w=k_g)
        dma("pool", v_ap, vs[h * NKB + kb].rearrange("p (k e) -> p k e", k=4), r=[vs_t[h * NKB + kb]], w=v_g)

    def kvblock(j, h, kb):
        gi = kvpos[(j, h, kb)]
        last_j = kvpos[(j, H - 1, 2 * j + 1)]
        while kvnext["g"] <= min(gi + c.KR - 2, last_j):
            kvload(kvnext["g"])
            kvnext["g"] += 1
        r_ = gi % c.KR
        k_ap, k_g = ktr(r_)
        v_ap, v_g = vr(r_)
        return k_ap, v_ap, k_g + v_g

    S_BANKS = [4, 5, 6]
    Q_BANKS = [0, 1, 2, 3, 4, 5]
    ESCALE = 1.0 / math.sqrt(128.0)
    out_ts = []
    for j in range(NOWN):
        dma("sp", cosb[:], cosq_d[j], w=[cs_t])
        dma("sp", sinb[:], sinq_d[j], w=[cs_t])
        dma("pool", mqb[:], mq_d[:, j * 512:(j + 1) * 512], w=[mqb_t])
        xi = nxt("xs", NXS)
        dma("sp", xs[xi][0:32, :], xo[j, 0:32, :], w=[xs_t[xi]])
        norm_transpose(xs[xi][0:32, :], [xs_t[xi]], 32, g1T, 0, halo=True)
        for sub in range(4):
            xi = nxt("xs", NXS)
            dma("sp", xs[xi][:], xo[j, 32 + sub * 128:32 + (sub + 1) * 128, :], w=[xs_t[xi]])
            norm_transpose(xs[xi][:], [xs_t[xi]], 128, g1T, 32 + sub * 128)
        dbg("xnT%d" % j, XT[:], XT_t + [XTh_t], [128, DC, 544], BF16)
        pend = None
        for hh in range(2 * H + 1):
            cur = None
            if hh < 2 * H:
                wq, wq_t = wblock(j, ("q", hh))
                ps, ps_t = stream_bank(Q_BANKS)
                for ch in range(DC):
                    P.op("pe", lambda e: e.matmul(ps[:], lhsT=wq[:, ch, :], rhs=XT[:, ch, 32:544],
                                                  start=(ch == 0), stop=(ch == DC - 1)),
                         r=[XT_t[ch], wq_t], w=[ps_t])
                cur = (hh, qk_part1(ps, ps_t, qgc[:, 0:1]))
            if pend is not None:
                ph_, st1 = pend
                q_ap, q_g = QT(ph_)
                qk_part2(st1, q_ap, q_g, Q_BANKS)
            pend = cur
        def conv_chunk(cc):
            u_ap, u_g = uext(cc)
            y_ap, y_g = ycv(cc)
            P.op("dve", lambda e: e.tensor_scalar(
                out=y_ap, in0=u_ap[:, 2:514], scalar1=cwt[:, cc * CW:cc * CW + 1], scalar2=cbt[:, cc:cc + 1],
                op0=ALU.mult, op1=ALU.add), r=u_g + [const_t], w=y_g)
            for tap in range(1, CW):
                P.op("dve", lambda e: e.scalar_tensor_tensor(
                    out=y_ap, in0=u_ap[:, 2 + tap:514 + tap], scalar=cwt[:, cc * CW + tap:cc * CW + tap + 1],
                    in1=y_ap, op0=ALU.mult, op1=ALU.add), r=u_g + y_g, w=y_g)

        for cc in range(CC):
            wa, wa_t = wblock(j, ("ga", cc))
            wg_, wg_t = wblock(j, ("gg", cc))
            pa, pa_t = stream_bank(Q_BANKS)
            pg, pg_t = stream_bank(Q_BANKS)
            ph, ph_t = stream_bank(Q_BANKS)
            for (w_, w_t, po, po_t, col) in ((wa, wa_t, pa, pa_t, 0), (wg_, wg_t, pg, pg_t, 32)):
                for ch in range(DC):
                    P.op("pe", lambda e, po=po, ch=ch, w_=w_: e.matmul(po[:], lhsT=w_[:, ch, :], rhs=XT[:, ch, 32:544],
                                                                       start=(ch == 0), stop=(ch == DC - 1)),
                         r=[XT_t[ch], w_t], w=[po_t])
                for ch in range(DC):
                    P.op("pe", lambda e, ch=ch, w_=w_, col=col, ph=ph: e.matmul(ph[:, col:col + 32], lhsT=w_[:, ch, :], rhs=XT[:, ch, 0:32],
                                                                               start=(ch == 0), stop=(ch == DC - 1)),
                         r=[XTh_t, w_t], w=[ph_t])
            u_ap, u_g = uext(cc)
            sg, sg_t = new_tmp()
            P.op("act", lambda e, sg=sg, pg=pg: e.activation(out=sg[:], in_=pg[:], func=AF.Sigmoid), r=[pg_t], w=[sg_t])
            P.op("dve", lambda e, sg=sg, pa=pa, u_ap=u_ap: e.tensor_tensor(out=u_ap[:, 32:544], in0=pa[:], in1=sg[:], op=ALU.mult),
                 r=[pa_t, sg_t], w=u_g)
            sh, sh_t = new_tmp()
            P.op("act", lambda e, sh=sh, ph=ph: e.activation(out=sh[:, 0:32], in_=ph[:, 32:64], func=AF.Sigmoid), r=[ph_t], w=[sh_t])
            P.op("dve", lambda e, sh=sh, ph=ph, u_ap=u_ap: e.tensor_tensor(out=u_ap[:, 0:32], in0=ph[:, 0:32], in1=sh[:, 0:32], op=ALU.mult),
                 r=[ph_t, sh_t], w=u_g)
        dbg("QT%d" % j, ar[:, QT0:QT0 + 2 * H * 512], G(QT0, 2 * H * 512), [128, 2 * H * 512], BF16)
        dbg("uext%d" % j, ar[:, 0:CC * 1536], G(0, CC * 1536), [128, CC * 1536], BF16)
        conv_done = 0
        per_head = (CC + H - 1) // H
        for h in range(H):
            for _ in range(per_head):
                if conv_done < CC:
                    conv_chunk(conv_done)
                    conv_done += 1
            nkb = 2 * j + 2
            its = [(kb, kti, cpart) for kb in range(nkb) for kti in range(4) for cpart in range(2)]
            blk = {}

            def emit_score(idx):
                kb, kti, cpart = its[idx]
                if kb not in blk:
                    blk[kb] = kvblock(j, h, kb)
                kt_, v_, kv_g = blk[kb]
                masked = kb >= 2 * j
                sbk, sbk_t = psf[6 + idx % 2], psf_t[6 + idx % 2]
                q_ap, q_g = QT(2 * h + cpart)
                P.op("pe", lambda e: e.matmul(sbk[:], lhsT=kt_[:, cpart, kti * 128:(kti + 1) * 128], rhs=q_ap,
                                              start=True, stop=(not masked)), r=kv_g + q_g, w=[sbk_t])
                if masked:
                    wdx = (kb - 2 * j) * 4 + kti
                    P.op("pe", lambda e: e.matmul(sbk[:], lhsT=mk[:, wdx * 128:(wdx + 1) * 128], rhs=mqb[:],
                                                  start=False, stop=True), r=[const_t, mqb_t], w=[sbk_t])
                pi = idx % NPT
                P.op("act", lambda e: e.activation(out=pt[pi][:], in_=sbk[:], func=AF.Exp, scale=ESCALE),
                     r=[sbk_t], w=[pt_t[pi]])

            def emit_pv(idx):
                kb, kti, cpart = its[idx]
                kt_, v_, kv_g = blk[kb]
                pi = idx % NPT
                first = (kb == 0 and kti == 0)
                last = (kb == nkb - 1) and (kti == 3)
                for ec in range(2):
                    P.op("pe", lambda e: e.matmul(psf[2 * cpart + ec][:], lhsT=v_[:, kti, ec * 128:(ec + 1) * 128], rhs=pt[pi][:],
                                                  start=first, stop=last), r=[pt_t[pi]] + kv_g, w=[psf_t[2 * cpart + ec]])
                P.op("pe", lambda e: e.matmul(psf[4 + cpart][:], lhsT=ones[:], rhs=pt[pi][:], start=first, stop=last),
                     r=[pt_t[pi], const_t], w=[psf_t[4 + cpart]])

            emit_score(0)
            for idx in range(len(its)):
                if idx + 1 < len(its):
                    emit_score(idx + 1)
                emit_pv(idx)
            r1, r1_t = new_tmp()
            r2, r2_t = new_tmp()
            P.op("dve", lambda e: e.reciprocal(out=r1[:], in_=psf[4][:]), r=[psf_t[4]], w=[r1_t])
            P.op("dve", lambda e: e.reciprocal(out=r2[:], in_=psf[5][:]), r=[psf_t[5]], w=[r2_t])
            P.op("dve", lambda e: e.tensor_scalar(out=r2[:], in0=r2[:], scalar1=nlam, scalar2=None, op0=ALU.mult),
                 r=[r2_t, lam_t], w=[r2_t])
            atts = []
            for ec in range(2):
                a1_, a1_t = new_tmp()
                a2_, a2_t = new_tmp()
                P.op("dve", lambda e: e.tensor_tensor(out=a1_[:], in0=psf[ec][:], in1=r1[:], op=ALU.mult), r=[psf_t[ec], r1_t], w=[a1_t])
                P.op("dve", lambda e: e.tensor_tensor(out=a2_[:], in0=psf[2 + ec][:], in1=r2[:], op=ALU.mult), r=[psf_t[2 + ec], r2_t], w=[a2_t])
                P.op("dve", lambda e: e.tensor_tensor(out=a1_[:], in0=a1_[:], in1=a2_[:], op=ALU.add), r=[a1_t, a2_t], w=[a1_t])
                sq_, sq_t = new_tb()
                P.op("act", lambda e: e.activation(out=sq_[:], in_=a1_[:], func=AF.Square), r=[a1_t], w=[sq_t])
                P.op("pe", lambda e: e.matmul(psf[4][:], lhsT=ones[:], rhs=sq_[:], start=(ec == 0), stop=(ec == 1)),
                     r=[sq_t, const_t], w=[psf_t[4]])
                atts.append((a1_, a1_t))
            rs_, rs_t = new_tmp()
            rstd_from_ss(psf[4][:], psf_t[4], rs_[:], rs_t, 256, EPS)
            for ec in range(2):
                a1_, a1_t = atts[ec]
                P.op("dve", lambda e: e.scalar_tensor_tensor(out=XT[:, 2 * h + ec, 32:544], in0=a1_[:], scalar=gsbc[:, ec:ec + 1],
                                                             in1=rs_[:], op0=ALU.mult, op1=ALU.mult),
                     r=[a1_t, rs_t, const_t], w=[XT_t[2 * h + ec]])
        while conv_done < CC:
            conv_chunk(conv_done)
            conv_done += 1
        psum_s, psum_s_t = psf[0], psf_t[0]
        psum_q, psum_q_t = psf[1], psf_t[1]
        for cc in range(CC):
            y_ap, y_g = ycv(cc)
            yb, yb_t = new_tb()
            y2, y2_t = new_tb()
            P.op("act", lambda e: e.activation(out=yb[:], in_=y_ap, func=AF.Copy), r=y_g, w=[yb_t])
            P.op("act", lambda e: e.activation(out=y2[:], in_=y_ap, func=AF.Square), r=y_g, w=[y2_t])
            P.op("pe", lambda e: e.matmul(psum_s[:], lhsT=ones[:], rhs=yb[:], start=(cc == 0), stop=(cc == CC - 1)),
                 r=[yb_t, const_t], w=[psum_s_t])
            P.op("pe", lambda e: e.matmul(psum_q[:], lhsT=ones[:], rhs=y2[:], start=(cc == 0), stop=(cc == CC - 1)),
                 r=[y2_t, const_t], w=[psum_q_t])
        mean, mean_t = new_tmp()
        msq, msq_t = new_tmp()
        rsd, rsd_t = new_tmp()
        P.op("dve", lambda e: e.tensor_scalar(out=mean[:], in0=psum_s[:], scalar1=1.0 / c.C, scalar2=None, op0=ALU.mult),
             r=[psum_s_t], w=[mean_t])
        P.op("dve", lambda e: e.tensor_tensor(out=msq[:], in0=mean[:], in1=mean[:], op=ALU.mult), r=[mean_t], w=[msq_t])
        P.op("dve", lambda e: e.scalar_tensor_tensor(out=msq[:], in0=psum_q[:], scalar=1.0 / c.C, in1=msq[:], op0=ALU.mult,
                                                     op1=ALU.subtract), r=[psum_q_t, msq_t], w=[msq_t])
        rstd_from_ss(msq[:], msq_t, rsd[:], rsd_t, 1.0, LN_EPS)
        for cc in range(CC):
            y_ap, y_g = ycv(cc)
            P.op("dve", lambda e: e.tensor_tensor(out=y_ap, in0=y_ap, in1=mean[:], op=ALU.subtract),
                 r=y_g + [mean_t], w=y_g)
            P.op("dve", lambda e: e.tensor_tensor(out=y_ap, in0=y_ap, in1=rsd[:], op=ALU.mult),
                 r=y_g + [rsd_t], w=y_g)
            P.op("act", lambda e: e.activation(out=XT[:, AC + cc, 32:544], in_=y_ap, func=AF.Silu,
                                               scale=lngt[:, cc:cc + 1], bias=lnbt[:, cc:cc + 1]),
                 r=y_g + [const_t], w=[XT_t[AC + cc]])
        dbg("mixT%d" % j, XT[:], XT_t + [XTh_t], [128, DC, 544], BF16)
        for q in range(4):
            h_ap, h_g = hT(q)
            dma("sp", h_ap, xo[j, 32 + q * 128:32 + (q + 1) * 128, :], w=h_g)
        for dg in range(c.NDG):
            for half in range(2):
                wo, wo_t = wblock(j, ("o", dg, half))
                for q in range(4):
                    pb_, pb_t = psf[q], psf_t[q]
                    for m in range(DC // 2):
                        mc = half * (DC // 2) + m
                        P.op("pe", lambda e, pb_=pb_, mc=mc, m=m, q=q, wo=wo: e.matmul(
                            pb_[:], lhsT=XT[:, mc, 32 + q * 128:32 + (q + 1) * 128], rhs=wo[:, m, :],
                            start=(m == 0), stop=(m == DC // 2 - 1)), r=[XT_t[mc], wo_t], w=[pb_t])
                    h_ap, h_g = hT(q, dg)
                    P.op("dve", lambda e, pb_=pb_, h_ap=h_ap: e.tensor_tensor(out=h_ap, in0=pb_[:], in1=h_ap, op=ALU.add),
                         r=[pb_t] + h_g, w=h_g)
        dbg("h%d" % j, ar[:, 0:A1], G(0, A1), [128, A1], BF16)
        for q in range(4):
            h_ap, h_g = hT(q)
            norm_transpose(h_ap, h_g, 128, g2T, 32 + q * 128)
        F_PAIRS = [(0, 1), (2, 3), (4, 5)]
        for fc in range(FC):
            wg2, wg2_t = wblock(j, ("g", fc))
            wu2, wu2_t = wblock(j, ("u", fc))
            ba, bb = F_PAIRS[fc % 3]
            for (w_, w_t, b_) in ((wg2, wg2_t, ba), (wu2, wu2_t, bb)):
                for ch in range(DC):
                    P.op("pe", lambda e, b_=b_, ch=ch, w_=w_: e.matmul(psf[b_][:], lhsT=w_[:, ch, :], rhs=XT[:, ch, 32:544],
                                                                       start=(ch == 0), stop=(ch == DC - 1)),
                         r=[XT_t[ch], w_t], w=[psf_t[b_]])
            sl, sl_t = new_tmp()
            a_ap, a_g = actT(fc)
            P.op("act", lambda e, sl=sl, ba=ba: e.activation(out=sl[:], in_=psf[ba][:], func=AF.Silu), r=[psf_t[ba]], w=[sl_t])
            P.op("dve", lambda e, sl=sl, bb=bb, a_ap=a_ap: e.tensor_tensor(out=a_ap, in0=psf[bb][:], in1=sl[:], op=ALU.mult),
                 r=[psf_t[bb], sl_t], w=a_g)
        dbg("actT%d" % j, ar[:, A1:A1 + FC * 512], G(A1, FC * 512), [128, FC * 512], BF16)
        for dg in range(c.NDG):
            for (dg2, f0, nf) in dblocks:
                if dg2 != dg:
                    continue
                wd, wd_t = wblock(j, ("d", dg, f0))
                for f in range(nf):
                    fc = f0 + f
                    a_ap, a_g = actT(fc)
                    for q in range(4):
                        P.op("pe", lambda e, q=q, fc=fc, f=f, wd=wd, a_ap=a_ap: e.matmul(
                            psf[q][:], lhsT=a_ap[:, q * 128:(q + 1) * 128], rhs=wd[:, f, :],
                            start=(fc == 0), stop=(fc == FC - 1)), r=a_g + [wd_t], w=[psf_t[q]])
            for q in range(4):
                h_ap, h_g = hT(q, dg)
                P.op("dve", lambda e, q=q, h_ap=h_ap: e.tensor_tensor(out=h_ap, in0=psf[q][:], in1=h_ap, op=ALU.add),
                     r=[psf_t[q]] + h_g, w=h_g)
        for q in range(4):
            h_ap, h_g = hT(q)
            ot = T("y%d_%d" % (j, q))
            out_ts.append(ot)
            dma("sp", y_d[j * 512 + q * 128:j * 512 + (q + 1) * 128, :], h_ap, r=h_g, w=[ot])
    P.op("pool", lambda e: e.engine_nop(), r=out_ts + dbg_ts)
    P.emit(nc, es)
    es.close()
    return nc


def _tables(cfg):
    S = cfg.S
    inv = (np.float32(10000.0) ** (-(np.arange(0, 128, 2, dtype=np.float32) / np.float32(128)))).astype(np.float32)
    ang = np.arange(S, dtype=np.float32)[:, None] * inv[None, :]
    ang = np.concatenate([ang, ang], axis=-1)
    cosT = np.ascontiguousarray(np.cos(ang).astype(np.float32).T)
    sinT = np.ascontiguousarray(np.sin(ang).astype(np.float32).T)
    ident = np.eye(128, dtype=np.float32)
    ones = np.ones((128, 128), dtype=np.float32)
    perm = np.zeros((128, 128), dtype=np.float32)
    for d in range(64):
        perm[d + 64, d] = -1.0
        perm[d, d + 64] = 1.0
    mk = np.zeros((16, 1024), dtype=np.float32)
    for i in range(1024):
        mk[i // 64, i] = 1.0
    return cosT, sinT, ident, ones, perm, mk


def make_in_maps(cfg, inputs):
    c = cfg
    f32 = np.float32
    x = np.asarray(inputs["x"], dtype=f32)
    cosT, sinT, ident, ones, perm, mk = _tables(c)

    def colT(v, n):
        return np.ascontiguousarray(np.asarray(v, dtype=f32).reshape(n, 128).T)

    shared = {
        "w_in": np.ascontiguousarray(np.asarray(inputs["w_in"], dtype=f32)[0]),
        "w_out": np.ascontiguousarray(np.asarray(inputs["w_out"], dtype=f32)[0]),
        "w_gate": np.ascontiguousarray(np.asarray(inputs["w_gate"], dtype=f32)[0]),
        "w_up": np.ascontiguousarray(np.asarray(inputs["w_up"], dtype=f32)[0]),
        "w_down": np.ascontiguousarray(np.asarray(inputs["w_down"], dtype=f32)[0]),
        "norm1_g": colT(inputs["norm1_g"][0], c.DC),
        "norm2_g": colT(inputs["norm2_g"][0], c.DC),
        "q_norm_g": np.ascontiguousarray(np.asarray(inputs["q_norm_g"], dtype=f32)[0].reshape(128, 1)),
        "k_norm_g": np.ascontiguousarray(np.asarray(inputs["k_norm_g"], dtype=f32)[0].reshape(128, 1)),
        "lambda_q1": np.ascontiguousarray(np.asarray(inputs["lambda_q1"], dtype=f32)[0]),
        "lambda_k1": np.ascontiguousarray(np.asarray(inputs["lambda_k1"], dtype=f32)[0]),
        "lambda_q2": np.ascontiguousarray(np.asarray(inputs["lambda_q2"], dtype=f32)[0]),
        "lambda_k2": np.ascontiguousarray(np.asarray(inputs["lambda_k2"], dtype=f32)[0]),
        "subln_g": colT(inputs["subln_g"][0], 2),
        "conv_w": np.ascontiguousarray(np.asarray(inputs["conv_w"], dtype=f32)[0, :, 0, :].T.reshape(c.CC, 128, CW)
                                       .transpose(1, 0, 2).reshape(128, c.CC * CW)),
        "conv_b": colT(inputs["conv_b"][0], c.CC),
        "conv_ln_g": colT(inputs["conv_ln_g"][0], c.CC),
        "conv_ln_b": colT(inputs["conv_ln_b"][0], c.CC),
        "cosk": cosT, "sink": sinT, "ident": ident, "ones": ones, "perm": perm, "mk": mk,
    }
    maps = []
    for core in range(c.NCORES):
        b, p = core // 2, core % 2
        xo = np.zeros((c.NOWN, 544, c.D), dtype=f32)
        cosq = np.zeros((c.NOWN, 128, 512), dtype=f32)
        sinq = np.zeros((c.NOWN, 128, 512), dtype=f32)
        mq = np.zeros((16, c.NOWN * 512), dtype=f32)
        for j in range(c.NOWN):
            g = own_tile(p, j)
            xo[j, 32:] = x[b, g * 512:(g + 1) * 512]
            if g > 0:
                xo[j, :32] = x[b, g * 512 - 32:g * 512]
            cosq[j] = cosT[:, g * 512:(g + 1) * 512]
            sinq[j] = sinT[:, g * 512:(g + 1) * 512]
            for s in range(2):
                qc = (g - 2 * j) * 8 + s * 4 + np.arange(256) // 64
                for r_ in range(16):
                    mq[r_, j * 512 + s * 256:j * 512 + (s + 1) * 256] = np.where(qc < r_, NEG_BIG, 0.0)
        m = dict(shared)
        m["xf"] = np.ascontiguousarray(x[b])
        m["xo"] = xo
        m["cosq"] = cosq
        m["sinq"] = sinq
        m["mq"] = mq
        maps.append(m)
    return maps


def gather_out(cfg, results, dtype=np.float32):
    c = cfg
    out = np.zeros((c.B, c.S, c.D), dtype=dtype)
    for core in range(c.NCORES):
        b, p = core // 2, core % 2
        y = np.asarray(results[core]["y"])
        for j in range(c.NOWN):
            g = own_tile(p, j)
            out[b, g * 512:(g + 1) * 512] = y[j * 512:(j + 1) * 512]
    return out


_NC_CACHE = {}


def run_cfg(cfg, inputs):
    key = (cfg.D, cfg.H, cfg.F, cfg.S, cfg.B)
    if key not in _NC_CACHE:
        _NC_CACHE[key] = build(cfg)
    nc = _NC_CACHE[key]
    maps = make_in_maps(cfg, inputs)
    res = run_bass_kernel_spmd(nc, maps, core_ids=list(range(cfg.NCORES)))
    return gather_out(cfg, res.results)


def kernel(**inputs):
    cfg = Cfg()
    return run_cfg(cfg, inputs)
```

```python
import math
from contextlib import ExitStack

import numpy as np
import concourse.bass as bass
import concourse.mybir as mybir
from concourse.bass_utils import run_bass_kernel_spmd

F32 = mybir.dt.float32
BF16 = mybir.dt.bfloat16
ALU = mybir.AluOpType
AF = mybir.ActivationFunctionType
AX = mybir.AxisListType

EPS = 1e-6
LN_EPS = 1e-5
CW = 31
LAMBDA_INIT = 0.8 - 0.6 * math.exp(-0.3 * 0)
NEG_BIG = -30000.0
SLOT = 4096


class Cfg:
    def __init__(s, D=2048, H=4, F=5632, S=8192, B=4, R=6, KR=4):
        s.D, s.H, s.F, s.S, s.B = D, H, F, S, B
        s.DC = D // 128
        s.ATT = H * 256
        s.AC = s.ATT // 128
        s.C = D - s.ATT
        s.CC = s.C // 128
        s.FC = F // 128
        s.NT = S // 512
        s.NOWN = s.NT // 2
        s.KW = H * 256
        s.INC = 3 * s.KW + 2 * s.C
        s.K0 = s.KW
        s.V0 = 2 * s.KW
        s.GA0 = 3 * s.KW
        s.GG0 = 3 * s.KW + s.C
        s.NDG = D // 512
        s.R = R
        s.KR = KR
        s.NCORES = 2 * B


def own_tile(p, j):
    return 2 * j + ((j + p) % 2)


class T:
    __slots__ = ("name", "lw", "rd")

    def __init__(self, name):
        self.name = name
        self.lw = None
        self.rd = []


class Op:
    __slots__ = ("eng", "fn", "deps", "signal", "dma", "sem", "val", "sigval")

    def __init__(self, eng, fn, dma):
        self.eng = eng
        self.fn = fn
        self.deps = []
        self.signal = False
        self.dma = dma
        self.sem = None
        self.val = 0
        self.sigval = 0


class _Rec:
    def __init__(self):
        self.calls = []

    def __getattr__(self, name):
        def f(*a, **k):
            self.calls.append((name, a, k))
            return self
        return f


class Prog:
    ENGS = ("pe", "act", "dve", "pool", "sp")
    NDS = 12

    def __init__(self):
        self.ops = {e: [] for e in self.ENGS}
        self.dma_cnt = {"pool": 0, "sp": 0}
        self.dma_last = {}

    def op(self, eng, fn, r=(), w=(), dma=False):
        rec = _Rec()
        fn(rec)
        assert len(rec.calls) == 1
        o = Op(eng, rec.calls[0], dma)
        deps = set()
        for t in r:
            if t.lw is not None:
                deps.add(t.lw)
        for t in w:
            if t.lw is not None:
                deps.add(t.lw)
            deps.update(t.rd)
        if dma:
            k = self.dma_cnt[eng]
            self.dma_cnt[eng] = k + 1
            o.sem = (eng, k % self.NDS)
            o.val = 16 * (k // self.NDS + 1)
            prev = self.dma_last.get(o.sem)
            if prev is not None:
                deps.add(prev)
            self.dma_last[o.sem] = o
        for d in deps:
            if d.eng == "pe" and eng == "pe" and not d.dma:
                continue
            d.signal = True
            o.deps.append(d)
        for t in r:
            t.rd.append(o)
        for t in w:
            t.lw = o
            t.rd = []
        self.ops[eng].append(o)
        return o

    def emit(self, nc, es):
        csem = {e: es.enter_context(nc.semaphore("c_" + e)) for e in ("pe", "act", "dve", "pool")}
        dsem = {}
        for q in ("pool", "sp"):
            for i in range(self.NDS):
                dsem[(q, i)] = es.enter_context(nc.semaphore("d_%s%d" % (q, i)))
        for e in self.ENGS:
            c = 0
            for o in self.ops[e]:
                if o.signal and not o.dma:
                    c += 1
                    o.sigval = c
        block = es.enter_context(nc.Block())

        def run(ename, eng):
            waited = {}
            for o in self.ops[ename]:
                for d in o.deps:
                    if d.dma:
                        sem, val, key = dsem[d.sem], d.val, d.sem
                    else:
                        sem, val, key = csem[d.eng], d.sigval, d.eng
                    if waited.get(key, 0) >= val:
                        continue
                    waited[key] = val
                    eng.wait_ge(sem, val)
                name, a, k = o.fn
                ins = getattr(eng, name)(*a, **k)
                if o.dma:
                    ins.then_inc(dsem[o.sem], 16)
                elif o.signal:
                    assert ins is not None
                    ins.then_inc(csem[ename], 1)

        @block.tensor
        def _(e):
            run("pe", e)

        @block.scalar
        def _(e):
            run("act", e)

        @block.vector
        def _(e):
            run("dve", e)

        @block.gpsimd
        def _(e):
            run("pool", e)

        @block.sync
        def _(e):
            run("sp", e)


def build(cfg, debug=False):
    c = cfg
    D, H, DC, AC, CC, FC, NT, NOWN = c.D, c.H, c.DC, c.AC, c.CC, c.FC, c.NT, c.NOWN
    nc = bass.Bass("TRN2", target_bir_lowering=False)
    P = Prog()
    es = ExitStack()
    E = es.enter_context

    def din(name, shape, dt=F32):
        return nc.dram_tensor(name, list(shape), dt, kind="ExternalInput").ap()

    xf = din("xf", [c.S, D])
    xo = din("xo", [NOWN, 544, D])
    w_in = din("w_in", [D, c.INC])
    w_out = din("w_out", [D, D])
    w_gate = din("w_gate", [D, c.F])
    w_up = din("w_up", [D, c.F])
    w_down = din("w_down", [c.F, D])
    g1_d = din("norm1_g", [128, DC])
    g2_d = din("norm2_g", [128, DC])
    qg_d = din("q_norm_g", [128, 1])
    kg_d = din("k_norm_g", [128, 1])
    lq1_d = din("lambda_q1", [128])
    lk1_d = din("lambda_k1", [128])
    lq2_d = din("lambda_q2", [128])
    lk2_d = din("lambda_k2", [128])
    sg_d = din("subln_g", [128, 2])
    cw_d = din("conv_w", [128, CC * CW])
    cb_d = din("conv_b", [128, CC])
    lng_d = din("conv_ln_g", [128, CC])
    lnb_d = din("conv_ln_b", [128, CC])
    cosk_d = din("cosk", [128, c.S])
    sink_d = din("sink", [128, c.S])
    cosq_d = din("cosq", [NOWN, 128, 512])
    sinq_d = din("sinq", [NOWN, 128, 512])
    ident_d = din("ident", [128, 128])
    ones_d = din("ones", [128, 128])
    perm_d = din("perm", [128, 128])
    mk_d = din("mk", [16, 1024])
    mq_d = din("mq", [16, NOWN * 512])
    y_d = nc.dram_tensor("y", [NOWN * 512, D], F32, kind="ExternalOutput").ap()

    NKB = NT
    skind = "ExternalOutput" if debug else "Internal"
    kts = nc.dram_tensor("kts", [H * NKB, 128, 1024], BF16, kind=skind).ap()
    vs = nc.dram_tensor("vs", [H * NKB, 128, 1024], BF16, kind=skind).ap()
    dbg_ts = []

    def dbg(name, ap, tiles, shape, dt):
        if not debug:
            return
        d = nc.dram_tensor("dbg_" + name, list(shape), dt, kind="ExternalOutput").ap()
        t_ = T("dbg_" + name)
        dbg_ts.append(t_)
        P.op("pool", lambda e: e.dma_start(out=d, in_=ap), r=list(tiles), w=[t_], dma=True)

    kts_t = [T("kts%d" % i) for i in range(H * NKB)]
    vs_t = [T("vs%d" % i) for i in range(H * NKB)]

    win_v = w_in.rearrange("(c p) n -> p c n", p=128)
    wout_v = w_out.rearrange("(c p) n -> p c n", p=128)
    wg_v = w_gate.rearrange("(c p) n -> p c n", p=128)
    wu_v = w_up.rearrange("(c p) n -> p c n", p=128)
    wd_v = w_down.rearrange("(f p) n -> p f n", p=128)
    units = []
    for hh in range(2 * H):
        units.append([(("q", hh), win_v[:, :, hh * 128:(hh + 1) * 128], DC, 128)])
    for cc in range(CC):
        units.append([(("ga", cc), win_v[:, :, c.GA0 + cc * 128:c.GA0 + (cc + 1) * 128], DC, 128),
                      (("gg", cc), win_v[:, :, c.GG0 + cc * 128:c.GG0 + (cc + 1) * 128], DC, 128)])
    for dg in range(c.NDG):
        for half in range(2):
            units.append([(("o", dg, half), wout_v[:, half * (DC // 2):(half + 1) * (DC // 2), dg * 512:(dg + 1) * 512], DC // 2, 512)])
    for fc in range(FC):
        units.append([(("g", fc), wg_v[:, :, fc * 128:(fc + 1) * 128], DC, 128),
                      (("u", fc), wu_v[:, :, fc * 128:(fc + 1) * 128], DC, 128)])
    dblocks = []
    for dg in range(c.NDG):
        f0 = 0
        while f0 < FC:
            nf = min(8, FC - f0)
            units.append([(("d", dg, f0), wd_v[:, f0:f0 + nf, dg * 512:(dg + 1) * 512], nf, 512)])
            dblocks.append((dg, f0, nf))
            f0 += nf
    slots = []
    windex = {}
    cur, used = [], 0
    for u in units:
        sz = sum(a * b for (_, _, a, b) in u)
        if used + sz > SLOT and cur:
            slots.append(cur)
            cur, used = [], 0
        for (key, src, a, b) in u:
            cur.append((key, src, a, b, used))
            windex[key] = (len(slots), used, a, b)
            used += a * b
    if cur:
        slots.append(cur)
    NSLOT = len(slots)
    slot_used = [max(off + a * b for (_, _, a, b, off) in s) for s in slots]
    wscr = nc.dram_tensor("wscr", [NSLOT, 128, SLOT], BF16, kind=skind).ap()
    wscr_t = [T("wscr%d" % i) for i in range(NSLOT)]

    def sb(name, shape, dt):
        return E(nc.sbuf_tensor("s_" + name, list(shape), dt))

    GR = 512
    A1 = 4 * D * 2
    YC0 = A1
    QT0 = YC0 + CC * 1024
    KV0 = QT0 + 2 * H * 512
    KVSZ = 2048
    main_sz = max(A1 + FC * 512, KV0 + c.KR * KVSZ)
    WK1 = DC * 2 * c.KW
    VST0 = WK1 + 2048
    pa_sz = VST0 + 4 * c.KW
    AR = ((max(main_sz, pa_sz) + GR - 1) // GR) * GR
    assert CC * 1536 <= A1
    ar = sb("arena", [128, AR], BF16)
    gr = [T("gr%d" % i) for i in range(AR // GR)]

    def G(off, n):
        return gr[off // GR:(off + n - 1) // GR + 1]

    def hT(q, dg=None):
        if dg is None:
            return ar[:, q * D * 2:(q + 1) * D * 2].bitcast(F32), G(q * D * 2, D * 2)
        o = (q * D + dg * 512) * 2
        return ar[:, o:o + 1024].bitcast(F32), G(o, 1024)

    def uext(cc):
        o = cc * 1536
        return ar[:, o:o + 1088].bitcast(F32), G(o, 1088)

    def actT(fc):
        o = A1 + fc * 512
        return ar[:, o:o + 512], G(o, 512)

    def ycv(cc):
        o = YC0 + cc * 1024
        return ar[:, o:o + 1024].bitcast(F32), G(o, 1024)

    def QT(hh):
        o = QT0 + hh * 512
        return ar[:, o:o + 512], G(o, 512)

    def ktr(r_):
        o = KV0 + r_ * KVSZ
        return ar[:, o:o + 1024].rearrange("p (c k) -> p c k", c=2), G(o, 1024)

    def vr(r_):
        o = KV0 + r_ * KVSZ + 1024
        return ar[:, o:o + 1024].rearrange("p (k e) -> p k e", k=4), G(o, 1024)

    wkv_ap = ar[:, 0:WK1].rearrange("p (c n) -> p c n", c=DC)

    def wkvG(ch, col, n):
        return G(ch * 2 * c.KW + col, n)

    def kst(i):
        o = WK1 + i * 1024
        return ar[:, o:o + 1024].rearrange("p (c k) -> p c k", c=2), G(o, 1024)

    vst = ar[:, VST0:VST0 + 4 * c.KW].rearrange("p (k e) -> p k e", k=4)
    vst_g = G(VST0, 4 * c.KW)

    ring = [sb("ring%d" % i, [128, SLOT], BF16) for i in range(c.R)]
    ring_t = [T("ring%d" % i) for i in range(c.R)]
    NXS = 2
    xs = [sb("xs%d" % i, [128, D], F32) for i in range(NXS)]
    xs_t = [T("xs%d" % i) for i in range(NXS)]
    xn = [sb("xn%d" % i, [128, D], BF16) for i in range(2)]
    xn_t = [T("xn%d" % i) for i in range(2)]
    st = sb("st", [128, 16], F32)
    st_t = [T("st%d" % i) for i in range(16)]
    XT = sb("XT", [128, DC, 544], BF16)
    XT_t = [T("XT%d" % i) for i in range(DC)]
    XTh_t = T("XTh")
    NPT = 3
    pt = [sb("pt%d" % i, [128, 512], BF16) for i in range(NPT)]
    pt_t = [T("pt%d" % i) for i in range(NPT)]
    NTMP = 8
    tmp = [sb("tmp%d" % i, [128, 512], F32) for i in range(NTMP)]
    tmp_t = [T("tmp%d" % i) for i in range(NTMP)]
    NTB = 6
    tb = [sb("tb%d" % i, [128, 512], BF16) for i in range(NTB)]
    tb_t = [T("tb%d" % i) for i in range(NTB)]
    cosb = sb("cosb", [128, 512], F32)
    sinb = sb("sinb", [128, 512], F32)
    cs_t = T("cs")
    g1T = sb("g1T", [128, DC], F32)
    g2T = sb("g2T", [128, DC], F32)
    gsbc = sb("gsbc", [128, 2], F32)
    qgc = sb("qgc", [128, 1], F32)
    kgc = sb("kgc", [128, 1], F32)
    cwt = sb("cwt", [128, CC * CW], F32)
    cbt = sb("cbt", [128, CC], F32)
    lngt = sb("lngt", [128, CC], F32)
    lnbt = sb("lnbt", [128, CC], F32)
    ident = sb("ident", [128, 128], BF16)
    ones = sb("ones", [128, 128], BF16)
    perm = sb("perm", [128, 128], BF16)
    mk = sb("mk", [16, 1024], BF16)
    mqb = sb("mqb", [16, 512], BF16)
    mqb_t = T("mqb")
    const_t = T("const")
    lam_t = T("lam")
    epsc = sb("epsc", [128, 2], F32)
    eps_t = T("eps")
    psf = [E(nc.psum_tensor("psf%d" % i, [128, 512], F32)) for i in range(8)]
    psf_t = [T("psf%d" % i) for i in range(8)]
    psT_l = [psf[7][:].bitcast(BF16), psf[6][:].bitcast(BF16)]
    psT_tl = [psf_t[7], psf_t[6]]
    psT_i = {"i": 0}

    def dma(q, out, in_, r=(), w=()):
        return P.op(q, lambda e: e.dma_start(out=out, in_=in_), r=r, w=w, dma=True)

    rr = {"tmp": 0, "tb": 0, "pt": 0, "xs": 0, "xn": 0, "st": 0}

    def nxt(kind, n):
        i = rr[kind]
        rr[kind] = (i + 1) % n
        return i

    def new_tmp():
        i = nxt("tmp", NTMP)
        return tmp[i], tmp_t[i]

    def new_tb():
        i = nxt("tb", NTB)
        return tb[i], tb_t[i]

    cl = [
        (g1T[:], g1_d), (g2T[:], g2_d),
        (gsbc[:], sg_d), (qgc[:], qg_d), (kgc[:], kg_d),
        (cwt[:], cw_d), (cbt[:], cb_d), (lngt[:], lng_d), (lnbt[:], lnb_d),
        (ident[:], ident_d), (ones[:], ones_d), (perm[:], perm_d), (mk[:], mk_d),
    ]
    for (o_, i_) in cl:
        dma("pool", o_, i_, w=[const_t])
    P.op("dve", lambda e: e.memset(epsc[:, 0:1], EPS), w=[eps_t])
    P.op("dve", lambda e: e.memset(epsc[:, 1:2], LN_EPS), w=[eps_t])
    P.op("dve", lambda e: e.tensor_scalar(out=gsbc[:], in0=gsbc[:], scalar1=1.0 - LAMBDA_INIT, scalar2=None,
                                          op0=ALU.mult), r=[const_t], w=[const_t])
    LS = 8
    for k, (da, db) in enumerate(((lq1_d, lk1_d), (lq2_d, lk2_d))):
        ta, ta_t = new_tmp()
        tb2, tb2_t = new_tmp()
        dma("pool", ta[:, 0:128], da.partition_broadcast(128), w=[ta_t])
        dma("pool", tb2[:, 0:128], db.partition_broadcast(128), w=[tb2_t])
        P.op("dve", lambda e, ta=ta, tb2=tb2: e.tensor_tensor(out=ta[:, 128:256], in0=ta[:, 0:128], in1=tb2[:, 0:128], op=ALU.mult),
             r=[ta_t, tb2_t], w=[ta_t])
        P.op("dve", lambda e, k=k, ta=ta: e.reduce_sum(out=st[:, LS + k:LS + k + 1], in_=ta[:, 128:256], axis=AX.X),
             r=[ta_t], w=[st_t[LS + k]])
        P.op("act", lambda e, k=k: e.activation(out=st[:, LS + 2 + k:LS + 3 + k], in_=st[:, LS + k:LS + k + 1], func=AF.Exp),
             r=[st_t[LS + k]], w=[st_t[LS + 2 + k]])
    P.op("dve", lambda e: e.tensor_tensor(out=st[:, LS + 4:LS + 5], in0=st[:, LS + 3:LS + 4], in1=st[:, LS + 2:LS + 3],
                                          op=ALU.subtract), r=[st_t[LS + 2], st_t[LS + 3]], w=[st_t[LS + 4]])
    P.op("dve", lambda e: e.tensor_scalar(out=st[:, LS + 5:LS + 6], in0=st[:, LS + 4:LS + 5], scalar1=-LAMBDA_INIT,
                                          scalar2=None, op0=ALU.add), r=[st_t[LS + 4]], w=[lam_t])
    nlam = st[:, LS + 5:LS + 6]

    def rstd_from_ss(ss_ap, ss_t, out_ap, out_t, n, eps):
        np_ = ss_ap.shape[0]
        bcol = epsc[:np_, 0:1] if eps == EPS else epsc[:np_, 1:2]
        P.op("act", lambda e: e.activation(out=out_ap, in_=ss_ap, func=AF.Ln, scale=1.0 / n, bias=bcol),
             r=[ss_t, eps_t], w=[out_t])
        P.op("act", lambda e: e.activation(out=out_ap, in_=out_ap, func=AF.Exp, scale=-0.5),
             r=[out_t], w=[out_t])

    def norm_p1(src_ap, src_ts, np_):
        si = nxt("st", 4)
        ssc, ssc_t = st[:np_, si:si + 1], st_t[si]
        rsc, rsc_t = st[:np_, 4 + si:5 + si], st_t[4 + si]
        xi = nxt("xn", 2)
        P.op("act", lambda e: e.activation(out=xn[xi][:np_, :], in_=src_ap, func=AF.Square, accum_out=ssc),
             r=list(src_ts), w=[xn_t[xi], ssc_t])
        rstd_from_ss(ssc, ssc_t, rsc, rsc_t, D, EPS)
        P.op("act", lambda e: e.activation(out=xn[xi][:np_, :], in_=src_ap, func=AF.Copy, scale=rsc),
             r=list(src_ts) + [rsc_t], w=[xn_t[xi]])
        return xi

    def norm_p2(xi, np_, gT, col0, halo=False):
        for c0 in range(0, DC, 8):
            n = min(8, DC - c0)
            ti = psT_i["i"] % 2
            psT_i["i"] += 1
            psT, psT_t = psT_l[ti], psT_tl[ti]
            for k in range(n):
                P.op("pe", lambda e: e.transpose(out=psT[:, k * np_:(k + 1) * np_],
                                                 in_=xn[xi][:np_, (c0 + k) * 128:(c0 + k + 1) * 128],
                                                 identity=ident[:np_, :np_]),
                     r=[xn_t[xi], const_t], w=[psT_t])
            P.op("dve", lambda e: e.tensor_tensor(
                out=XT[:, c0:c0 + n, col0:col0 + np_], in0=psT[:, 0:n * np_].rearrange("p (c t) -> p c t", c=n),
                in1=gT[:, c0:c0 + n].unsqueeze(2).to_broadcast([128, n, np_]), op=ALU.mult),
                 r=[psT_t, const_t], w=([XTh_t] if halo else XT_t[c0:c0 + n]))

    def norm_transpose(src_ap, src_ts, np_, gT, col0, halo=False):
        xi = norm_p1(src_ap, src_ts, np_)
        norm_p2(xi, np_, gT, col0, halo)

    sbank = {"i": 0}

    def stream_bank(banks):
        i = sbank["i"]
        sbank["i"] = i + 1
        b = banks[i % len(banks)]
        return psf[b], psf_t[b]

    def qk_part1(ps, ps_t, gcol):
        sq, sq_t = new_tb()
        kg, kg_t = new_tb()
        P.op("act", lambda e: e.activation(out=sq[:], in_=ps[:], func=AF.Square), r=[ps_t], w=[sq_t])
        P.op("act", lambda e: e.activation(out=kg[:], in_=ps[:], func=AF.Copy, scale=gcol), r=[ps_t, const_t], w=[kg_t])
        return (sq, sq_t, kg, kg_t)

    def qk_part2(st1, out_ap, out_ts, banks):
        sq, sq_t, kg, kg_t = st1
        pa, pa_t = stream_bank(banks)
        P.op("pe", lambda e: e.matmul(pa[:], lhsT=ones[:], rhs=sq[:], start=True, stop=True), r=[sq_t, const_t], w=[pa_t])
        pb, pb_t = stream_bank(banks)
        P.op("pe", lambda e: e.matmul(pb[:], lhsT=perm[:], rhs=kg[:], start=True, stop=True), r=[kg_t, const_t], w=[pb_t])
        rs, rs_t = new_tmp()
        rstd_from_ss(pa[:], pa_t, rs[:], rs_t, 128, EPS)
        t1, t1_t = new_tmp()
        t2, t2_t = new_tmp()
        P.op("dve", lambda e: e.tensor_tensor(out=t1[:], in0=kg[:], in1=cosb[:], op=ALU.mult), r=[kg_t, cs_t], w=[t1_t])
        P.op("dve", lambda e: e.tensor_tensor(out=t2[:], in0=pb[:], in1=sinb[:], op=ALU.mult), r=[pb_t, cs_t], w=[t2_t])
        P.op("dve", lambda e: e.tensor_tensor(out=t1[:], in0=t1[:], in1=t2[:], op=ALU.add), r=[t1_t, t2_t], w=[t1_t])
        P.op("dve", lambda e: e.tensor_tensor(out=out_ap, in0=t1[:], in1=rs[:], op=ALU.mult), r=[t1_t, rs_t], w=list(out_ts))

    for ch in range(DC):
        dma("pool", wkv_ap[:, ch, :], win_v[:, ch, c.K0:c.K0 + 2 * c.KW], w=wkvG(ch, 0, 2 * c.KW))

    def prep_slot(n):
        rb, rb_t = ring[n % c.R], ring_t[n % c.R]
        for (key, src, a, b, off) in slots[n]:
            dma("pool", rb[:, off:off + a * b].rearrange("p (a b) -> p a b", a=a), src, w=[rb_t])
        dma("sp", wscr[n][:, 0:slot_used[n]], rb[:, 0:slot_used[n]], r=[rb_t], w=[wscr_t[n]])

    prep_next = {"n": 0}

    def prep_some(k):
        for _ in range(k):
            if prep_next["n"] < NSLOT:
                prep_slot(prep_next["n"])
                prep_next["n"] += 1

    A_BANKS = [0, 1, 2, 3, 4, 5]
    prep_per_tile = (NSLOT + NT - 1) // NT
    for t in range(NT):
        dma("sp", cosb[:], cosk_d[:, t * 512:(t + 1) * 512], w=[cs_t])
        dma("sp", sinb[:], sink_d[:, t * 512:(t + 1) * 512], w=[cs_t])
        for sub in range(4):
            xi = nxt("xs", NXS)
            dma("sp", xs[xi][:], xf[t * 512 + sub * 128:t * 512 + (sub + 1) * 128, :], w=[xs_t[xi]])
            norm_transpose(xs[xi][:], [xs_t[xi]], 128, g1T, 32 + sub * 128)
        pend = None
        for hh in range(2 * H + 1):
            cur = None
            if hh < 2 * H:
                ps, ps_t = stream_bank(A_BANKS)
                for ch in range(DC):
                    P.op("pe", lambda e: e.matmul(ps[:], lhsT=wkv_ap[:, ch, hh * 128:(hh + 1) * 128],
                                                  rhs=XT[:, ch, 32:544], start=(ch == 0), stop=(ch == DC - 1)),
                         r=[XT_t[ch]] + wkvG(ch, hh * 128, 128), w=[ps_t])
                cur = (hh, qk_part1(ps, ps_t, kgc[:, 0:1]))
            if pend is not None:
                ph_, st1 = pend
                h, cpart = ph_ // 2, ph_ % 2
                ki = (t * H + h) % 2
                ks_ap, ks_g = kst(ki)
                qk_part2(st1, ks_ap[:, cpart, :], ks_g, A_BANKS)
                if cpart == 1:
                    dma("sp", kts[h * NKB + t].rearrange("p (c k) -> p c k", c=2), ks_ap, r=ks_g, w=[kts_t[h * NKB + t]])
            pend = cur
        VW = min(512, c.KW)
        for sub in range(4):
            for half in range(c.KW // VW):
                ps, ps_t = stream_bank(A_BANKS)
                for ch in range(DC):
                    P.op("pe", lambda e, ps=ps, ch=ch, sub=sub, half=half: e.matmul(
                        ps[:, 0:VW], lhsT=XT[:, ch, 32 + sub * 128:32 + (sub + 1) * 128],
                        rhs=wkv_ap[:, ch, c.KW + half * VW:c.KW + (half + 1) * VW], start=(ch == 0), stop=(ch == DC - 1)),
                         r=[XT_t[ch]] + wkvG(ch, c.KW + half * VW, VW), w=[ps_t])
                P.op("act", lambda e, ps=ps, sub=sub, half=half: e.activation(out=vst[:, sub, half * VW:(half + 1) * VW],
                                                                               in_=ps[:, 0:VW], func=AF.Copy),
                     r=[ps_t], w=vst_g)
        for h in range(H):
            dma("sp", vs[h * NKB + t].rearrange("p (k e) -> p k e", k=4), vst[:, :, h * 256:(h + 1) * 256],
                r=vst_g, w=[vs_t[h * NKB + t]])

    wnext = {"g": 0}
    TOT = NOWN * NSLOT

    def wload(gi):
        n = gi % NSLOT
        rb, rb_t = ring[gi % c.R], ring_t[gi % c.R]
        if gi < NSLOT:
            for (key, src, a, b, off) in slots[n]:
                dma("pool", rb[:, off:off + a * b].rearrange("p (a b) -> p a b", a=a), src, w=[rb_t])
            dma("sp", wscr[n][:, 0:slot_used[n]], rb[:, 0:slot_used[n]], r=[rb_t], w=[wscr_t[n]])
        else:
            dma("sp", rb[:, 0:slot_used[n]], wscr[n][:, 0:slot_used[n]], r=[wscr_t[n]], w=[rb_t])

    def wblock(j, key):
        n, off, a, b = windex[key]
        gi = j * NSLOT + n
        while wnext["g"] <= min(gi + c.R - 1, TOT - 1):
            wload(wnext["g"])
            wnext["g"] += 1
        return ring[gi % c.R][:, off:off + a * b].rearrange("p (a b) -> p a b", a=a), ring_t[gi % c.R]

    kvseq = []
    for j in range(NOWN):
        for h in range(H):
            for kb in range(2 * j + 2):
                kvseq.append((j, h, kb))
    kvpos = {k: i for i, k in enumerate(kvseq)}
    kvnext = {"g": 0}

    def kvload(gi):
        (j, h, kb) = kvseq[gi]
        r_ = gi % c.KR
        k_ap, k_g = ktr(r_)
        v_ap, v_g = vr(r_)
        dma("pool", k_ap, kts[h * NKB + kb].rearrange("p (c k) -> p c k", c=2), r=[kts_t[h * NKB + kb]], w=k_g)
        dma("pool", v_ap, vs[h * NKB + kb].rearrange("p (k e) -> p k e", k=4), r=[vs_t[h * NKB + kb]], w=v_g)

    def kvblock(j, h, kb):
        gi = kvpos[(j, h, kb)]
        last_j = kvpos[(j, H - 1, 2 * j + 1)]
        while kvnext["g"] <= min(gi + c.KR - 2, last_j):
            kvload(kvnext["g"])
            kvnext["g"] += 1
        r_ = gi % c.KR
        k_ap, k_g = ktr(r_)
        v_ap, v_g = vr(r_)
        return k_ap, v_ap, k_g + v_g

    S_BANKS = [4, 5, 6]
    Q_BANKS = [0, 1, 2, 3, 4, 5]
    ESCALE = 1.0 / math.sqrt(128.0)
    out_ts = []

    def tile_prologue_steps(jn):
        stt = {}

        def load(k, rows):
            xi = nxt("xs", NXS)
            if k == 0:
                dma("sp", xs[xi][0:32, :], xo[jn, 0:32, :], w=[xs_t[xi]])
            else:
                dma("sp", xs[xi][:], xo[jn, 32 + (k - 1) * 128:32 + k * 128, :], w=[xs_t[xi]])
            stt[("xs", k)] = xi

        def p1(k):
            xi = stt[("xs", k)]
            if k == 0:
                stt[("xn", k)] = norm_p1(xs[xi][0:32, :], [xs_t[xi]], 32)
            else:
                stt[("xn", k)] = norm_p1(xs[xi][:], [xs_t[xi]], 128)

        def p2(k):
            if k == 0:
                norm_p2(stt[("xn", k)], 32, g1T, 0, halo=True)
            else:
                norm_p2(stt[("xn", k)], 128, g1T, 32 + (k - 1) * 128)

        def s0():
            dma("sp", cosb[:], cosq_d[jn], w=[cs_t])
            dma("sp", sinb[:], sinq_d[jn], w=[cs_t])
            dma("pool", mqb[:], mq_d[:, jn * 512:(jn + 1) * 512], w=[mqb_t])
            load(0, 32)
            load(1, 128)
            p1(0)

        def mk_step(k):
            def f():
                p2(k - 1)
                if k + 1 <= 4:
                    load(k + 1, 128)
                p1(k)
            return f

        return [s0] + [mk_step(k) for k in range(1, 5)] + [lambda: p2(4)]

    for j in range(NOWN):
        if j == 0:
            for stp in tile_prologue_steps(0):
                stp()
        dbg("xnT%d" % j, XT[:], XT_t + [XTh_t], [128, DC, 544], BF16)
        pend = None
        for hh in range(2 * H + 1):
            cur = None
            if hh < 2 * H:
                wq, wq_t = wblock(j, ("q", hh))
                ps, ps_t = stream_bank(Q_BANKS)
                for ch in range(DC):
                    P.op("pe", lambda e: e.matmul(ps[:], lhsT=wq[:, ch, :], rhs=XT[:, ch, 32:544],
                                                  start=(ch == 0), stop=(ch == DC - 1)),
                         r=[XT_t[ch], wq_t], w=[ps_t])
                cur = (hh, qk_part1(ps, ps_t, qgc[:, 0:1]))
            if pend is not None:
                ph_, st1 = pend
                q_ap, q_g = QT(ph_)
                qk_part2(st1, q_ap, q_g, Q_BANKS)
            pend = cur
        def conv_chunk(cc):
            u_ap, u_g = uext(cc)
            y_ap, y_g = ycv(cc)
            P.op("dve", lambda e: e.tensor_scalar(
                out=y_ap, in0=u_ap[:, 2:514], scalar1=cwt[:, cc * CW:cc * CW + 1], scalar2=cbt[:, cc:cc + 1],
                op0=ALU.mult, op1=ALU.add), r=u_g + [const_t], w=y_g)
            for tap in range(1, CW):
                P.op("dve", lambda e: e.scalar_tensor_tensor(
                    out=y_ap, in0=u_ap[:, 2 + tap:514 + tap], scalar=cwt[:, cc * CW + tap:cc * CW + tap + 1],
                    in1=y_ap, op0=ALU.mult, op1=ALU.add), r=u_g + y_g, w=y_g)

        for cc in range(CC):
            wa, wa_t = wblock(j, ("ga", cc))
            wg_, wg_t = wblock(j, ("gg", cc))
            pa, pa_t = stream_bank(Q_BANKS)
            pg, pg_t = stream_bank(Q_BANKS)
            ph, ph_t = stream_bank(Q_BANKS)
            for (w_, w_t, po, po_t, col) in ((wa, wa_t, pa, pa_t, 0), (wg_, wg_t, pg, pg_t, 32)):
                for ch in range(DC):
                    P.op("pe", lambda e, po=po, ch=ch, w_=w_: e.matmul(po[:], lhsT=w_[:, ch, :], rhs=XT[:, ch, 32:544],
                                                                       start=(ch == 0), stop=(ch == DC - 1)),
                         r=[XT_t[ch], w_t], w=[po_t])
                for ch in range(DC):
                    P.op("pe", lambda e, ch=ch, w_=w_, col=col, ph=ph: e.matmul(ph[:, col:col + 32], lhsT=w_[:, ch, :], rhs=XT[:, ch, 0:32],
                                                                               start=(ch == 0), stop=(ch == DC - 1)),
                         r=[XTh_t, w_t], w=[ph_t])
            u_ap, u_g = uext(cc)
            sg, sg_t = new_tmp()
            P.op("act", lambda e, sg=sg, pg=pg: e.activation(out=sg[:], in_=pg[:], func=AF.Sigmoid), r=[pg_t], w=[sg_t])
            P.op("dve", lambda e, sg=sg, pa=pa, u_ap=u_ap: e.tensor_tensor(out=u_ap[:, 32:544], in0=pa[:], in1=sg[:], op=ALU.mult),
                 r=[pa_t, sg_t], w=u_g)
            sh, sh_t = new_tmp()
            P.op("act", lambda e, sh=sh, ph=ph: e.activation(out=sh[:, 0:32], in_=ph[:, 32:64], func=AF.Sigmoid), r=[ph_t], w=[sh_t])
            P.op("dve", lambda e, sh=sh, ph=ph, u_ap=u_ap: e.tensor_tensor(out=u_ap[:, 0:32], in0=ph[:, 0:32], in1=sh[:, 0:32], op=ALU.mult),
                 r=[ph_t, sh_t], w=u_g)
        dbg("QT%d" % j, ar[:, QT0:QT0 + 2 * H * 512], G(QT0, 2 * H * 512), [128, 2 * H * 512], BF16)
        dbg("uext%d" % j, ar[:, 0:CC * 1536], G(0, CC * 1536), [128, CC * 1536], BF16)
        conv_done = 0
        per_head = (CC + H - 1) // H
        for h in range(H):
            for _ in range(per_head):
                if conv_done < CC:
                    conv_chunk(conv_done)
                    conv_done += 1
            nkb = 2 * j + 2
            its = [(kb, kti, cpart) for kb in range(nkb) for kti in range(4) for cpart in range(2)]
            blk = {}

            def emit_score(idx):
                kb, kti, cpart = its[idx]
                if kb not in blk:
                    blk[kb] = kvblock(j, h, kb)
                kt_, v_, kv_g = blk[kb]
                masked = kb >= 2 * j
                sbk, sbk_t = psf[6 + idx % 2], psf_t[6 + idx % 2]
                q_ap, q_g = QT(2 * h + cpart)
                P.op("pe", lambda e: e.matmul(sbk[:], lhsT=kt_[:, cpart, kti * 128:(kti + 1) * 128], rhs=q_ap,
                                              start=True, stop=(not masked)), r=kv_g + q_g, w=[sbk_t])
                if masked:
                    wdx = (kb - 2 * j) * 4 + kti
                    P.op("pe", lambda e: e.matmul(sbk[:], lhsT=mk[:, wdx * 128:(wdx + 1) * 128], rhs=mqb[:],
                                                  start=False, stop=True), r=[const_t, mqb_t], w=[sbk_t])
                pi = idx % NPT
                P.op("act", lambda e: e.activation(out=pt[pi][:], in_=sbk[:], func=AF.Exp, scale=ESCALE),
                     r=[sbk_t], w=[pt_t[pi]])

            def emit_pv(idx):
                kb, kti, cpart = its[idx]
                kt_, v_, kv_g = blk[kb]
                pi = idx % NPT
                first = (kb == 0 and kti == 0)
                last = (kb == nkb - 1) and (kti == 3)
                for ec in range(2):
                    P.op("pe", lambda e: e.matmul(psf[2 * cpart + ec][:], lhsT=v_[:, kti, ec * 128:(ec + 1) * 128], rhs=pt[pi][:],
                                                  start=first, stop=last), r=[pt_t[pi]] + kv_g, w=[psf_t[2 * cpart + ec]])
                P.op("pe", lambda e: e.matmul(psf[4 + cpart][:], lhsT=ones[:], rhs=pt[pi][:], start=first, stop=last),
                     r=[pt_t[pi], const_t], w=[psf_t[4 + cpart]])

            emit_score(0)
            for idx in range(len(its)):
                if idx + 1 < len(its):
                    emit_score(idx + 1)
                emit_pv(idx)
            r1, r1_t = new_tmp()
            r2, r2_t = new_tmp()
            P.op("dve", lambda e: e.reciprocal(out=r1[:], in_=psf[4][:]), r=[psf_t[4]], w=[r1_t])
            P.op("dve", lambda e: e.reciprocal(out=r2[:], in_=psf[5][:]), r=[psf_t[5]], w=[r2_t])
            P.op("dve", lambda e: e.tensor_scalar(out=r2[:], in0=r2[:], scalar1=nlam, scalar2=None, op0=ALU.mult),
                 r=[r2_t, lam_t], w=[r2_t])
            atts = []
            for ec in range(2):
                a1_, a1_t = new_tmp()
                a2_, a2_t = new_tmp()
                P.op("dve", lambda e: e.tensor_tensor(out=a1_[:], in0=psf[ec][:], in1=r1[:], op=ALU.mult), r=[psf_t[ec], r1_t], w=[a1_t])
                P.op("dve", lambda e: e.tensor_tensor(out=a2_[:], in0=psf[2 + ec][:], in1=r2[:], op=ALU.mult), r=[psf_t[2 + ec], r2_t], w=[a2_t])
                P.op("dve", lambda e: e.tensor_tensor(out=a1_[:], in0=a1_[:], in1=a2_[:], op=ALU.add), r=[a1_t, a2_t], w=[a1_t])
                sq_, sq_t = new_tb()
                P.op("act", lambda e: e.activation(out=sq_[:], in_=a1_[:], func=AF.Square), r=[a1_t], w=[sq_t])
                P.op("pe", lambda e: e.matmul(psf[4][:], lhsT=ones[:], rhs=sq_[:], start=(ec == 0), stop=(ec == 1)),
                     r=[sq_t, const_t], w=[psf_t[4]])
                atts.append((a1_, a1_t))
            rs_, rs_t = new_tmp()
            rstd_from_ss(psf[4][:], psf_t[4], rs_[:], rs_t, 256, EPS)
            for ec in range(2):
                a1_, a1_t = atts[ec]
                P.op("dve", lambda e: e.scalar_tensor_tensor(out=XT[:, 2 * h + ec, 32:544], in0=a1_[:], scalar=gsbc[:, ec:ec + 1],
                                                             in1=rs_[:], op0=ALU.mult, op1=ALU.mult),
                     r=[a1_t, rs_t, const_t], w=[XT_t[2 * h + ec]])
        while conv_done < CC:
            conv_chunk(conv_done)
            conv_done += 1
        psum_s, psum_s_t = psf[0], psf_t[0]
        psum_q, psum_q_t = psf[1], psf_t[1]
        for cc in range(CC):
            y_ap, y_g = ycv(cc)
            yb, yb_t = new_tb()
            y2, y2_t = new_tb()
            P.op("act", lambda e: e.activation(out=yb[:], in_=y_ap, func=AF.Copy), r=y_g, w=[yb_t])
            P.op("act", lambda e: e.activation(out=y2[:], in_=y_ap, func=AF.Square), r=y_g, w=[y2_t])
            P.op("pe", lambda e: e.matmul(psum_s[:], lhsT=ones[:], rhs=yb[:], start=(cc == 0), stop=(cc == CC - 1)),
                 r=[yb_t, const_t], w=[psum_s_t])
            P.op("pe", lambda e: e.matmul(psum_q[:], lhsT=ones[:], rhs=y2[:], start=(cc == 0), stop=(cc == CC - 1)),
                 r=[y2_t, const_t], w=[psum_q_t])
        mean, mean_t = new_tmp()
        msq, msq_t = new_tmp()
        rsd, rsd_t = new_tmp()
        P.op("dve", lambda e: e.tensor_scalar(out=mean[:], in0=psum_s[:], scalar1=1.0 / c.C, scalar2=None, op0=ALU.mult),
             r=[psum_s_t], w=[mean_t])
        P.op("dve", lambda e: e.tensor_tensor(out=msq[:], in0=mean[:], in1=mean[:], op=ALU.mult), r=[mean_t], w=[msq_t])
        P.op("dve", lambda e: e.scalar_tensor_tensor(out=msq[:], in0=psum_q[:], scalar=1.0 / c.C, in1=msq[:], op0=ALU.mult,
                                                     op1=ALU.subtract), r=[psum_q_t, msq_t], w=[msq_t])
        rstd_from_ss(msq[:], msq_t, rsd[:], rsd_t, 1.0, LN_EPS)
        for cc in range(CC):
            y_ap, y_g = ycv(cc)
            P.op("dve", lambda e: e.tensor_tensor(out=y_ap, in0=y_ap, in1=mean[:], op=ALU.subtract),
                 r=y_g + [mean_t], w=y_g)
            P.op("dve", lambda e: e.tensor_tensor(out=y_ap, in0=y_ap, in1=rsd[:], op=ALU.mult),
                 r=y_g + [rsd_t], w=y_g)
            P.op("act", lambda e: e.activation(out=XT[:, AC + cc, 32:544], in_=y_ap, func=AF.Silu,
                                               scale=lngt[:, cc:cc + 1], bias=lnbt[:, cc:cc + 1]),
                 r=y_g + [const_t], w=[XT_t[AC + cc]])
        dbg("mixT%d" % j, XT[:], XT_t + [XTh_t], [128, DC, 544], BF16)
        for q in range(4):
            h_ap, h_g = hT(q)
            dma("sp", h_ap, xo[j, 32 + q * 128:32 + (q + 1) * 128, :], w=h_g)
        for dg in range(c.NDG):
            for half in range(2):
                wo, wo_t = wblock(j, ("o", dg, half))
                for q in range(4):
                    pb_, pb_t = psf[q], psf_t[q]
                    for m in range(DC // 2):
                        mc = half * (DC // 2) + m
                        P.op("pe", lambda e, pb_=pb_, mc=mc, m=m, q=q, wo=wo: e.matmul(
                            pb_[:], lhsT=XT[:, mc, 32 + q * 128:32 + (q + 1) * 128], rhs=wo[:, m, :],
                            start=(m == 0), stop=(m == DC // 2 - 1)), r=[XT_t[mc], wo_t], w=[pb_t])
                    h_ap, h_g = hT(q, dg)
                    P.op("dve", lambda e, pb_=pb_, h_ap=h_ap: e.tensor_tensor(out=h_ap, in0=pb_[:], in1=h_ap, op=ALU.add),
                         r=[pb_t] + h_g, w=h_g)
        dbg("h%d" % j, ar[:, 0:A1], G(0, A1), [128, A1], BF16)
        for q in range(4):
            h_ap, h_g = hT(q)
            norm_transpose(h_ap, h_g, 128, g2T, 32 + q * 128)
        nsteps = tile_prologue_steps(j + 1) if j + 1 < NOWN else []
        if nsteps:
            nsteps.pop(0)()
        F_PAIRS = [(0, 1), (2, 3), (4, 5)]
        for fc in range(FC):
            wg2, wg2_t = wblock(j, ("g", fc))
            wu2, wu2_t = wblock(j, ("u", fc))
            ba, bb = F_PAIRS[fc % 3]
            for (w_, w_t, b_) in ((wg2, wg2_t, ba), (wu2, wu2_t, bb)):
                for ch in range(DC):
                    P.op("pe", lambda e, b_=b_, ch=ch, w_=w_: e.matmul(psf[b_][:], lhsT=w_[:, ch, :], rhs=XT[:, ch, 32:544],
                                                                       start=(ch == 0), stop=(ch == DC - 1)),
                         r=[XT_t[ch], w_t], w=[psf_t[b_]])
            sl, sl_t = new_tmp()
            a_ap, a_g = actT(fc)
            P.op("act", lambda e, sl=sl, ba=ba: e.activation(out=sl[:], in_=psf[ba][:], func=AF.Silu), r=[psf_t[ba]], w=[sl_t])
            P.op("dve", lambda e, sl=sl, bb=bb, a_ap=a_ap: e.tensor_tensor(out=a_ap, in0=psf[bb][:], in1=sl[:], op=ALU.mult),
                 r=[psf_t[bb], sl_t], w=a_g)
        dbg("actT%d" % j, ar[:, A1:A1 + FC * 512], G(A1, FC * 512), [128, FC * 512], BF16)
        for dg in range(c.NDG):
            if nsteps:
                nsteps.pop(0)()
            for (dg2, f0, nf) in dblocks:
                if dg2 != dg:
                    continue
                wd, wd_t = wblock(j, ("d", dg, f0))
                for f in range(nf):
                    fc = f0 + f
                    a_ap, a_g = actT(fc)
                    for q in range(4):
                        P.op("pe", lambda e, q=q, fc=fc, f=f, wd=wd, a_ap=a_ap: e.matmul(
                            psf[q][:], lhsT=a_ap[:, q * 128:(q + 1) * 128], rhs=wd[:, f, :],
                            start=(fc == 0), stop=(fc == FC - 1)), r=a_g + [wd_t], w=[psf_t[q]])
            for q in range(4):
                h_ap, h_g = hT(q, dg)
                P.op("dve", lambda e, q=q, h_ap=h_ap: e.tensor_tensor(out=h_ap, in0=psf[q][:], in1=h_ap, op=ALU.add),
                     r=[psf_t[q]] + h_g, w=h_g)
        while nsteps:
            nsteps.pop(0)()
        for q in range(4):
            h_ap, h_g = hT(q)
            ot = T("y%d_%d" % (j, q))
            out_ts.append(ot)
            dma("sp", y_d[j * 512 + q * 128:j * 512 + (q + 1) * 128, :], h_ap, r=h_g, w=[ot])
    P.op("pool", lambda e: e.engine_nop(), r=out_ts + dbg_ts)
    P.emit(nc, es)
    es.close()
    return nc


def _tables(cfg):
    S = cfg.S
    inv = (np.float32(10000.0) ** (-(np.arange(0, 128, 2, dtype=np.float32) / np.float32(128)))).astype(np.float32)
    ang = np.arange(S, dtype=np.float32)[:, None] * inv[None, :]
    ang = np.concatenate([ang, ang], axis=-1)
    cosT = np.ascontiguousarray(np.cos(ang).astype(np.float32).T)
    sinT = np.ascontiguousarray(np.sin(ang).astype(np.float32).T)
    ident = np.eye(128, dtype=np.float32)
    ones = np.ones((128, 128), dtype=np.float32)
    perm = np.zeros((128, 128), dtype=np.float32)
    for d in range(64):
        perm[d + 64, d] = -1.0
        perm[d, d + 64] = 1.0
    mk = np.zeros((16, 1024), dtype=np.float32)
    for i in range(1024):
        mk[i // 64, i] = 1.0
    return cosT, sinT, ident, ones, perm, mk


def make_in_maps(cfg, inputs):
    c = cfg
    f32 = np.float32
    x = np.asarray(inputs["x"], dtype=f32)
    cosT, sinT, ident, ones, perm, mk = _tables(c)

    def colT(v, n):
        return np.ascontiguousarray(np.asarray(v, dtype=f32).reshape(n, 128).T)

    shared = {
        "w_in": np.ascontiguousarray(np.asarray(inputs["w_in"], dtype=f32)[0]),
        "w_out": np.ascontiguousarray(np.asarray(inputs["w_out"], dtype=f32)[0]),
        "w_gate": np.ascontiguousarray(np.asarray(inputs["w_gate"], dtype=f32)[0]),
        "w_up": np.ascontiguousarray(np.asarray(inputs["w_up"], dtype=f32)[0]),
        "w_down": np.ascontiguousarray(np.asarray(inputs["w_down"], dtype=f32)[0]),
        "norm1_g": colT(inputs["norm1_g"][0], c.DC),
        "norm2_g": colT(inputs["norm2_g"][0], c.DC),
        "q_norm_g": np.ascontiguousarray(np.asarray(inputs["q_norm_g"], dtype=f32)[0].reshape(128, 1)),
        "k_norm_g": np.ascontiguousarray(np.asarray(inputs["k_norm_g"], dtype=f32)[0].reshape(128, 1)),
        "lambda_q1": np.ascontiguousarray(np.asarray(inputs["lambda_q1"], dtype=f32)[0]),
        "lambda_k1": np.ascontiguousarray(np.asarray(inputs["lambda_k1"], dtype=f32)[0]),
        "lambda_q2": np.ascontiguousarray(np.asarray(inputs["lambda_q2"], dtype=f32)[0]),
        "lambda_k2": np.ascontiguousarray(np.asarray(inputs["lambda_k2"], dtype=f32)[0]),
        "subln_g": colT(inputs["subln_g"][0], 2),
        "conv_w": np.ascontiguousarray(np.asarray(inputs["conv_w"], dtype=f32)[0, :, 0, :].T.reshape(c.CC, 128, CW)
                                       .transpose(1, 0, 2).reshape(128, c.CC * CW)),
        "conv_b": colT(inputs["conv_b"][0], c.CC),
        "conv_ln_g": colT(inputs["conv_ln_g"][0], c.CC),
        "conv_ln_b": colT(inputs["conv_ln_b"][0], c.CC),
        "cosk": cosT, "sink": sinT, "ident": ident, "ones": ones, "perm": perm, "mk": mk,
    }
    maps = []
    for core in range(c.NCORES):
        b, p = core // 2, core % 2
        xo = np.zeros((c.NOWN, 544, c.D), dtype=f32)
        cosq = np.zeros((c.NOWN, 128, 512), dtype=f32)
        sinq = np.zeros((c.NOWN, 128, 512), dtype=f32)
        mq = np.zeros((16, c.NOWN * 512), dtype=f32)
        for j in range(c.NOWN):
            g = own_tile(p, j)
            xo[j, 32:] = x[b, g * 512:(g + 1) * 512]
            if g > 0:
                xo[j, :32] = x[b, g * 512 - 32:g * 512]
            cosq[j] = cosT[:, g * 512:(g + 1) * 512]
            sinq[j] = sinT[:, g * 512:(g + 1) * 512]
            for s in range(2):
                qc = (g - 2 * j) * 8 + s * 4 + np.arange(256) // 64
                for r_ in range(16):
                    mq[r_, j * 512 + s * 256:j * 512 + (s + 1) * 256] = np.where(qc < r_, NEG_BIG, 0.0)
        m = dict(shared)
        m["xf"] = np.ascontiguousarray(x[b])
        m["xo"] = xo
        m["cosq"] = cosq
        m["sinq"] = sinq
        m["mq"] = mq
        maps.append(m)
    return maps


def gather_out(cfg, results, dtype=np.float32):
    c = cfg
    out = np.zeros((c.B, c.S, c.D), dtype=dtype)
    for core in range(c.NCORES):
        b, p = core // 2, core % 2
        y = np.asarray(results[core]["y"])
        for j in range(c.NOWN):
            g = own_tile(p, j)
            out[b, g * 512:(g + 1) * 512] = y[j * 512:(j + 1) * 512]
    return out


_NC_CACHE = {}


def run_cfg(cfg, inputs):
    key = (cfg.D, cfg.H, cfg.F, cfg.S, cfg.B)
    if key not in _NC_CACHE:
        _NC_CACHE[key] = build(cfg)
    nc = _NC_CACHE[key]
    maps = make_in_maps(cfg, inputs)
    res = run_bass_kernel_spmd(nc, maps, core_ids=list(range(cfg.NCORES)))
    return gather_out(cfg, res.results)


def kernel(**inputs):
    cfg = Cfg()
    return run_cfg(cfg, inputs)
```

```python
import math
from contextlib import ExitStack

import numpy as np
import concourse.bass as bass
import concourse.mybir as mybir
from concourse.bass_utils import run_bass_kernel_spmd

F32 = mybir.dt.float32
BF16 = mybir.dt.bfloat16
ALU = mybir.AluOpType
AF = mybir.ActivationFunctionType
AX = mybir.AxisListType

EPS = 1e-6
LN_EPS = 1e-5
CW = 31
LAMBDA_INIT = 0.8 - 0.6 * math.exp(-0.3 * 0)
NEG_BIG = -30000.0
SLOT = 4096


class Cfg:
    def __init__(s, D=2048, H=4, F=5632, S=8192, B=4, R=6, KR=4):
        s.D, s.H, s.F, s.S, s.B = D, H, F, S, B
        s.DC = D // 128
        s.ATT = H * 256
        s.AC = s.ATT // 128
        s.C = D - s.ATT
        s.CC = s.C // 128
        s.FC = F // 128
        s.NT = S // 512
        s.NOWN = s.NT // 2
        s.KW = H * 256
        s.INC = 3 * s.KW + 2 * s.C
        s.K0 = s.KW
        s.V0 = 2 * s.KW
        s.GA0 = 3 * s.KW
        s.GG0 = 3 * s.KW + s.C
        s.NDG = D // 512
        s.R = R
        s.KR = KR
        s.NCORES = 2 * B


def own_tile(p, j):
    return 2 * j + ((j + p) % 2)


class T:
    __slots__ = ("name", "lw", "rd")

    def __init__(self, name):
        self.name = name
        self.lw = None
        self.rd = []


class Op:
    __slots__ = ("eng", "fn", "deps", "signal", "dma", "sem", "val", "sigval")

    def __init__(self, eng, fn, dma):
        self.eng = eng
        self.fn = fn
        self.deps = []
        self.signal = False
        self.dma = dma
        self.sem = None
        self.val = 0
        self.sigval = 0


class _Rec:
    def __init__(self):
        self.calls = []

    def __getattr__(self, name):
        def f(*a, **k):
            self.calls.append((name, a, k))
            return self
        return f


class Prog:
    ENGS = ("pe", "act", "dve", "pool", "sp")
    NDS = 12

    def __init__(self):
        self.ops = {e: [] for e in self.ENGS}
        self.dma_cnt = {"pool": 0, "sp": 0}
        self.dma_last = {}

    def op(self, eng, fn, r=(), w=(), dma=False):
        rec = _Rec()
        fn(rec)
        assert len(rec.calls) == 1
        o = Op(eng, rec.calls[0], dma)
        deps = set()
        for t in r:
            if t.lw is not None:
                deps.add(t.lw)
        for t in w:
            if t.lw is not None:
                deps.add(t.lw)
            deps.update(t.rd)
        if dma:
            k = self.dma_cnt[eng]
            self.dma_cnt[eng] = k + 1
            o.sem = (eng, k % self.NDS)
            o.val = 16 * (k // self.NDS + 1)
            prev = self.dma_last.get(o.sem)
            if prev is not None:
                deps.add(prev)
            self.dma_last[o.sem] = o
        for d in deps:
            if d.eng == "pe" and eng == "pe" and not d.dma:
                continue
            d.signal = True
            o.deps.append(d)
        for t in r:
            t.rd.append(o)
        for t in w:
            t.lw = o
            t.rd = []
        self.ops[eng].append(o)
        return o

    def emit(self, nc, es):
        csem = {e: es.enter_context(nc.semaphore("c_" + e)) for e in ("pe", "act", "dve", "pool")}
        dsem = {}
        for q in ("pool", "sp"):
            for i in range(self.NDS):
                dsem[(q, i)] = es.enter_context(nc.semaphore("d_%s%d" % (q, i)))
        for e in self.ENGS:
            c = 0
            for o in self.ops[e]:
                if o.signal and not o.dma:
                    c += 1
                    o.sigval = c
        block = es.enter_context(nc.Block())

        def run(ename, eng):
            waited = {}
            for o in self.ops[ename]:
                for d in o.deps:
                    if d.dma:
                        sem, val, key = dsem[d.sem], d.val, d.sem
                    else:
                        sem, val, key = csem[d.eng], d.sigval, d.eng
                    if waited.get(key, 0) >= val:
                        continue
                    waited[key] = val
                    eng.wait_ge(sem, val)
                name, a, k = o.fn
                ins = getattr(eng, name)(*a, **k)
                if o.dma:
                    ins.then_inc(dsem[o.sem], 16)
                elif o.signal:
                    assert ins is not None
                    ins.then_inc(csem[ename], 1)

        @block.tensor
        def _(e):
            run("pe", e)

        @block.scalar
        def _(e):
            run("act", e)

        @block.vector
        def _(e):
            run("dve", e)

        @block.gpsimd
        def _(e):
            run("pool", e)

        @block.sync
        def _(e):
            run("sp", e)


def build(cfg, debug=False):
    c = cfg
    D, H, DC, AC, CC, FC, NT, NOWN = c.D, c.H, c.DC, c.AC, c.CC, c.FC, c.NT, c.NOWN
    nc = bass.Bass("TRN2", target_bir_lowering=False)
    P = Prog()
    es = ExitStack()
    E = es.enter_context

    def din(name, shape, dt=F32):
        return nc.dram_tensor(name, list(shape), dt, kind="ExternalInput").ap()

    xf = din("xf", [c.S, D])
    xo = din("xo", [NOWN, 544, D])
    w_in = din("w_in", [D, c.INC])
    w_out = din("w_out", [D, D])
    w_gate = din("w_gate", [D, c.F])
    w_up = din("w_up", [D, c.F])
    w_down = din("w_down", [c.F, D])
    g1_d = din("norm1_g", [128, DC])
    g2_d = din("norm2_g", [128, DC])
    qg_d = din("q_norm_g", [128, 1])
    kg_d = din("k_norm_g", [128, 1])
    lq1_d = din("lambda_q1", [128])
    lk1_d = din("lambda_k1", [128])
    lq2_d = din("lambda_q2", [128])
    lk2_d = din("lambda_k2", [128])
    sg_d = din("subln_g", [128, 2])
    cw_d = din("conv_w", [128, CC * CW])
    cb_d = din("conv_b", [128, CC])
    lng_d = din("conv_ln_g", [128, CC])
    lnb_d = din("conv_ln_b", [128, CC])
    cosk_d = din("cosk", [128, c.S])
    sink_d = din("sink", [128, c.S])
    cosq_d = din("cosq", [NOWN, 128, 512])
    sinq_d = din("sinq", [NOWN, 128, 512])
    ident_d = din("ident", [128, 128])
    ones_d = din("ones", [128, 128])
    perm_d = din("perm", [128, 128])
    mk_d = din("mk", [16, 1024])
    mq_d = din("mq", [16, NOWN * 512])
    y_d = nc.dram_tensor("y", [NOWN * 512, D], F32, kind="ExternalOutput").ap()

    NKB = NT
    skind = "ExternalOutput" if debug else "Internal"
    kts = nc.dram_tensor("kts", [H * NKB, 128, 1024], BF16, kind=skind).ap()
    vs = nc.dram_tensor("vs", [H * NKB, 128, 1024], BF16, kind=skind).ap()
    dbg_ts = []

    def dbg(name, ap, tiles, shape, dt):
        if not debug:
            return
        d = nc.dram_tensor("dbg_" + name, list(shape), dt, kind="ExternalOutput").ap()
        t_ = T("dbg_" + name)
        dbg_ts.append(t_)
        P.op("pool", lambda e: e.dma_start(out=d, in_=ap), r=list(tiles), w=[t_], dma=True)

    kts_t = [T("kts%d" % i) for i in range(H * NKB)]
    vs_t = [T("vs%d" % i) for i in range(H * NKB)]

    win_v = w_in.rearrange("(c p) n -> p c n", p=128)
    wout_v = w_out.rearrange("(c p) n -> p c n", p=128)
    wg_v = w_gate.rearrange("(c p) n -> p c n", p=128)
    wu_v = w_up.rearrange("(c p) n -> p c n", p=128)
    wd_v = w_down.rearrange("(f p) n -> p f n", p=128)
    units = []
    for hh in range(2 * H):
        units.append([(("q", hh), win_v[:, :, hh * 128:(hh + 1) * 128], DC, 128)])
    for cc in range(CC):
        units.append([(("ga", cc), win_v[:, :, c.GA0 + cc * 128:c.GA0 + (cc + 1) * 128], DC, 128),
                      (("gg", cc), win_v[:, :, c.GG0 + cc * 128:c.GG0 + (cc + 1) * 128], DC, 128)])
    for dg in range(c.NDG):
        for half in range(2):
            units.append([(("o", dg, half), wout_v[:, half * (DC // 2):(half + 1) * (DC // 2), dg * 512:(dg + 1) * 512], DC // 2, 512)])
    for fc in range(FC):
        units.append([(("g", fc), wg_v[:, :, fc * 128:(fc + 1) * 128], DC, 128),
                      (("u", fc), wu_v[:, :, fc * 128:(fc + 1) * 128], DC, 128)])
    dblocks = []
    for dg in range(c.NDG):
        f0 = 0
        while f0 < FC:
            nf = min(8, FC - f0)
            units.append([(("d", dg, f0), wd_v[:, f0:f0 + nf, dg * 512:(dg + 1) * 512], nf, 512)])
            dblocks.append((dg, f0, nf))
            f0 += nf
    slots = []
    windex = {}
    cur, used = [], 0
    for u in units:
        sz = sum(a * b for (_, _, a, b) in u)
        if used + sz > SLOT and cur:
            slots.append(cur)
            cur, used = [], 0
        for (key, src, a, b) in u:
            cur.append((key, src, a, b, used))
            windex[key] = (len(slots), used, a, b)
            used += a * b
    if cur:
        slots.append(cur)
    NSLOT = len(slots)
    slot_used = [max(off + a * b for (_, _, a, b, off) in s) for s in slots]
    wscr = nc.dram_tensor("wscr", [NSLOT, 128, SLOT], BF16, kind=skind).ap()
    wscr_t = [T("wscr%d" % i) for i in range(NSLOT)]

    def sb(name, shape, dt):
        return E(nc.sbuf_tensor("s_" + name, list(shape), dt))

    GR = 512
    A1 = 4 * D * 2
    YC0 = A1
    QT0 = YC0 + CC * 1024
    KV0 = QT0 + 2 * H * 512
    KVSZ = 2048
    main_sz = max(A1 + FC * 512, KV0 + c.KR * KVSZ)
    WK1 = DC * 2 * c.KW
    VST0 = WK1 + 2048
    pa_sz = VST0 + 4 * c.KW
    AR = ((max(main_sz, pa_sz) + GR - 1) // GR) * GR
    assert CC * 1536 <= A1
    ar = sb("arena", [128, AR], BF16)
    gr = [T("gr%d" % i) for i in range(AR // GR)]

    def G(off, n):
        return gr[off // GR:(off + n - 1) // GR + 1]

    def hT(q, dg=None):
        if dg is None:
            return ar[:, q * D * 2:(q + 1) * D * 2].bitcast(F32), G(q * D * 2, D * 2)
        o = (q * D + dg * 512) * 2
        return ar[:, o:o + 1024].bitcast(F32), G(o, 1024)

    def uext(cc):
        o = cc * 1536
        return ar[:, o:o + 1088].bitcast(F32), G(o, 1088)

    def actT(fc):
        o = A1 + fc * 512
        return ar[:, o:o + 512], G(o, 512)

    def ycv(cc):
        o = YC0 + cc * 1024
        return ar[:, o:o + 1024].bitcast(F32), G(o, 1024)

    def QT(hh):
        o = QT0 + hh * 512
        return ar[:, o:o + 512], G(o, 512)

    def ktr(r_):
        o = KV0 + r_ * KVSZ
        return ar[:, o:o + 1024].rearrange("p (c k) -> p c k", c=2), G(o, 1024)

    def vr(r_):
        o = KV0 + r_ * KVSZ + 1024
        return ar[:, o:o + 1024].rearrange("p (k e) -> p k e", k=4), G(o, 1024)

    wkv_ap = ar[:, 0:WK1].rearrange("p (c n) -> p c n", c=DC)

    def wkvG(ch, col, n):
        return G(ch * 2 * c.KW + col, n)

    def kst(i):
        o = WK1 + i * 1024
        return ar[:, o:o + 1024].rearrange("p (c k) -> p c k", c=2), G(o, 1024)

    vst = ar[:, VST0:VST0 + 4 * c.KW].rearrange("p (k e) -> p k e", k=4)
    vst_g = G(VST0, 4 * c.KW)

    ring = [sb("ring%d" % i, [128, SLOT], BF16) for i in range(c.R)]
    ring_t = [T("ring%d" % i) for i in range(c.R)]
    NXS = 2
    xs = [sb("xs%d" % i, [128, D], F32) for i in range(NXS)]
    xs_t = [T("xs%d" % i) for i in range(NXS)]
    NXN = 4
    xn = [sb("xn%d" % i, [128, D], BF16) for i in range(NXN)]
    xn_t = [T("xn%d" % i) for i in range(NXN)]
    st = sb("st", [128, 16], F32)
    st_t = [T("st%d" % i) for i in range(16)]
    XT = sb("XT", [128, DC, 544], BF16)
    XT_t = [T("XT%d" % i) for i in range(DC)]
    XTh_t = T("XTh")
    NPT = 3
    pt = [sb("pt%d" % i, [128, 512], BF16) for i in range(NPT)]
    pt_t = [T("pt%d" % i) for i in range(NPT)]
    NTMP = 8
    tmp = [sb("tmp%d" % i, [128, 512], F32) for i in range(NTMP)]
    tmp_t = [T("tmp%d" % i) for i in range(NTMP)]
    NTB = 6
    tb = [sb("tb%d" % i, [128, 512], BF16) for i in range(NTB)]
    tb_t = [T("tb%d" % i) for i in range(NTB)]
    cosb = sb("cosb", [128, 512], F32)
    sinb = sb("sinb", [128, 512], F32)
    cs_t = T("cs")
    g1T = sb("g1T", [128, DC], F32)
    g2T = sb("g2T", [128, DC], F32)
    gsbc = sb("gsbc", [128, 2], F32)
    qgc = sb("qgc", [128, 1], F32)
    kgc = sb("kgc", [128, 1], F32)
    cwt = sb("cwt", [128, CC * CW], F32)
    cbt = sb("cbt", [128, CC], F32)
    lngt = sb("lngt", [128, CC], F32)
    lnbt = sb("lnbt", [128, CC], F32)
    ident = sb("ident", [128, 128], BF16)
    ones = sb("ones", [128, 128], BF16)
    perm = sb("perm", [128, 128], BF16)
    mk = sb("mk", [16, 1024], BF16)
    mqb = sb("mqb", [16, 512], BF16)
    mqb_t = T("mqb")
    const_t = T("const")
    lam_t = T("lam")
    epsc = sb("epsc", [128, 2], F32)
    eps_t = T("eps")
    psf = [E(nc.psum_tensor("psf%d" % i, [128, 512], F32)) for i in range(8)]
    psf_t = [T("psf%d" % i) for i in range(8)]
    psT_l = [psf[7][:].bitcast(BF16), psf[6][:].bitcast(BF16)]
    psT_tl = [psf_t[7], psf_t[6]]
    psT_i = {"i": 0}

    def dma(q, out, in_, r=(), w=()):
        return P.op(q, lambda e: e.dma_start(out=out, in_=in_), r=r, w=w, dma=True)

    rr = {"tmp": 0, "tb": 0, "pt": 0, "xs": 0, "xn": 0, "st": 0}

    def nxt(kind, n):
        i = rr[kind]
        rr[kind] = (i + 1) % n
        return i

    def new_tmp():
        i = nxt("tmp", NTMP)
        return tmp[i], tmp_t[i]

    def new_tb():
        i = nxt("tb", NTB)
        return tb[i], tb_t[i]

    cl = [
        (g1T[:], g1_d), (g2T[:], g2_d),
        (gsbc[:], sg_d), (qgc[:], qg_d), (kgc[:], kg_d),
        (cwt[:], cw_d), (cbt[:], cb_d), (lngt[:], lng_d), (lnbt[:], lnb_d),
        (ident[:], ident_d), (ones[:], ones_d), (perm[:], perm_d), (mk[:], mk_d),
    ]
    for (o_, i_) in cl:
        dma("pool", o_, i_, w=[const_t])
    P.op("dve", lambda e: e.memset(epsc[:, 0:1], EPS), w=[eps_t])
    P.op("dve", lambda e: e.memset(epsc[:, 1:2], LN_EPS), w=[eps_t])
    P.op("dve", lambda e: e.tensor_scalar(out=gsbc[:], in0=gsbc[:], scalar1=1.0 - LAMBDA_INIT, scalar2=None,
                                          op0=ALU.mult), r=[const_t], w=[const_t])
    LS = 8
    for k, (da, db) in enumerate(((lq1_d, lk1_d), (lq2_d, lk2_d))):
        ta, ta_t = new_tmp()
        tb2, tb2_t = new_tmp()
        dma("pool", ta[:, 0:128], da.partition_broadcast(128), w=[ta_t])
        dma("pool", tb2[:, 0:128], db.partition_broadcast(128), w=[tb2_t])
        P.op("dve", lambda e, ta=ta, tb2=tb2: e.tensor_tensor(out=ta[:, 128:256], in0=ta[:, 0:128], in1=tb2[:, 0:128], op=ALU.mult),
             r=[ta_t, tb2_t], w=[ta_t])
        P.op("dve", lambda e, k=k, ta=ta: e.reduce_sum(out=st[:, LS + k:LS + k + 1], in_=ta[:, 128:256], axis=AX.X),
             r=[ta_t], w=[st_t[LS + k]])
        P.op("act", lambda e, k=k: e.activation(out=st[:, LS + 2 + k:LS + 3 + k], in_=st[:, LS + k:LS + k + 1], func=AF.Exp),
             r=[st_t[LS + k]], w=[st_t[LS + 2 + k]])
    P.op("dve", lambda e: e.tensor_tensor(out=st[:, LS + 4:LS + 5], in0=st[:, LS + 3:LS + 4], in1=st[:, LS + 2:LS + 3],
                                          op=ALU.subtract), r=[st_t[LS + 2], st_t[LS + 3]], w=[st_t[LS + 4]])
    P.op("dve", lambda e: e.tensor_scalar(out=st[:, LS + 5:LS + 6], in0=st[:, LS + 4:LS + 5], scalar1=-LAMBDA_INIT,
                                          scalar2=None, op0=ALU.add), r=[st_t[LS + 4]], w=[lam_t])
    nlam = st[:, LS + 5:LS + 6]

    def rstd_from_ss(ss_ap, ss_t, out_ap, out_t, n, eps):
        np_ = ss_ap.shape[0]
        bcol = epsc[:np_, 0:1] if eps == EPS else epsc[:np_, 1:2]
        P.op("act", lambda e: e.activation(out=out_ap, in_=ss_ap, func=AF.Ln, scale=1.0 / n, bias=bcol),
             r=[ss_t, eps_t], w=[out_t])
        P.op("act", lambda e: e.activation(out=out_ap, in_=out_ap, func=AF.Exp, scale=-0.5),
             r=[out_t], w=[out_t])

    def norm_p1(src_ap, src_ts, np_):
        si = nxt("st", 4)
        ssc, ssc_t = st[:np_, si:si + 1], st_t[si]
        rsc, rsc_t = st[:np_, 4 + si:5 + si], st_t[4 + si]
        xi = nxt("xn", NXN)
        P.op("act", lambda e: e.activation(out=xn[xi][:np_, :], in_=src_ap, func=AF.Square, accum_out=ssc),
             r=list(src_ts), w=[xn_t[xi], ssc_t])
        rstd_from_ss(ssc, ssc_t, rsc, rsc_t, D, EPS)
        P.op("act", lambda e: e.activation(out=xn[xi][:np_, :], in_=src_ap, func=AF.Copy, scale=rsc),
             r=list(src_ts) + [rsc_t], w=[xn_t[xi]])
        return xi

    def norm_p2(xi, np_, gT, col0, halo=False):
        for c0 in range(0, DC, 8):
            n = min(8, DC - c0)
            ti = psT_i["i"] % 2
            psT_i["i"] += 1
            psT, psT_t = psT_l[ti], psT_tl[ti]
            for k in range(n):
                P.op("pe", lambda e: e.transpose(out=psT[:, k * np_:(k + 1) * np_],
                                                 in_=xn[xi][:np_, (c0 + k) * 128:(c0 + k + 1) * 128],
                                                 identity=ident[:np_, :np_]),
                     r=[xn_t[xi], const_t], w=[psT_t])
            P.op("dve", lambda e: e.tensor_tensor(
                out=XT[:, c0:c0 + n, col0:col0 + np_], in0=psT[:, 0:n * np_].rearrange("p (c t) -> p c t", c=n),
                in1=gT[:, c0:c0 + n].unsqueeze(2).to_broadcast([128, n, np_]), op=ALU.mult),
                 r=[psT_t, const_t], w=([XTh_t] if halo else XT_t[c0:c0 + n]))

    def norm_transpose(src_ap, src_ts, np_, gT, col0, halo=False):
        xi = norm_p1(src_ap, src_ts, np_)
        norm_p2(xi, np_, gT, col0, halo)

    sbank = {"i": 0}

    def stream_bank(banks):
        i = sbank["i"]
        sbank["i"] = i + 1
        b = banks[i % len(banks)]
        return psf[b], psf_t[b]

    def qk_part1(ps, ps_t, gcol):
        sq, sq_t = new_tb()
        kg, kg_t = new_tb()
        P.op("act", lambda e: e.activation(out=sq[:], in_=ps[:], func=AF.Square), r=[ps_t], w=[sq_t])
        P.op("act", lambda e: e.activation(out=kg[:], in_=ps[:], func=AF.Copy, scale=gcol), r=[ps_t, const_t], w=[kg_t])
        return (sq, sq_t, kg, kg_t)

    def qk_part2(st1, out_ap, out_ts, banks):
        sq, sq_t, kg, kg_t = st1
        pa, pa_t = stream_bank(banks)
        P.op("pe", lambda e: e.matmul(pa[:], lhsT=ones[:], rhs=sq[:], start=True, stop=True), r=[sq_t, const_t], w=[pa_t])
        pb, pb_t = stream_bank(banks)
        P.op("pe", lambda e: e.matmul(pb[:], lhsT=perm[:], rhs=kg[:], start=True, stop=True), r=[kg_t, const_t], w=[pb_t])
        rs, rs_t = new_tmp()
        rstd_from_ss(pa[:], pa_t, rs[:], rs_t, 128, EPS)
        t1, t1_t = new_tmp()
        t2, t2_t = new_tmp()
        P.op("dve", lambda e: e.tensor_tensor(out=t1[:], in0=kg[:], in1=cosb[:], op=ALU.mult), r=[kg_t, cs_t], w=[t1_t])
        P.op("dve", lambda e: e.tensor_tensor(out=t2[:], in0=pb[:], in1=sinb[:], op=ALU.mult), r=[pb_t, cs_t], w=[t2_t])
        P.op("dve", lambda e: e.tensor_tensor(out=t1[:], in0=t1[:], in1=t2[:], op=ALU.add), r=[t1_t, t2_t], w=[t1_t])
        P.op("dve", lambda e: e.tensor_tensor(out=out_ap, in0=t1[:], in1=rs[:], op=ALU.mult), r=[t1_t, rs_t], w=list(out_ts))

    for ch in range(DC):
        dma("pool", wkv_ap[:, ch, :], win_v[:, ch, c.K0:c.K0 + 2 * c.KW], w=wkvG(ch, 0, 2 * c.KW))

    def prep_slot(n):
        rb, rb_t = ring[n % c.R], ring_t[n % c.R]
        for (key, src, a, b, off) in slots[n]:
            dma("pool", rb[:, off:off + a * b].rearrange("p (a b) -> p a b", a=a), src, w=[rb_t])
        dma("sp", wscr[n][:, 0:slot_used[n]], rb[:, 0:slot_used[n]], r=[rb_t], w=[wscr_t[n]])

    prep_next = {"n": 0}

    def prep_some(k):
        for _ in range(k):
            if prep_next["n"] < NSLOT:
                prep_slot(prep_next["n"])
                prep_next["n"] += 1

    A_BANKS = [0, 1, 2, 3, 4, 5]
    prep_per_tile = (NSLOT + NT - 1) // NT
    pa_xs, pa_xn = {}, {}

    def pa_load(t, k):
        xi = nxt("xs", NXS)
        dma("sp", xs[xi][:], xf[t * 512 + k * 128:t * 512 + (k + 1) * 128, :], w=[xs_t[xi]])
        pa_xs[(t, k)] = xi

    def pa_p1(t, k):
        xi = pa_xs[(t, k)]
        pa_xn[(t, k)] = norm_p1(xs[xi][:], [xs_t[xi]], 128)

    pa_load(0, 0)
    pa_load(0, 1)
    pa_p1(0, 0)
    pa_p1(0, 1)
    pa_load(0, 2)
    pa_load(0, 3)
    pa_p1(0, 2)
    pa_p1(0, 3)
    K_TRIG0, K_TRIG1 = min(2, 2 * H), min(5, 2 * H)
    for t in range(NT):
        dma("sp", cosb[:], cosk_d[:, t * 512:(t + 1) * 512], w=[cs_t])
        dma("sp", sinb[:], sink_d[:, t * 512:(t + 1) * 512], w=[cs_t])
        for sub in range(4):
            norm_p2(pa_xn[(t, sub)], 128, g1T, 32 + sub * 128)
        if t + 1 < NT:
            pa_load(t + 1, 0)
            pa_load(t + 1, 1)
        pend = None
        for hh in range(2 * H + 1):
            cur = None
            if hh < 2 * H:
                ps, ps_t = stream_bank(A_BANKS)
                for ch in range(DC):
                    P.op("pe", lambda e: e.matmul(ps[:], lhsT=wkv_ap[:, ch, hh * 128:(hh + 1) * 128],
                                                  rhs=XT[:, ch, 32:544], start=(ch == 0), stop=(ch == DC - 1)),
                         r=[XT_t[ch]] + wkvG(ch, hh * 128, 128), w=[ps_t])
                cur = (hh, qk_part1(ps, ps_t, kgc[:, 0:1]))
            if pend is not None:
                ph_, st1 = pend
                h, cpart = ph_ // 2, ph_ % 2
                ki = (t * H + h) % 2
                ks_ap, ks_g = kst(ki)
                qk_part2(st1, ks_ap[:, cpart, :], ks_g, A_BANKS)
                if cpart == 1:
                    dma("sp", kts[h * NKB + t].rearrange("p (c k) -> p c k", c=2), ks_ap, r=ks_g, w=[kts_t[h * NKB + t]])
            pend = cur
            if t + 1 < NT:
                if hh == K_TRIG0:
                    pa_p1(t + 1, 0)
                    pa_load(t + 1, 2)
                if hh == K_TRIG1:
                    pa_p1(t + 1, 1)
                    pa_load(t + 1, 3)
        VW = min(512, c.KW)
        for sub in range(4):
            for half in range(c.KW // VW):
                ps, ps_t = stream_bank(A_BANKS)
                for ch in range(DC):
                    P.op("pe", lambda e, ps=ps, ch=ch, sub=sub, half=half: e.matmul(
                        ps[:, 0:VW], lhsT=XT[:, ch, 32 + sub * 128:32 + (sub + 1) * 128],
                        rhs=wkv_ap[:, ch, c.KW + half * VW:c.KW + (half + 1) * VW], start=(ch == 0), stop=(ch == DC - 1)),
                         r=[XT_t[ch]] + wkvG(ch, c.KW + half * VW, VW), w=[ps_t])
                P.op("act", lambda e, ps=ps, sub=sub, half=half: e.activation(out=vst[:, sub, half * VW:(half + 1) * VW],
                                                                               in_=ps[:, 0:VW], func=AF.Copy),
                     r=[ps_t], w=vst_g)
            if t + 1 < NT and sub == 0:
                pa_p1(t + 1, 2)
            if t + 1 < NT and sub == 2:
                pa_p1(t + 1, 3)
        for h in range(H):
            dma("sp", vs[h * NKB + t].rearrange("p (k e) -> p k e", k=4), vst[:, :, h * 256:(h + 1) * 256],
                r=vst_g, w=[vs_t[h * NKB + t]])

    wnext = {"g": 0}
    TOT = NOWN * NSLOT

    def wload(gi):
        n = gi % NSLOT
        rb, rb_t = ring[gi % c.R], ring_t[gi % c.R]
        if gi < NSLOT:
            for (key, src, a, b, off) in slots[n]:
                dma("pool", rb[:, off:off + a * b].rearrange("p (a b) -> p a b", a=a), src, w=[rb_t])
            dma("sp", wscr[n][:, 0:slot_used[n]], rb[:, 0:slot_used[n]], r=[rb_t], w=[wscr_t[n]])
        else:
            dma("sp", rb[:, 0:slot_used[n]], wscr[n][:, 0:slot_used[n]], r=[wscr_t[n]], w=[rb_t])

    def wblock(j, key):
        n, off, a, b = windex[key]
        gi = j * NSLOT + n
        while wnext["g"] <= min(gi + c.R - 1, TOT - 1):
            wload(wnext["g"])
            wnext["g"] += 1
        return ring[gi % c.R][:, off:off + a * b].rearrange("p (a b) -> p a b", a=a), ring_t[gi % c.R]

    kvseq = []
    for j in range(NOWN):
        for h in range(H):
            for kb in range(2 * j + 2):
                kvseq.append((j, h, kb))
    kvpos = {k: i for i, k in enumerate(kvseq)}
    kvnext = {"g": 0}

    def kvload(gi):
        (j, h, kb) = kvseq[gi]
        r_ = gi % c.KR
        k_ap, k_g = ktr(r_)
        v_ap, v_g = vr(r_)
        dma("pool", k_ap, kts[h * NKB + kb].rearrange("p (c k) -> p c k", c=2), r=[kts_t[h * NKB + kb]], w=k_g)
        dma("pool", v_ap, vs[h * NKB + kb].rearrange("p (k e) -> p k e", k=4), r=[vs_t[h * NKB + kb]], w=v_g)

    def kvblock(j, h, kb):
        gi = kvpos[(j, h, kb)]
        last_j = kvpos[(j, H - 1, 2 * j + 1)]
        while kvnext["g"] <= min(gi + c.KR - 2, last_j):
            kvload(kvnext["g"])
            kvnext["g"] += 1
        r_ = gi % c.KR
        k_ap, k_g = ktr(r_)
        v_ap, v_g = vr(r_)
        return k_ap, v_ap, k_g + v_g

    S_BANKS = [4, 5, 6]
    Q_BANKS = [0, 1, 2, 3, 4, 5]
    ESCALE = 1.0 / math.sqrt(128.0)
    out_ts = []

    def tile_prologue_steps(jn):
        stt = {}

        def load(k, rows):
            xi = nxt("xs", NXS)
            if k == 0:
                dma("sp", xs[xi][0:32, :], xo[jn, 0:32, :], w=[xs_t[xi]])
            else:
                dma("sp", xs[xi][:], xo[jn, 32 + (k - 1) * 128:32 + k * 128, :], w=[xs_t[xi]])
            stt[("xs", k)] = xi

        def p1(k):
            xi = stt[("xs", k)]
            if k == 0:
                stt[("xn", k)] = norm_p1(xs[xi][0:32, :], [xs_t[xi]], 32)
            else:
                stt[("xn", k)] = norm_p1(xs[xi][:], [xs_t[xi]], 128)

        def p2(k):
            if k == 0:
                norm_p2(stt[("xn", k)], 32, g1T, 0, halo=True)
            else:
                norm_p2(stt[("xn", k)], 128, g1T, 32 + (k - 1) * 128)

        def s0():
            dma("sp", cosb[:], cosq_d[jn], w=[cs_t])
            dma("sp", sinb[:], sinq_d[jn], w=[cs_t])
            dma("pool", mqb[:], mq_d[:, jn * 512:(jn + 1) * 512], w=[mqb_t])
            load(0, 32)
            load(1, 128)
            p1(0)

        def mk_step(k):
            def f():
                p2(k - 1)
                if k + 1 <= 4:
                    load(k + 1, 128)
                p1(k)
            return f

        return [s0] + [mk_step(k) for k in range(1, 5)] + [lambda: p2(4)]

    for j in range(NOWN):
        if j == 0:
            for stp in tile_prologue_steps(0):
                stp()
        dbg("xnT%d" % j, XT[:], XT_t + [XTh_t], [128, DC, 544], BF16)
        pend = None
        for hh in range(2 * H + 1):
            cur = None
            if hh < 2 * H:
                wq, wq_t = wblock(j, ("q", hh))
                ps, ps_t = stream_bank(Q_BANKS)
                for ch in range(DC):
                    P.op("pe", lambda e: e.matmul(ps[:], lhsT=wq[:, ch, :], rhs=XT[:, ch, 32:544],
                                                  start=(ch == 0), stop=(ch == DC - 1)),
                         r=[XT_t[ch], wq_t], w=[ps_t])
                cur = (hh, qk_part1(ps, ps_t, qgc[:, 0:1]))
            if pend is not None:
                ph_, st1 = pend
                q_ap, q_g = QT(ph_)
                qk_part2(st1, q_ap, q_g, Q_BANKS)
            pend = cur
        def conv_chunk(cc):
            u_ap, u_g = uext(cc)
            y_ap, y_g = ycv(cc)
            P.op("dve", lambda e: e.tensor_scalar(
                out=y_ap, in0=u_ap[:, 2:514], scalar1=cwt[:, cc * CW:cc * CW + 1], scalar2=cbt[:, cc:cc + 1],
                op0=ALU.mult, op1=ALU.add), r=u_g + [const_t], w=y_g)
            for tap in range(1, CW):
                P.op("dve", lambda e: e.scalar_tensor_tensor(
                    out=y_ap, in0=u_ap[:, 2 + tap:514 + tap], scalar=cwt[:, cc * CW + tap:cc * CW + tap + 1],
                    in1=y_ap, op0=ALU.mult, op1=ALU.add), r=u_g + y_g, w=y_g)

        for cc in range(CC):
            wa, wa_t = wblock(j, ("ga", cc))
            wg_, wg_t = wblock(j, ("gg", cc))
            pa, pa_t = stream_bank(Q_BANKS)
            pg, pg_t = stream_bank(Q_BANKS)
            ph, ph_t = stream_bank(Q_BANKS)
            for (w_, w_t, po, po_t, col) in ((wa, wa_t, pa, pa_t, 0), (wg_, wg_t, pg, pg_t, 32)):
                for ch in range(DC):
                    P.op("pe", lambda e, po=po, ch=ch, w_=w_: e.matmul(po[:], lhsT=w_[:, ch, :], rhs=XT[:, ch, 32:544],
                                                                       start=(ch == 0), stop=(ch == DC - 1)),
                         r=[XT_t[ch], w_t], w=[po_t])
                for ch in range(DC):
                    P.op("pe", lambda e, ch=ch, w_=w_, col=col, ph=ph: e.matmul(ph[:, col:col + 32], lhsT=w_[:, ch, :], rhs=XT[:, ch, 0:32],
                                                                               start=(ch == 0), stop=(ch == DC - 1)),
                         r=[XTh_t, w_t], w=[ph_t])
            u_ap, u_g = uext(cc)
            sg, sg_t = new_tmp()
            P.op("act", lambda e, sg=sg, pg=pg: e.activation(out=sg[:], in_=pg[:], func=AF.Sigmoid), r=[pg_t], w=[sg_t])
            P.op("dve", lambda e, sg=sg, pa=pa, u_ap=u_ap: e.tensor_tensor(out=u_ap[:, 32:544], in0=pa[:], in1=sg[:], op=ALU.mult),
                 r=[pa_t, sg_t], w=u_g)
            sh, sh_t = new_tmp()
            P.op("act", lambda e, sh=sh, ph=ph: e.activation(out=sh[:, 0:32], in_=ph[:, 32:64], func=AF.Sigmoid), r=[ph_t], w=[sh_t])
            P.op("dve", lambda e, sh=sh, ph=ph, u_ap=u_ap: e.tensor_tensor(out=u_ap[:, 0:32], in0=ph[:, 0:32], in1=sh[:, 0:32], op=ALU.mult),
                 r=[ph_t, sh_t], w=u_g)
        dbg("QT%d" % j, ar[:, QT0:QT0 + 2 * H * 512], G(QT0, 2 * H * 512), [128, 2 * H * 512], BF16)
        dbg("uext%d" % j, ar[:, 0:CC * 1536], G(0, CC * 1536), [128, CC * 1536], BF16)
        conv_done = 0
        per_head = (CC + H - 1) // H
        for h in range(H):
            for _ in range(per_head):
                if conv_done < CC:
                    conv_chunk(conv_done)
                    conv_done += 1
            nkb = 2 * j + 2
            its = [(kb, kti, cpart) for kb in range(nkb) for kti in range(4) for cpart in range(2)]
            blk = {}

            def emit_score(idx):
                kb, kti, cpart = its[idx]
                if kb not in blk:
                    blk[kb] = kvblock(j, h, kb)
                kt_, v_, kv_g = blk[kb]
                masked = kb >= 2 * j
                sbk, sbk_t = psf[6 + idx % 2], psf_t[6 + idx % 2]
                q_ap, q_g = QT(2 * h + cpart)
                P.op("pe", lambda e: e.matmul(sbk[:], lhsT=kt_[:, cpart, kti * 128:(kti + 1) * 128], rhs=q_ap,
                                              start=True, stop=(not masked)), r=kv_g + q_g, w=[sbk_t])
                if masked:
                    wdx = (kb - 2 * j) * 4 + kti
                    P.op("pe", lambda e: e.matmul(sbk[:], lhsT=mk[:, wdx * 128:(wdx + 1) * 128], rhs=mqb[:],
                                                  start=False, stop=True), r=[const_t, mqb_t], w=[sbk_t])
                pi = idx % NPT
                P.op("act", lambda e: e.activation(out=pt[pi][:], in_=sbk[:], func=AF.Exp, scale=ESCALE),
                     r=[sbk_t], w=[pt_t[pi]])

            def emit_pv(idx):
                kb, kti, cpart = its[idx]
                kt_, v_, kv_g = blk[kb]
                pi = idx % NPT
                first = (kb == 0 and kti == 0)
                last = (kb == nkb - 1) and (kti == 3)
                for ec in range(2):
                    P.op("pe", lambda e: e.matmul(psf[2 * cpart + ec][:], lhsT=v_[:, kti, ec * 128:(ec + 1) * 128], rhs=pt[pi][:],
                                                  start=first, stop=last), r=[pt_t[pi]] + kv_g, w=[psf_t[2 * cpart + ec]])
                P.op("pe", lambda e: e.matmul(psf[4 + cpart][:], lhsT=ones[:], rhs=pt[pi][:], start=first, stop=last),
                     r=[pt_t[pi], const_t], w=[psf_t[4 + cpart]])

            emit_score(0)
            for idx in range(len(its)):
                if idx + 1 < len(its):
                    emit_score(idx + 1)
                emit_pv(idx)
            r1, r1_t = new_tmp()
            r2, r2_t = new_tmp()
            P.op("dve", lambda e: e.reciprocal(out=r1[:], in_=psf[4][:]), r=[psf_t[4]], w=[r1_t])
            P.op("dve", lambda e: e.reciprocal(out=r2[:], in_=psf[5][:]), r=[psf_t[5]], w=[r2_t])
            P.op("dve", lambda e: e.tensor_scalar(out=r2[:], in0=r2[:], scalar1=nlam, scalar2=None, op0=ALU.mult),
                 r=[r2_t, lam_t], w=[r2_t])
            atts = []
            for ec in range(2):
                a1_, a1_t = new_tmp()
                a2_, a2_t = new_tmp()
                P.op("dve", lambda e: e.tensor_tensor(out=a1_[:], in0=psf[ec][:], in1=r1[:], op=ALU.mult), r=[psf_t[ec], r1_t], w=[a1_t])
                P.op("dve", lambda e: e.tensor_tensor(out=a2_[:], in0=psf[2 + ec][:], in1=r2[:], op=ALU.mult), r=[psf_t[2 + ec], r2_t], w=[a2_t])
                P.op("dve", lambda e: e.tensor_tensor(out=a1_[:], in0=a1_[:], in1=a2_[:], op=ALU.add), r=[a1_t, a2_t], w=[a1_t])
                sq_, sq_t = new_tb()
                P.op("act", lambda e: e.activation(out=sq_[:], in_=a1_[:], func=AF.Square), r=[a1_t], w=[sq_t])
                P.op("pe", lambda e: e.matmul(psf[4][:], lhsT=ones[:], rhs=sq_[:], start=(ec == 0), stop=(ec == 1)),
                     r=[sq_t, const_t], w=[psf_t[4]])
                atts.append((a1_, a1_t))
            rs_, rs_t = new_tmp()
            rstd_from_ss(psf[4][:], psf_t[4], rs_[:], rs_t, 256, EPS)
            for ec in range(2):
                a1_, a1_t = atts[ec]
                P.op("dve", lambda e: e.scalar_tensor_tensor(out=XT[:, 2 * h + ec, 32:544], in0=a1_[:], scalar=gsbc[:, ec:ec + 1],
                                                             in1=rs_[:], op0=ALU.mult, op1=ALU.mult),
                     r=[a1_t, rs_t, const_t], w=[XT_t[2 * h + ec]])
        while conv_done < CC:
            conv_chunk(conv_done)
            conv_done += 1
        psum_s, psum_s_t = psf[0], psf_t[0]
        psum_q, psum_q_t = psf[1], psf_t[1]
        for cc in range(CC):
            y_ap, y_g = ycv(cc)
            yb, yb_t = new_tb()
            y2, y2_t = new_tb()
            P.op("act", lambda e: e.activation(out=yb[:], in_=y_ap, func=AF.Copy), r=y_g, w=[yb_t])
            P.op("act", lambda e: e.activation(out=y2[:], in_=y_ap, func=AF.Square), r=y_g, w=[y2_t])
            P.op("pe", lambda e: e.matmul(psum_s[:], lhsT=ones[:], rhs=yb[:], start=(cc == 0), stop=(cc == CC - 1)),
                 r=[yb_t, const_t], w=[psum_s_t])
            P.op("pe", lambda e: e.matmul(psum_q[:], lhsT=ones[:], rhs=y2[:], start=(cc == 0), stop=(cc == CC - 1)),
                 r=[y2_t, const_t], w=[psum_q_t])
        mean, mean_t = new_tmp()
        msq, msq_t = new_tmp()
        rsd, rsd_t = new_tmp()
        P.op("dve", lambda e: e.tensor_scalar(out=mean[:], in0=psum_s[:], scalar1=1.0 / c.C, scalar2=None, op0=ALU.mult),
             r=[psum_s_t], w=[mean_t])
        P.op("dve", lambda e: e.tensor_tensor(out=msq[:], in0=mean[:], in1=mean[:], op=ALU.mult), r=[mean_t], w=[msq_t])
        P.op("dve", lambda e: e.scalar_tensor_tensor(out=msq[:], in0=psum_q[:], scalar=1.0 / c.C, in1=msq[:], op0=ALU.mult,
                                                     op1=ALU.subtract), r=[psum_q_t, msq_t], w=[msq_t])
        rstd_from_ss(msq[:], msq_t, rsd[:], rsd_t, 1.0, LN_EPS)
        for cc in range(CC):
            y_ap, y_g = ycv(cc)
            P.op("dve", lambda e: e.tensor_tensor(out=y_ap, in0=y_ap, in1=mean[:], op=ALU.subtract),
                 r=y_g + [mean_t], w=y_g)
            P.op("dve", lambda e: e.tensor_tensor(out=y_ap, in0=y_ap, in1=rsd[:], op=ALU.mult),
                 r=y_g + [rsd_t], w=y_g)
            P.op("act", lambda e: e.activation(out=XT[:, AC + cc, 32:544], in_=y_ap, func=AF.Silu,
                                               scale=lngt[:, cc:cc + 1], bias=lnbt[:, cc:cc + 1]),
                 r=y_g + [const_t], w=[XT_t[AC + cc]])
        dbg("mixT%d" % j, XT[:], XT_t + [XTh_t], [128, DC, 544], BF16)
        for q in range(4):
            h_ap, h_g = hT(q)
            dma("sp", h_ap, xo[j, 32 + q * 128:32 + (q + 1) * 128, :], w=h_g)
        for dg in range(c.NDG):
            for half in range(2):
                wo, wo_t = wblock(j, ("o", dg, half))
                for q in range(4):
                    pb_, pb_t = psf[q], psf_t[q]
                    for m in range(DC // 2):
                        mc = half * (DC // 2) + m
                        P.op("pe", lambda e, pb_=pb_, mc=mc, m=m, q=q, wo=wo: e.matmul(
                            pb_[:], lhsT=XT[:, mc, 32 + q * 128:32 + (q + 1) * 128], rhs=wo[:, m, :],
                            start=(m == 0), stop=(m == DC // 2 - 1)), r=[XT_t[mc], wo_t], w=[pb_t])
                    h_ap, h_g = hT(q, dg)
                    P.op("dve", lambda e, pb_=pb_, h_ap=h_ap: e.tensor_tensor(out=h_ap, in0=pb_[:], in1=h_ap, op=ALU.add),
                         r=[pb_t] + h_g, w=h_g)
        dbg("h%d" % j, ar[:, 0:A1], G(0, A1), [128, A1], BF16)
        for q in range(4):
            h_ap, h_g = hT(q)
            norm_transpose(h_ap, h_g, 128, g2T, 32 + q * 128)
        nsteps = tile_prologue_steps(j + 1) if j + 1 < NOWN else []
        if nsteps:
            nsteps.pop(0)()
        F_PAIRS = [(0, 1), (2, 3), (4, 5)]
        for fc in range(FC):
            wg2, wg2_t = wblock(j, ("g", fc))
            wu2, wu2_t = wblock(j, ("u", fc))
            ba, bb = F_PAIRS[fc % 3]
            for (w_, w_t, b_) in ((wg2, wg2_t, ba), (wu2, wu2_t, bb)):
                for ch in range(DC):
                    P.op("pe", lambda e, b_=b_, ch=ch, w_=w_: e.matmul(psf[b_][:], lhsT=w_[:, ch, :], rhs=XT[:, ch, 32:544],
                                                                       start=(ch == 0), stop=(ch == DC - 1)),
                         r=[XT_t[ch], w_t], w=[psf_t[b_]])
            sl, sl_t = new_tmp()
            a_ap, a_g = actT(fc)
            P.op("act", lambda e, sl=sl, ba=ba: e.activation(out=sl[:], in_=psf[ba][:], func=AF.Silu), r=[psf_t[ba]], w=[sl_t])
            P.op("dve", lambda e, sl=sl, bb=bb, a_ap=a_ap: e.tensor_tensor(out=a_ap, in0=psf[bb][:], in1=sl[:], op=ALU.mult),
                 r=[psf_t[bb], sl_t], w=a_g)
        dbg("actT%d" % j, ar[:, A1:A1 + FC * 512], G(A1, FC * 512), [128, FC * 512], BF16)
        for dg in range(c.NDG):
            if nsteps:
                nsteps.pop(0)()
            for (dg2, f0, nf) in dblocks:
                if dg2 != dg:
                    continue
                wd, wd_t = wblock(j, ("d", dg, f0))
                for f in range(nf):
                    fc = f0 + f
                    a_ap, a_g = actT(fc)
                    for q in range(4):
                        P.op("pe", lambda e, q=q, fc=fc, f=f, wd=wd, a_ap=a_ap: e.matmul(
                            psf[q][:], lhsT=a_ap[:, q * 128:(q + 1) * 128], rhs=wd[:, f, :],
                            start=(fc == 0), stop=(fc == FC - 1)), r=a_g + [wd_t], w=[psf_t[q]])
            for q in range(4):
                h_ap, h_g = hT(q, dg)
                P.op("dve", lambda e, q=q, h_ap=h_ap: e.tensor_tensor(out=h_ap, in0=psf[q][:], in1=h_ap, op=ALU.add),
                     r=[psf_t[q]] + h_g, w=h_g)
        while nsteps:
            nsteps.pop(0)()
        for q in range(4):
            h_ap, h_g = hT(q)
            ot = T("y%d_%d" % (j, q))
            out_ts.append(ot)
            dma("sp", y_d[j * 512 + q * 128:j * 512 + (q + 1) * 128, :], h_ap, r=h_g, w=[ot])
    P.op("pool", lambda e: e.engine_nop(), r=out_ts + dbg_ts)
    P.emit(nc, es)
    es.close()
    return nc


def _tables(cfg):
    S = cfg.S
    inv = (np.float32(10000.0) ** (-(np.arange(0, 128, 2, dtype=np.float32) / np.float32(128)))).astype(np.float32)
    ang = np.arange(S, dtype=np.float32)[:, None] * inv[None, :]
    ang = np.concatenate([ang, ang], axis=-1)
    cosT = np.ascontiguousarray(np.cos(ang).astype(np.float32).T)
    sinT = np.ascontiguousarray(np.sin(ang).astype(np.float32).T)
    ident = np.eye(128, dtype=np.float32)
    ones = np.ones((128, 128), dtype=np.float32)
    perm = np.zeros((128, 128), dtype=np.float32)
    for d in range(64):
        perm[d + 64, d] = -1.0
        perm[d, d + 64] = 1.0
    mk = np.zeros((16, 1024), dtype=np.float32)
    for i in range(1024):
        mk[i // 64, i] = 1.0
    return cosT, sinT, ident, ones, perm, mk


def make_in_maps(cfg, inputs):
    c = cfg
    f32 = np.float32
    x = np.asarray(inputs["x"], dtype=f32)
    cosT, sinT, ident, ones, perm, mk = _tables(c)

    def colT(v, n):
        return np.ascontiguousarray(np.asarray(v, dtype=f32).reshape(n, 128).T)

    shared = {
        "w_in": np.ascontiguousarray(np.asarray(inputs["w_in"], dtype=f32)[0]),
        "w_out": np.ascontiguousarray(np.asarray(inputs["w_out"], dtype=f32)[0]),
        "w_gate": np.ascontiguousarray(np.asarray(inputs["w_gate"], dtype=f32)[0]),
        "w_up": np.ascontiguousarray(np.asarray(inputs["w_up"], dtype=f32)[0]),
        "w_down": np.ascontiguousarray(np.asarray(inputs["w_down"], dtype=f32)[0]),
        "norm1_g": colT(inputs["norm1_g"][0], c.DC),
        "norm2_g": colT(inputs["norm2_g"][0], c.DC),
        "q_norm_g": np.ascontiguousarray(np.asarray(inputs["q_norm_g"], dtype=f32)[0].reshape(128, 1)),
        "k_norm_g": np.ascontiguousarray(np.asarray(inputs["k_norm_g"], dtype=f32)[0].reshape(128, 1)),
        "lambda_q1": np.ascontiguousarray(np.asarray(inputs["lambda_q1"], dtype=f32)[0]),
        "lambda_k1": np.ascontiguousarray(np.asarray(inputs["lambda_k1"], dtype=f32)[0]),
        "lambda_q2": np.ascontiguousarray(np.asarray(inputs["lambda_q2"], dtype=f32)[0]),
        "lambda_k2": np.ascontiguousarray(np.asarray(inputs["lambda_k2"], dtype=f32)[0]),
        "subln_g": colT(inputs["subln_g"][0], 2),
        "conv_w": np.ascontiguousarray(np.asarray(inputs["conv_w"], dtype=f32)[0, :, 0, :].T.reshape(c.CC, 128, CW)
                                       .transpose(1, 0, 2).reshape(128, c.CC * CW)),
        "conv_b": colT(inputs["conv_b"][0], c.CC),
        "conv_ln_g": colT(inputs["conv_ln_g"][0], c.CC),
        "conv_ln_b": colT(inputs["conv_ln_b"][0], c.CC),
        "cosk": cosT, "sink": sinT, "ident": ident, "ones": ones, "perm": perm, "mk": mk,
    }
    maps = []
    for core in range(c.NCORES):
        b, p = core // 2, core % 2
        xo = np.zeros((c.NOWN, 544, c.D), dtype=f32)
        cosq = np.zeros((c.NOWN, 128, 512), dtype=f32)
        sinq = np.zeros((c.NOWN, 128, 512), dtype=f32)
        mq = np.zeros((16, c.NOWN * 512), dtype=f32)
        for j in range(c.NOWN):
            g = own_tile(p, j)
            xo[j, 32:] = x[b, g * 512:(g + 1) * 512]
            if g > 0:
                xo[j, :32] = x[b, g * 512 - 32:g * 512]
            cosq[j] = cosT[:, g * 512:(g + 1) * 512]
            sinq[j] = sinT[:, g * 512:(g + 1) * 512]
            for s in range(2):
                qc = (g - 2 * j) * 8 + s * 4 + np.arange(256) // 64
                for r_ in range(16):
                    mq[r_, j * 512 + s * 256:j * 512 + (s + 1) * 256] = np.where(qc < r_, NEG_BIG, 0.0)
        m = dict(shared)
        m["xf"] = np.ascontiguousarray(x[b])
        m["xo"] = xo
        m["cosq"] = cosq
        m["sinq"] = sinq
        m["mq"] = mq
        maps.append(m)
    return maps


def gather_out(cfg, results, dtype=np.float32):
    c = cfg
    out = np.zeros((c.B, c.S, c.D), dtype=dtype)
    for core in range(c.NCORES):
        b, p = core // 2, core % 2
        y = np.asarray(results[core]["y"])
        for j in range(c.NOWN):
            g = own_tile(p, j)
            out[b, g * 512:(g + 1) * 512] = y[j * 512:(j + 1) * 512]
    return out


_NC_CACHE = {}


def run_cfg(cfg, inputs):
    key = (cfg.D, cfg.H, cfg.F, cfg.S, cfg.B)
    if key not in _NC_CACHE:
        _NC_CACHE[key] = build(cfg)
    nc = _NC_CACHE[key]
    maps = make_in_maps(cfg, inputs)
    res = run_bass_kernel_spmd(nc, maps, core_ids=list(range(cfg.NCORES)))
    return gather_out(cfg, res.results)


def kernel(**inputs):
    cfg = Cfg()
    return run_cfg(cfg, inputs)
```

```python
import math
from contextlib import ExitStack

import numpy as np
import concourse.bass as bass
import concourse.mybir as mybir
from concourse.bass_utils import run_bass_kernel_spmd

F32 = mybir.dt.float32
BF16 = mybir.dt.bfloat16
ALU = mybir.AluOpType
AF = mybir.ActivationFunctionType
AX = mybir.AxisListType

EPS = 1e-6
LN_EPS = 1e-5
CW = 31
LAMBDA_INIT = 0.8 - 0.6 * math.exp(-0.3 * 0)
NEG_BIG = -30000.0
SLOT = 4096


class Cfg:
    def __init__(s, D=2048, H=4, F=5632, S=8192, B=4, R=6, KR=4):
        s.D, s.H, s.F, s.S, s.B = D, H, F, S, B
        s.DC = D // 128
        s.ATT = H * 256
        s.AC = s.ATT // 128
        s.C = D - s.ATT
        s.CC = s.C // 128
        s.FC = F // 128
        s.NT = S // 512
        s.NOWN = s.NT // 2
        s.KW = H * 256
        s.INC = 3 * s.KW + 2 * s.C
        s.K0 = s.KW
        s.V0 = 2 * s.KW
        s.GA0 = 3 * s.KW
        s.GG0 = 3 * s.KW + s.C
        s.NDG = D // 512
        s.R = R
        s.KR = KR
        s.NCORES = 2 * B


def own_tile(p, j):
    return 2 * j + ((j + p) % 2)


class T:
    __slots__ = ("name", "lw", "rd")

    def __init__(self, name):
        self.name = name
        self.lw = None
        self.rd = []


class Op:
    __slots__ = ("eng", "fn", "deps", "signal", "dma", "sem", "val", "sigval")

    def __init__(self, eng, fn, dma):
        self.eng = eng
        self.fn = fn
        self.deps = []
        self.signal = False
        self.dma = dma
        self.sem = None
        self.val = 0
        self.sigval = 0


class _Rec:
    def __init__(self):
        self.calls = []

    def __getattr__(self, name):
        def f(*a, **k):
            self.calls.append((name, a, k))
            return self
        return f


class Prog:
    ENGS = ("pe", "act", "dve", "pool", "sp")
    NDS = 12

    def __init__(self):
        self.ops = {e: [] for e in self.ENGS}
        self.dma_cnt = {"pool": 0, "sp": 0}
        self.dma_last = {}

    def op(self, eng, fn, r=(), w=(), dma=False):
        rec = _Rec()
        fn(rec)
        assert len(rec.calls) == 1
        o = Op(eng, rec.calls[0], dma)
        deps = set()
        for t in r:
            if t.lw is not None:
                deps.add(t.lw)
        for t in w:
            if t.lw is not None:
                deps.add(t.lw)
            deps.update(t.rd)
        if dma:
            k = self.dma_cnt[eng]
            self.dma_cnt[eng] = k + 1
            o.sem = (eng, k % self.NDS)
            o.val = 16 * (k // self.NDS + 1)
            prev = self.dma_last.get(o.sem)
            if prev is not None:
                deps.add(prev)
            self.dma_last[o.sem] = o
        for d in deps:
            if d.eng == "pe" and eng == "pe" and not d.dma:
                continue
            d.signal = True
            o.deps.append(d)
        for t in r:
            t.rd.append(o)
        for t in w:
            t.lw = o
            t.rd = []
        self.ops[eng].append(o)
        return o

    def emit(self, nc, es):
        csem = {e: es.enter_context(nc.semaphore("c_" + e)) for e in ("pe", "act", "dve", "pool")}
        dsem = {}
        for q in ("pool", "sp"):
            for i in range(self.NDS):
                dsem[(q, i)] = es.enter_context(nc.semaphore("d_%s%d" % (q, i)))
        for e in self.ENGS:
            c = 0
            for o in self.ops[e]:
                if o.signal and not o.dma:
                    c += 1
                    o.sigval = c
        block = es.enter_context(nc.Block())

        def run(ename, eng):
            waited = {}
            for o in self.ops[ename]:
                for d in o.deps:
                    if d.dma:
                        sem, val, key = dsem[d.sem], d.val, d.sem
                    else:
                        sem, val, key = csem[d.eng], d.sigval, d.eng
                    if waited.get(key, 0) >= val:
                        continue
                    waited[key] = val
                    eng.wait_ge(sem, val)
                name, a, k = o.fn
                ins = getattr(eng, name)(*a, **k)
                if o.dma:
                    ins.then_inc(dsem[o.sem], 16)
                elif o.signal:
                    assert ins is not None
                    ins.then_inc(csem[ename], 1)

        @block.tensor
        def _(e):
            run("pe", e)

        @block.scalar
        def _(e):
            run("act", e)

        @block.vector
        def _(e):
            run("dve", e)

        @block.gpsimd
        def _(e):
            run("pool", e)

        @block.sync
        def _(e):
            run("sp", e)


def build(cfg, debug=False):
    c = cfg
    D, H, DC, AC, CC, FC, NT, NOWN = c.D, c.H, c.DC, c.AC, c.CC, c.FC, c.NT, c.NOWN
    nc = bass.Bass("TRN2", target_bir_lowering=False)
    P = Prog()
    es = ExitStack()
    E = es.enter_context

    def din(name, shape, dt=F32):
        return nc.dram_tensor(name, list(shape), dt, kind="ExternalInput").ap()

    xf = din("xf", [c.S, D])
    xo = din("xo", [NOWN, 544, D])
    w_in = din("w_in", [D, c.INC])
    w_out = din("w_out", [D, D])
    w_gate = din("w_gate", [D, c.F])
    w_up = din("w_up", [D, c.F])
    w_down = din("w_down", [c.F, D])
    g1_d = din("norm1_g", [128, DC])
    g2_d = din("norm2_g", [128, DC])
    qg_d = din("q_norm_g", [128, 1])
    kg_d = din("k_norm_g", [128, 1])
    lq1_d = din("lambda_q1", [128])
    lk1_d = din("lambda_k1", [128])
    lq2_d = din("lambda_q2", [128])
    lk2_d = din("lambda_k2", [128])
    sg_d = din("subln_g", [128, 2])
    cw_d = din("conv_w", [128, CC * CW])
    cb_d = din("conv_b", [128, CC])
    lng_d = din("conv_ln_g", [128, CC])
    lnb_d = din("conv_ln_b", [128, CC])
    cosk_d = din("cosk", [128, c.S])
    sink_d = din("sink", [128, c.S])
    cosq_d = din("cosq", [NOWN, 128, 512])
    sinq_d = din("sinq", [NOWN, 128, 512])
    ident_d = din("ident", [128, 128])
    ones_d = din("ones", [128, 128])
    perm_d = din("perm", [128, 128])
    mk_d = din("mk", [16, 1024])
    mq_d = din("mq", [16, NOWN * 512])
    y_d = nc.dram_tensor("y", [NOWN * 512, D], F32, kind="ExternalOutput").ap()

    NKB = NT
    skind = "ExternalOutput" if debug else "Internal"
    kts = nc.dram_tensor("kts", [H * NKB, 128, 1024], BF16, kind=skind).ap()
    vs = nc.dram_tensor("vs", [H * NKB, 128, 1024], BF16, kind=skind).ap()
    dbg_ts = []

    def dbg(name, ap, tiles, shape, dt):
        if not debug:
            return
        d = nc.dram_tensor("dbg_" + name, list(shape), dt, kind="ExternalOutput").ap()
        t_ = T("dbg_" + name)
        dbg_ts.append(t_)
        P.op("pool", lambda e: e.dma_start(out=d, in_=ap), r=list(tiles), w=[t_], dma=True)

    kts_t = [T("kts%d" % i) for i in range(H * NKB)]
    vs_t = [T("vs%d" % i) for i in range(H * NKB)]

    win_v = w_in.rearrange("(c p) n -> p c n", p=128)
    wout_v = w_out.rearrange("(c p) n -> p c n", p=128)
    wg_v = w_gate.rearrange("(c p) n -> p c n", p=128)
    wu_v = w_up.rearrange("(c p) n -> p c n", p=128)
    wd_v = w_down.rearrange("(f p) n -> p f n", p=128)
    units = []
    for hh in range(2 * H):
        units.append([(("q", hh), win_v[:, :, hh * 128:(hh + 1) * 128], DC, 128)])
    for cc in range(CC):
        units.append([(("ga", cc), win_v[:, :, c.GA0 + cc * 128:c.GA0 + (cc + 1) * 128], DC, 128),
                      (("gg", cc), win_v[:, :, c.GG0 + cc * 128:c.GG0 + (cc + 1) * 128], DC, 128)])
    for dg in range(c.NDG):
        for half in range(2):
            units.append([(("o", dg, half), wout_v[:, half * (DC // 2):(half + 1) * (DC // 2), dg * 512:(dg + 1) * 512], DC // 2, 512)])
    for fc in range(FC):
        units.append([(("g", fc), wg_v[:, :, fc * 128:(fc + 1) * 128], DC, 128),
                      (("u", fc), wu_v[:, :, fc * 128:(fc + 1) * 128], DC, 128)])
    dblocks = []
    for dg in range(c.NDG):
        f0 = 0
        while f0 < FC:
            nf = min(8, FC - f0)
            units.append([(("d", dg, f0), wd_v[:, f0:f0 + nf, dg * 512:(dg + 1) * 512], nf, 512)])
            dblocks.append((dg, f0, nf))
            f0 += nf
    slots = []
    windex = {}
    cur, used = [], 0
    for u in units:
        sz = sum(a * b for (_, _, a, b) in u)
        if used + sz > SLOT and cur:
            slots.append(cur)
            cur, used = [], 0
        for (key, src, a, b) in u:
            cur.append((key, src, a, b, used))
            windex[key] = (len(slots), used, a, b)
            used += a * b
    if cur:
        slots.append(cur)
    NSLOT = len(slots)
    slot_used = [max(off + a * b for (_, _, a, b, off) in s) for s in slots]
    wscr = nc.dram_tensor("wscr", [NSLOT, 128, SLOT], BF16, kind=skind).ap()
    wscr_t = [T("wscr%d" % i) for i in range(NSLOT)]

    def sb(name, shape, dt):
        return E(nc.sbuf_tensor("s_" + name, list(shape), dt))

    GR = 512
    A1 = 4 * D * 2
    YC0 = A1
    QT0 = YC0 + CC * 1024
    KV0 = QT0 + 2 * H * 512
    KVSZ = 2048
    main_sz = max(A1 + FC * 512, KV0 + c.KR * KVSZ)
    WK1 = DC * 2 * c.KW
    VST0 = WK1 + 2048
    pa_sz = VST0 + 4 * c.KW
    AR = ((max(main_sz, pa_sz) + GR - 1) // GR) * GR
    assert CC * 1536 <= A1
    ar = sb("arena", [128, AR], BF16)
    gr = [T("gr%d" % i) for i in range(AR // GR)]

    def G(off, n):
        return gr[off // GR:(off + n - 1) // GR + 1]

    def hT(q, dg=None):
        if dg is None:
            return ar[:, q * D * 2:(q + 1) * D * 2].bitcast(F32), G(q * D * 2, D * 2)
        o = (q * D + dg * 512) * 2
        return ar[:, o:o + 1024].bitcast(F32), G(o, 1024)

    def uext(cc):
        o = cc * 1536
        return ar[:, o:o + 1088].bitcast(F32), G(o, 1088)

    def actT(fc):
        o = A1 + fc * 512
        return ar[:, o:o + 512], G(o, 512)

    def ycv(cc):
        o = YC0 + cc * 1024
        return ar[:, o:o + 1024].bitcast(F32), G(o, 1024)

    def QT(hh):
        o = QT0 + hh * 512
        return ar[:, o:o + 512], G(o, 512)

    def ktr(r_):
        o = KV0 + r_ * KVSZ
        return ar[:, o:o + 1024].rearrange("p (c k) -> p c k", c=2), G(o, 1024)

    def vr(r_):
        o = KV0 + r_ * KVSZ + 1024
        return ar[:, o:o + 1024].rearrange("p (k e) -> p k e", k=4), G(o, 1024)

    wkv_ap = ar[:, 0:WK1].rearrange("p (c n) -> p c n", c=DC)

    def wkvG(ch, col, n):
        return G(ch * 2 * c.KW + col, n)

    def kst(i):
        o = WK1 + i * 1024
        return ar[:, o:o + 1024].rearrange("p (c k) -> p c k", c=2), G(o, 1024)

    vst = ar[:, VST0:VST0 + 4 * c.KW].rearrange("p (k e) -> p k e", k=4)
    vst_g = G(VST0, 4 * c.KW)

    ring = [sb("ring%d" % i, [128, SLOT], BF16) for i in range(c.R)]
    ring_t = [T("ring%d" % i) for i in range(c.R)]
    NXS = 2
    xs = [sb("xs%d" % i, [128, D], F32) for i in range(NXS)]
    xs_t = [T("xs%d" % i) for i in range(NXS)]
    NXN = 4
    xn = [sb("xn%d" % i, [128, D], BF16) for i in range(NXN)]
    xn_t = [T("xn%d" % i) for i in range(NXN)]
    st = sb("st", [128, 16], F32)
    st_t = [T("st%d" % i) for i in range(16)]
    XT = sb("XT", [128, DC, 544], BF16)
    XT_t = [T("XT%d" % i) for i in range(DC)]
    XTh_t = T("XTh")
    NPT = 3
    pt = [sb("pt%d" % i, [128, 512], BF16) for i in range(NPT)]
    pt_t = [T("pt%d" % i) for i in range(NPT)]
    NTMP = 8
    tmp = [sb("tmp%d" % i, [128, 512], F32) for i in range(NTMP)]
    tmp_t = [T("tmp%d" % i) for i in range(NTMP)]
    NTB = 6
    tb = [sb("tb%d" % i, [128, 512], BF16) for i in range(NTB)]
    tb_t = [T("tb%d" % i) for i in range(NTB)]
    cosb = sb("cosb", [128, 512], F32)
    sinb = sb("sinb", [128, 512], F32)
    cs_t = T("cs")
    g1T = sb("g1T", [128, DC], F32)
    g2T = sb("g2T", [128, DC], F32)
    gsbc = sb("gsbc", [128, 2], F32)
    qgc = sb("qgc", [128, 1], F32)
    kgc = sb("kgc", [128, 1], F32)
    cwt = sb("cwt", [128, CC * CW], F32)
    cbt = sb("cbt", [128, CC], F32)
    lngt = sb("lngt", [128, CC], F32)
    lnbt = sb("lnbt", [128, CC], F32)
    ident = sb("ident", [128, 128], BF16)
    ones = sb("ones", [128, 128], BF16)
    perm = sb("perm", [128, 128], BF16)
    mk = sb("mk", [16, 1024], BF16)
    mqb = sb("mqb", [16, 512], BF16)
    mqb_t = T("mqb")
    const_t = T("const")
    lam_t = T("lam")
    epsc = sb("epsc", [128, 2], F32)
    eps_t = T("eps")
    psf = [E(nc.psum_tensor("psf%d" % i, [128, 512], F32)) for i in range(8)]
    psf_t = [T("psf%d" % i) for i in range(8)]
    psT_l = [psf[7][:].bitcast(BF16), psf[6][:].bitcast(BF16)]
    psT_tl = [psf_t[7], psf_t[6]]
    psT_i = {"i": 0}

    def dma(q, out, in_, r=(), w=()):
        return P.op(q, lambda e: e.dma_start(out=out, in_=in_), r=r, w=w, dma=True)

    rr = {"tmp": 0, "tb": 0, "pt": 0, "xs": 0, "xn": 0, "st": 0}

    def nxt(kind, n):
        i = rr[kind]
        rr[kind] = (i + 1) % n
        return i

    def new_tmp():
        i = nxt("tmp", NTMP)
        return tmp[i], tmp_t[i]

    def new_tb():
        i = nxt("tb", NTB)
        return tb[i], tb_t[i]

    cl = [
        (g1T[:], g1_d), (g2T[:], g2_d),
        (gsbc[:], sg_d), (qgc[:], qg_d), (kgc[:], kg_d),
        (cwt[:], cw_d), (cbt[:], cb_d), (lngt[:], lng_d), (lnbt[:], lnb_d),
        (ident[:], ident_d), (ones[:], ones_d), (perm[:], perm_d), (mk[:], mk_d),
    ]
    for (o_, i_) in cl:
        dma("pool", o_, i_, w=[const_t])
    P.op("dve", lambda e: e.memset(epsc[:, 0:1], EPS), w=[eps_t])
    P.op("dve", lambda e: e.memset(epsc[:, 1:2], LN_EPS), w=[eps_t])
    P.op("dve", lambda e: e.tensor_scalar(out=gsbc[:], in0=gsbc[:], scalar1=1.0 - LAMBDA_INIT, scalar2=None,
                                          op0=ALU.mult), r=[const_t], w=[const_t])
    LS = 8
    for k, (da, db) in enumerate(((lq1_d, lk1_d), (lq2_d, lk2_d))):
        ta, ta_t = new_tmp()
        tb2, tb2_t = new_tmp()
        dma("pool", ta[:, 0:128], da.partition_broadcast(128), w=[ta_t])
        dma("pool", tb2[:, 0:128], db.partition_broadcast(128), w=[tb2_t])
        P.op("dve", lambda e, ta=ta, tb2=tb2: e.tensor_tensor(out=ta[:, 128:256], in0=ta[:, 0:128], in1=tb2[:, 0:128], op=ALU.mult),
             r=[ta_t, tb2_t], w=[ta_t])
        P.op("dve", lambda e, k=k, ta=ta: e.reduce_sum(out=st[:, LS + k:LS + k + 1], in_=ta[:, 128:256], axis=AX.X),
             r=[ta_t], w=[st_t[LS + k]])
        P.op("act", lambda e, k=k: e.activation(out=st[:, LS + 2 + k:LS + 3 + k], in_=st[:, LS + k:LS + k + 1], func=AF.Exp),
             r=[st_t[LS + k]], w=[st_t[LS + 2 + k]])
    P.op("dve", lambda e: e.tensor_tensor(out=st[:, LS + 4:LS + 5], in0=st[:, LS + 3:LS + 4], in1=st[:, LS + 2:LS + 3],
                                          op=ALU.subtract), r=[st_t[LS + 2], st_t[LS + 3]], w=[st_t[LS + 4]])
    P.op("dve", lambda e: e.tensor_scalar(out=st[:, LS + 5:LS + 6], in0=st[:, LS + 4:LS + 5], scalar1=-LAMBDA_INIT,
                                          scalar2=None, op0=ALU.add), r=[st_t[LS + 4]], w=[lam_t])
    nlam = st[:, LS + 5:LS + 6]

    def rstd_from_ss(ss_ap, ss_t, out_ap, out_t, n, eps):
        np_ = ss_ap.shape[0]
        bcol = epsc[:np_, 0:1] if eps == EPS else epsc[:np_, 1:2]
        P.op("act", lambda e: e.activation(out=out_ap, in_=ss_ap, func=AF.Ln, scale=1.0 / n, bias=bcol),
             r=[ss_t, eps_t], w=[out_t])
        P.op("act", lambda e: e.activation(out=out_ap, in_=out_ap, func=AF.Exp, scale=-0.5),
             r=[out_t], w=[out_t])

    def norm_p1(src_ap, src_ts, np_):
        si = nxt("st", 4)
        ssc, ssc_t = st[:np_, si:si + 1], st_t[si]
        rsc, rsc_t = st[:np_, 4 + si:5 + si], st_t[4 + si]
        xi = nxt("xn", NXN)
        P.op("act", lambda e: e.activation(out=xn[xi][:np_, :], in_=src_ap, func=AF.Square, accum_out=ssc),
             r=list(src_ts), w=[xn_t[xi], ssc_t])
        rstd_from_ss(ssc, ssc_t, rsc, rsc_t, D, EPS)
        P.op("act", lambda e: e.activation(out=xn[xi][:np_, :], in_=src_ap, func=AF.Copy, scale=rsc),
             r=list(src_ts) + [rsc_t], w=[xn_t[xi]])
        return xi

    def norm_p2(xi, np_, gT, col0, halo=False):
        for c0 in range(0, DC, 8):
            n = min(8, DC - c0)
            ti = psT_i["i"] % 2
            psT_i["i"] += 1
            psT, psT_t = psT_l[ti], psT_tl[ti]
            for k in range(n):
                P.op("pe", lambda e: e.transpose(out=psT[:, k * np_:(k + 1) * np_],
                                                 in_=xn[xi][:np_, (c0 + k) * 128:(c0 + k + 1) * 128],
                                                 identity=ident[:np_, :np_]),
                     r=[xn_t[xi], const_t], w=[psT_t])
            P.op("dve", lambda e: e.tensor_tensor(
                out=XT[:, c0:c0 + n, col0:col0 + np_], in0=psT[:, 0:n * np_].rearrange("p (c t) -> p c t", c=n),
                in1=gT[:, c0:c0 + n].unsqueeze(2).to_broadcast([128, n, np_]), op=ALU.mult),
                 r=[psT_t, const_t], w=([XTh_t] if halo else XT_t[c0:c0 + n]))

    def norm_transpose(src_ap, src_ts, np_, gT, col0, halo=False):
        xi = norm_p1(src_ap, src_ts, np_)
        norm_p2(xi, np_, gT, col0, halo)

    sbank = {"i": 0}

    def stream_bank(banks):
        i = sbank["i"]
        sbank["i"] = i + 1
        b = banks[i % len(banks)]
        return psf[b], psf_t[b]

    def qk_part1(ps, ps_t, gcol):
        sq, sq_t = new_tb()
        kg, kg_t = new_tb()
        P.op("act", lambda e: e.activation(out=sq[:], in_=ps[:], func=AF.Square), r=[ps_t], w=[sq_t])
        P.op("act", lambda e: e.activation(out=kg[:], in_=ps[:], func=AF.Copy, scale=gcol), r=[ps_t, const_t], w=[kg_t])
        return (sq, sq_t, kg, kg_t)

    def qk_part2(st1, out_ap, out_ts, banks):
        sq, sq_t, kg, kg_t = st1
        pa, pa_t = stream_bank(banks)
        P.op("pe", lambda e: e.matmul(pa[:], lhsT=ones[:], rhs=sq[:], start=True, stop=True), r=[sq_t, const_t], w=[pa_t])
        pb, pb_t = stream_bank(banks)
        P.op("pe", lambda e: e.matmul(pb[:], lhsT=perm[:], rhs=kg[:], start=True, stop=True), r=[kg_t, const_t], w=[pb_t])
        rs, rs_t = new_tmp()
        rstd_from_ss(pa[:], pa_t, rs[:], rs_t, 128, EPS)
        t1, t1_t = new_tmp()
        t2, t2_t = new_tmp()
        P.op("dve", lambda e: e.tensor_tensor(out=t1[:], in0=kg[:], in1=cosb[:], op=ALU.mult), r=[kg_t, cs_t], w=[t1_t])
        P.op("dve", lambda e: e.tensor_tensor(out=t2[:], in0=pb[:], in1=sinb[:], op=ALU.mult), r=[pb_t, cs_t], w=[t2_t])
        P.op("dve", lambda e: e.tensor_tensor(out=t1[:], in0=t1[:], in1=t2[:], op=ALU.add), r=[t1_t, t2_t], w=[t1_t])
        P.op("dve", lambda e: e.tensor_tensor(out=out_ap, in0=t1[:], in1=rs[:], op=ALU.mult), r=[t1_t, rs_t], w=list(out_ts))

    for ch in range(DC):
        dma("pool", wkv_ap[:, ch, :], win_v[:, ch, c.K0:c.K0 + 2 * c.KW], w=wkvG(ch, 0, 2 * c.KW))

    def prep_slot(n):
        rb, rb_t = ring[n % c.R], ring_t[n % c.R]
        for (key, src, a, b, off) in slots[n]:
            dma("pool", rb[:, off:off + a * b].rearrange("p (a b) -> p a b", a=a), src, w=[rb_t])
        dma("sp", wscr[n][:, 0:slot_used[n]], rb[:, 0:slot_used[n]], r=[rb_t], w=[wscr_t[n]])

    prep_next = {"n": 0}

    def prep_some(k):
        for _ in range(k):
            if prep_next["n"] < NSLOT:
                prep_slot(prep_next["n"])
                prep_next["n"] += 1

    A_BANKS = [0, 1, 2, 3, 4, 5]
    prep_per_tile = (NSLOT + NT - 1) // NT
    pa_xs, pa_xn = {}, {}

    def pa_load(t, k):
        xi = nxt("xs", NXS)
        dma("sp", xs[xi][:], xf[t * 512 + k * 128:t * 512 + (k + 1) * 128, :], w=[xs_t[xi]])
        pa_xs[(t, k)] = xi

    def pa_p1(t, k):
        xi = pa_xs[(t, k)]
        pa_xn[(t, k)] = norm_p1(xs[xi][:], [xs_t[xi]], 128)

    pa_load(0, 0)
    pa_load(0, 1)
    pa_p1(0, 0)
    pa_p1(0, 1)
    pa_load(0, 2)
    pa_load(0, 3)
    pa_p1(0, 2)
    pa_p1(0, 3)
    K_TRIG0, K_TRIG1 = min(2, 2 * H), min(5, 2 * H)
    pre_slots = [n for n in range(NSLOT) if slots[n][0][0][0] in ("g", "u", "d")]
    prepped = set()
    pre_per_tile = (len(pre_slots) + NT - 1) // NT

    def precast(k):
        for _ in range(k):
            if not pre_slots:
                return
            n = pre_slots.pop(0)
            rb, rb_t = ring[n % c.R], ring_t[n % c.R]
            for (key, src, a, b, off) in slots[n]:
                dma("pool", rb[:, off:off + a * b].rearrange("p (a b) -> p a b", a=a), src, w=[rb_t])
            dma("sp", wscr[n][:, 0:slot_used[n]], rb[:, 0:slot_used[n]], r=[rb_t], w=[wscr_t[n]])
            prepped.add(n)

    for t in range(NT):
        precast(pre_per_tile)
        dma("sp", cosb[:], cosk_d[:, t * 512:(t + 1) * 512], w=[cs_t])
        dma("sp", sinb[:], sink_d[:, t * 512:(t + 1) * 512], w=[cs_t])
        for sub in range(4):
            norm_p2(pa_xn[(t, sub)], 128, g1T, 32 + sub * 128)
        if t + 1 < NT:
            pa_load(t + 1, 0)
            pa_load(t + 1, 1)
        pend = None
        for hh in range(2 * H + 1):
            cur = None
            if hh < 2 * H:
                ps, ps_t = stream_bank(A_BANKS)
                for ch in range(DC):
                    P.op("pe", lambda e: e.matmul(ps[:], lhsT=wkv_ap[:, ch, hh * 128:(hh + 1) * 128],
                                                  rhs=XT[:, ch, 32:544], start=(ch == 0), stop=(ch == DC - 1)),
                         r=[XT_t[ch]] + wkvG(ch, hh * 128, 128), w=[ps_t])
                cur = (hh, qk_part1(ps, ps_t, kgc[:, 0:1]))
            if pend is not None:
                ph_, st1 = pend
                h, cpart = ph_ // 2, ph_ % 2
                ki = (t * H + h) % 2
                ks_ap, ks_g = kst(ki)
                qk_part2(st1, ks_ap[:, cpart, :], ks_g, A_BANKS)
                if cpart == 1:
                    dma("sp", kts[h * NKB + t].rearrange("p (c k) -> p c k", c=2), ks_ap, r=ks_g, w=[kts_t[h * NKB + t]])
            pend = cur
            if t + 1 < NT:
                if hh == K_TRIG0:
                    pa_p1(t + 1, 0)
                    pa_load(t + 1, 2)
                if hh == K_TRIG1:
                    pa_p1(t + 1, 1)
                    pa_load(t + 1, 3)
        VW = min(512, c.KW)
        for sub in range(4):
            for half in range(c.KW // VW):
                ps, ps_t = stream_bank(A_BANKS)
                for ch in range(DC):
                    P.op("pe", lambda e, ps=ps, ch=ch, sub=sub, half=half: e.matmul(
                        ps[:, 0:VW], lhsT=XT[:, ch, 32 + sub * 128:32 + (sub + 1) * 128],
                        rhs=wkv_ap[:, ch, c.KW + half * VW:c.KW + (half + 1) * VW], start=(ch == 0), stop=(ch == DC - 1)),
                         r=[XT_t[ch]] + wkvG(ch, c.KW + half * VW, VW), w=[ps_t])
                P.op("act", lambda e, ps=ps, sub=sub, half=half: e.activation(out=vst[:, sub, half * VW:(half + 1) * VW],
                                                                               in_=ps[:, 0:VW], func=AF.Copy),
                     r=[ps_t], w=vst_g)
            if t + 1 < NT and sub == 0:
                pa_p1(t + 1, 2)
            if t + 1 < NT and sub == 2:
                pa_p1(t + 1, 3)
        for h in range(H):
            dma("sp", vs[h * NKB + t].rearrange("p (k e) -> p k e", k=4), vst[:, :, h * 256:(h + 1) * 256],
                r=vst_g, w=[vs_t[h * NKB + t]])

    wnext = {"g": 0}
    TOT = NOWN * NSLOT

    def wload(gi):
        n = gi % NSLOT
        rb, rb_t = ring[gi % c.R], ring_t[gi % c.R]
        if gi < NSLOT and n not in prepped:
            for (key, src, a, b, off) in slots[n]:
                dma("pool", rb[:, off:off + a * b].rearrange("p (a b) -> p a b", a=a), src, w=[rb_t])
            dma("sp", wscr[n][:, 0:slot_used[n]], rb[:, 0:slot_used[n]], r=[rb_t], w=[wscr_t[n]])
        else:
            dma("sp", rb[:, 0:slot_used[n]], wscr[n][:, 0:slot_used[n]], r=[wscr_t[n]], w=[rb_t])

    def wblock(j, key):
        n, off, a, b = windex[key]
        gi = j * NSLOT + n
        while wnext["g"] <= min(gi + c.R - 1, TOT - 1):
            wload(wnext["g"])
            wnext["g"] += 1
        return ring[gi % c.R][:, off:off + a * b].rearrange("p (a b) -> p a b", a=a), ring_t[gi % c.R]

    kvseq = []
    for j in range(NOWN):
        for h in range(H):
            for kb in range(2 * j + 2):
                kvseq.append((j, h, kb))
    kvpos = {k: i for i, k in enumerate(kvseq)}
    kvnext = {"g": 0}

    def kvload(gi):
        (j, h, kb) = kvseq[gi]
        r_ = gi % c.KR
        k_ap, k_g = ktr(r_)
        v_ap, v_g = vr(r_)
        dma("pool", k_ap, kts[h * NKB + kb].rearrange("p (c k) -> p c k", c=2), r=[kts_t[h * NKB + kb]], w=k_g)
        dma("pool", v_ap, vs[h * NKB + kb].rearrange("p (k e) -> p k e", k=4), r=[vs_t[h * NKB + kb]], w=v_g)

    def kvblock(j, h, kb):
        gi = kvpos[(j, h, kb)]
        last_j = kvpos[(j, H - 1, 2 * j + 1)]
        while kvnext["g"] <= min(gi + c.KR - 2, last_j):
            kvload(kvnext["g"])
            kvnext["g"] += 1
        r_ = gi % c.KR
        k_ap, k_g = ktr(r_)
        v_ap, v_g = vr(r_)
        return k_ap, v_ap, k_g + v_g

    S_BANKS = [4, 5, 6]
    Q_BANKS = [0, 1, 2, 3, 4, 5]
    ESCALE = 1.0 / math.sqrt(128.0)
    out_ts = []

    def tile_prologue_steps(jn):
        stt = {}

        def load(k, rows):
            xi = nxt("xs", NXS)
            if k == 0:
                dma("sp", xs[xi][0:32, :], xo[jn, 0:32, :], w=[xs_t[xi]])
            else:
                dma("sp", xs[xi][:], xo[jn, 32 + (k - 1) * 128:32 + k * 128, :], w=[xs_t[xi]])
            stt[("xs", k)] = xi

        def p1(k):
            xi = stt[("xs", k)]
            if k == 0:
                stt[("xn", k)] = norm_p1(xs[xi][0:32, :], [xs_t[xi]], 32)
            else:
                stt[("xn", k)] = norm_p1(xs[xi][:], [xs_t[xi]], 128)

        def p2(k):
            if k == 0:
                norm_p2(stt[("xn", k)], 32, g1T, 0, halo=True)
            else:
                norm_p2(stt[("xn", k)], 128, g1T, 32 + (k - 1) * 128)

        def s0():
            dma("sp", cosb[:], cosq_d[jn], w=[cs_t])
            dma("sp", sinb[:], sinq_d[jn], w=[cs_t])
            dma("pool", mqb[:], mq_d[:, jn * 512:(jn + 1) * 512], w=[mqb_t])
            load(0, 32)
            load(1, 128)
            p1(0)

        def mk_step(k):
            def f():
                p2(k - 1)
                if k + 1 <= 4:
                    load(k + 1, 128)
                p1(k)
            return f

        return [s0] + [mk_step(k) for k in range(1, 5)] + [lambda: p2(4)]

    for j in range(NOWN):
        if j == 0:
            for stp in tile_prologue_steps(0):
                stp()
        dbg("xnT%d" % j, XT[:], XT_t + [XTh_t], [128, DC, 544], BF16)
        pend = None
        for hh in range(2 * H + 1):
            cur = None
            if hh < 2 * H:
                wq, wq_t = wblock(j, ("q", hh))
                ps, ps_t = stream_bank(Q_BANKS)
                for ch in range(DC):
                    P.op("pe", lambda e: e.matmul(ps[:], lhsT=wq[:, ch, :], rhs=XT[:, ch, 32:544],
                                                  start=(ch == 0), stop=(ch == DC - 1)),
                         r=[XT_t[ch], wq_t], w=[ps_t])
                cur = (hh, qk_part1(ps, ps_t, qgc[:, 0:1]))
            if pend is not None:
                ph_, st1 = pend
                q_ap, q_g = QT(ph_)
                qk_part2(st1, q_ap, q_g, Q_BANKS)
            pend = cur
        def conv_chunk(cc):
            u_ap, u_g = uext(cc)
            y_ap, y_g = ycv(cc)
            P.op("dve", lambda e: e.tensor_scalar(
                out=y_ap, in0=u_ap[:, 2:514], scalar1=cwt[:, cc * CW:cc * CW + 1], scalar2=cbt[:, cc:cc + 1],
                op0=ALU.mult, op1=ALU.add), r=u_g + [const_t], w=y_g)
            for tap in range(1, CW):
                P.op("dve", lambda e: e.scalar_tensor_tensor(
                    out=y_ap, in0=u_ap[:, 2 + tap:514 + tap], scalar=cwt[:, cc * CW + tap:cc * CW + tap + 1],
                    in1=y_ap, op0=ALU.mult, op1=ALU.add), r=u_g + y_g, w=y_g)

        for cc in range(CC):
            wa, wa_t = wblock(j, ("ga", cc))
            wg_, wg_t = wblock(j, ("gg", cc))
            pa, pa_t = stream_bank(Q_BANKS)
            pg, pg_t = stream_bank(Q_BANKS)
            ph, ph_t = stream_bank(Q_BANKS)
            for (w_, w_t, po, po_t, col) in ((wa, wa_t, pa, pa_t, 0), (wg_, wg_t, pg, pg_t, 32)):
                for ch in range(DC):
                    P.op("pe", lambda e, po=po, ch=ch, w_=w_: e.matmul(po[:], lhsT=w_[:, ch, :], rhs=XT[:, ch, 32:544],
                                                                       start=(ch == 0), stop=(ch == DC - 1)),
                         r=[XT_t[ch], w_t], w=[po_t])
                for ch in range(DC):
                    P.op("pe", lambda e, ch=ch, w_=w_, col=col, ph=ph: e.matmul(ph[:, col:col + 32], lhsT=w_[:, ch, :], rhs=XT[:, ch, 0:32],
                                                                               start=(ch == 0), stop=(ch == DC - 1)),
                         r=[XTh_t, w_t], w=[ph_t])
            u_ap, u_g = uext(cc)
            sg, sg_t = new_tmp()
            P.op("act", lambda e, sg=sg, pg=pg: e.activation(out=sg[:], in_=pg[:], func=AF.Sigmoid), r=[pg_t], w=[sg_t])
            P.op("dve", lambda e, sg=sg, pa=pa, u_ap=u_ap: e.tensor_tensor(out=u_ap[:, 32:544], in0=pa[:], in1=sg[:], op=ALU.mult),
                 r=[pa_t, sg_t], w=u_g)
            sh, sh_t = new_tmp()
            P.op("act", lambda e, sh=sh, ph=ph: e.activation(out=sh[:, 0:32], in_=ph[:, 32:64], func=AF.Sigmoid), r=[ph_t], w=[sh_t])
            P.op("dve", lambda e, sh=sh, ph=ph, u_ap=u_ap: e.tensor_tensor(out=u_ap[:, 0:32], in0=ph[:, 0:32], in1=sh[:, 0:32], op=ALU.mult),
                 r=[ph_t, sh_t], w=u_g)
        dbg("QT%d" % j, ar[:, QT0:QT0 + 2 * H * 512], G(QT0, 2 * H * 512), [128, 2 * H * 512], BF16)
        dbg("uext%d" % j, ar[:, 0:CC * 1536], G(0, CC * 1536), [128, CC * 1536], BF16)
        conv_done = 0
        per_head = (CC + H - 1) // H
        for h in range(H):
            for _ in range(per_head):
                if conv_done < CC:
                    conv_chunk(conv_done)
                    conv_done += 1
            nkb = 2 * j + 2
            its = [(kb, kti, cpart) for kb in range(nkb) for kti in range(4) for cpart in range(2)]
            blk = {}

            def emit_score(idx):
                kb, kti, cpart = its[idx]
                if kb not in blk:
                    blk[kb] = kvblock(j, h, kb)
                kt_, v_, kv_g = blk[kb]
                masked = kb >= 2 * j
                sbk, sbk_t = psf[6 + idx % 2], psf_t[6 + idx % 2]
                q_ap, q_g = QT(2 * h + cpart)
                P.op("pe", lambda e: e.matmul(sbk[:], lhsT=kt_[:, cpart, kti * 128:(kti + 1) * 128], rhs=q_ap,
                                              start=True, stop=(not masked)), r=kv_g + q_g, w=[sbk_t])
                if masked:
                    wdx = (kb - 2 * j) * 4 + kti
                    P.op("pe", lambda e: e.matmul(sbk[:], lhsT=mk[:, wdx * 128:(wdx + 1) * 128], rhs=mqb[:],
                                                  start=False, stop=True), r=[const_t, mqb_t], w=[sbk_t])
                pi = idx % NPT
                P.op("act", lambda e: e.activation(out=pt[pi][:], in_=sbk[:], func=AF.Exp, scale=ESCALE),
                     r=[sbk_t], w=[pt_t[pi]])

            def emit_pv(idx):
                kb, kti, cpart = its[idx]
                kt_, v_, kv_g = blk[kb]
                pi = idx % NPT
                first = (kb == 0 and kti == 0)
                last = (kb == nkb - 1) and (kti == 3)
                for ec in range(2):
                    P.op("pe", lambda e: e.matmul(psf[2 * cpart + ec][:], lhsT=v_[:, kti, ec * 128:(ec + 1) * 128], rhs=pt[pi][:],
                                                  start=first, stop=last), r=[pt_t[pi]] + kv_g, w=[psf_t[2 * cpart + ec]])
                P.op("pe", lambda e: e.matmul(psf[4 + cpart][:], lhsT=ones[:], rhs=pt[pi][:], start=first, stop=last),
                     r=[pt_t[pi], const_t], w=[psf_t[4 + cpart]])

            emit_score(0)
            for idx in range(len(its)):
                if idx + 1 < len(its):
                    emit_score(idx + 1)
                emit_pv(idx)
            r1, r1_t = new_tmp()
            r2, r2_t = new_tmp()
            P.op("dve", lambda e: e.reciprocal(out=r1[:], in_=psf[4][:]), r=[psf_t[4]], w=[r1_t])
            P.op("dve", lambda e: e.reciprocal(out=r2[:], in_=psf[5][:]), r=[psf_t[5]], w=[r2_t])
            P.op("dve", lambda e: e.tensor_scalar(out=r2[:], in0=r2[:], scalar1=nlam, scalar2=None, op0=ALU.mult),
                 r=[r2_t, lam_t], w=[r2_t])
            atts = []
            for ec in range(2):
                a1_, a1_t = new_tmp()
                a2_, a2_t = new_tmp()
                P.op("dve", lambda e: e.tensor_tensor(out=a1_[:], in0=psf[ec][:], in1=r1[:], op=ALU.mult), r=[psf_t[ec], r1_t], w=[a1_t])
                P.op("dve", lambda e: e.tensor_tensor(out=a2_[:], in0=psf[2 + ec][:], in1=r2[:], op=ALU.mult), r=[psf_t[2 + ec], r2_t], w=[a2_t])
                P.op("dve", lambda e: e.tensor_tensor(out=a1_[:], in0=a1_[:], in1=a2_[:], op=ALU.add), r=[a1_t, a2_t], w=[a1_t])
                sq_, sq_t = new_tb()
                P.op("act", lambda e: e.activation(out=sq_[:], in_=a1_[:], func=AF.Square), r=[a1_t], w=[sq_t])
                P.op("pe", lambda e: e.matmul(psf[4][:], lhsT=ones[:], rhs=sq_[:], start=(ec == 0), stop=(ec == 1)),
                     r=[sq_t, const_t], w=[psf_t[4]])
                atts.append((a1_, a1_t))
            rs_, rs_t = new_tmp()
            rstd_from_ss(psf[4][:], psf_t[4], rs_[:], rs_t, 256, EPS)
            for ec in range(2):
                a1_, a1_t = atts[ec]
                P.op("dve", lambda e: e.scalar_tensor_tensor(out=XT[:, 2 * h + ec, 32:544], in0=a1_[:], scalar=gsbc[:, ec:ec + 1],
                                                             in1=rs_[:], op0=ALU.mult, op1=ALU.mult),
                     r=[a1_t, rs_t, const_t], w=[XT_t[2 * h + ec]])
        while conv_done < CC:
            conv_chunk(conv_done)
            conv_done += 1
        psum_s, psum_s_t = psf[0], psf_t[0]
        psum_q, psum_q_t = psf[1], psf_t[1]
        for cc in range(CC):
            y_ap, y_g = ycv(cc)
            yb, yb_t = new_tb()
            y2, y2_t = new_tb()
            P.op("act", lambda e: e.activation(out=yb[:], in_=y_ap, func=AF.Copy), r=y_g, w=[yb_t])
            P.op("act", lambda e: e.activation(out=y2[:], in_=y_ap, func=AF.Square), r=y_g, w=[y2_t])
            P.op("pe", lambda e: e.matmul(psum_s[:], lhsT=ones[:], rhs=yb[:], start=(cc == 0), stop=(cc == CC - 1)),
                 r=[yb_t, const_t], w=[psum_s_t])
            P.op("pe", lambda e: e.matmul(psum_q[:], lhsT=ones[:], rhs=y2[:], start=(cc == 0), stop=(cc == CC - 1)),
                 r=[y2_t, const_t], w=[psum_q_t])
        mean, mean_t = new_tmp()
        msq, msq_t = new_tmp()
        rsd, rsd_t = new_tmp()
        P.op("dve", lambda e: e.tensor_scalar(out=mean[:], in0=psum_s[:], scalar1=1.0 / c.C, scalar2=None, op0=ALU.mult),
             r=[psum_s_t], w=[mean_t])
        P.op("dve", lambda e: e.tensor_tensor(out=msq[:], in0=mean[:], in1=mean[:], op=ALU.mult), r=[mean_t], w=[msq_t])
        P.op("dve", lambda e: e.scalar_tensor_tensor(out=msq[:], in0=psum_q[:], scalar=1.0 / c.C, in1=msq[:], op0=ALU.mult,
                                                     op1=ALU.subtract), r=[psum_q_t, msq_t], w=[msq_t])
        rstd_from_ss(msq[:], msq_t, rsd[:], rsd_t, 1.0, LN_EPS)
        for cc in range(CC):
            y_ap, y_g = ycv(cc)
            P.op("dve", lambda e: e.tensor_tensor(out=y_ap, in0=y_ap, in1=mean[:], op=ALU.subtract),
                 r=y_g + [mean_t], w=y_g)
            P.op("dve", lambda e: e.tensor_tensor(out=y_ap, in0=y_ap, in1=rsd[:], op=ALU.mult),
                 r=y_g + [rsd_t], w=y_g)
            P.op("act", lambda e: e.activation(out=XT[:, AC + cc, 32:544], in_=y_ap, func=AF.Silu,
                                               scale=lngt[:, cc:cc + 1], bias=lnbt[:, cc:cc + 1]),
                 r=y_g + [const_t], w=[XT_t[AC + cc]])
        dbg("mixT%d" % j, XT[:], XT_t + [XTh_t], [128, DC, 544], BF16)
        for q in range(4):
            h_ap, h_g = hT(q)
            dma("sp", h_ap, xo[j, 32 + q * 128:32 + (q + 1) * 128, :], w=h_g)
        for dg in range(c.NDG):
            for half in range(2):
                wo, wo_t = wblock(j, ("o", dg, half))
                for q in range(4):
                    pb_, pb_t = psf[q], psf_t[q]
                    for m in range(DC // 2):
                        mc = half * (DC // 2) + m
                        P.op("pe", lambda e, pb_=pb_, mc=mc, m=m, q=q, wo=wo: e.matmul(
                            pb_[:], lhsT=XT[:, mc, 32 + q * 128:32 + (q + 1) * 128], rhs=wo[:, m, :],
                            start=(m == 0), stop=(m == DC // 2 - 1)), r=[XT_t[mc], wo_t], w=[pb_t])
                    h_ap, h_g = hT(q, dg)
                    P.op("dve", lambda e, pb_=pb_, h_ap=h_ap: e.tensor_tensor(out=h_ap, in0=pb_[:], in1=h_ap, op=ALU.add),
                         r=[pb_t] + h_g, w=h_g)
        dbg("h%d" % j, ar[:, 0:A1], G(0, A1), [128, A1], BF16)
        for q in range(4):
            h_ap, h_g = hT(q)
            norm_transpose(h_ap, h_g, 128, g2T, 32 + q * 128)
        nsteps = tile_prologue_steps(j + 1) if j + 1 < NOWN else []
        if nsteps:
            nsteps.pop(0)()
        F_PAIRS = [(0, 1), (2, 3), (4, 5)]
        for fc in range(FC):
            wg2, wg2_t = wblock(j, ("g", fc))
            wu2, wu2_t = wblock(j, ("u", fc))
            ba, bb = F_PAIRS[fc % 3]
            for (w_, w_t, b_) in ((wg2, wg2_t, ba), (wu2, wu2_t, bb)):
                for ch in range(DC):
                    P.op("pe", lambda e, b_=b_, ch=ch, w_=w_: e.matmul(psf[b_][:], lhsT=w_[:, ch, :], rhs=XT[:, ch, 32:544],
                                                                       start=(ch == 0), stop=(ch == DC - 1)),
                         r=[XT_t[ch], w_t], w=[psf_t[b_]])
            sl, sl_t = new_tmp()
            a_ap, a_g = actT(fc)
            P.op("act", lambda e, sl=sl, ba=ba: e.activation(out=sl[:], in_=psf[ba][:], func=AF.Silu), r=[psf_t[ba]], w=[sl_t])
            P.op("dve", lambda e, sl=sl, bb=bb, a_ap=a_ap: e.tensor_tensor(out=a_ap, in0=psf[bb][:], in1=sl[:], op=ALU.mult),
                 r=[psf_t[bb], sl_t], w=a_g)
        dbg("actT%d" % j, ar[:, A1:A1 + FC * 512], G(A1, FC * 512), [128, FC * 512], BF16)
        for dg in range(c.NDG):
            if nsteps:
                nsteps.pop(0)()
            for (dg2, f0, nf) in dblocks:
                if dg2 != dg:
                    continue
                wd, wd_t = wblock(j, ("d", dg, f0))
                for f in range(nf):
                    fc = f0 + f
                    a_ap, a_g = actT(fc)
                    for q in range(4):
                        P.op("pe", lambda e, q=q, fc=fc, f=f, wd=wd, a_ap=a_ap: e.matmul(
                            psf[q][:], lhsT=a_ap[:, q * 128:(q + 1) * 128], rhs=wd[:, f, :],
                            start=(fc == 0), stop=(fc == FC - 1)), r=a_g + [wd_t], w=[psf_t[q]])
            for q in range(4):
                h_ap, h_g = hT(q, dg)
                P.op("dve", lambda e, q=q, h_ap=h_ap: e.tensor_tensor(out=h_ap, in0=psf[q][:], in1=h_ap, op=ALU.add),
                     r=[psf_t[q]] + h_g, w=h_g)
        while nsteps:
            nsteps.pop(0)()
        for q in range(4):
            h_ap, h_g = hT(q)
            ot = T("y%d_%d" % (j, q))
            out_ts.append(ot)
            dma("sp", y_d[j * 512 + q * 128:j * 512 + (q + 1) * 128, :], h_ap, r=h_g, w=[ot])
    P.op("pool", lambda e: e.engine_nop(), r=out_ts + dbg_ts)
    P.emit(nc, es)
    es.close()
    return nc


def _tables(cfg):
    S = cfg.S
    inv = (np.float32(10000.0) ** (-(np.arange(0, 128, 2, dtype=np.float32) / np.float32(128)))).astype(np.float32)
    ang = np.arange(S, dtype=np.float32)[:, None] * inv[None, :]
    ang = np.concatenate([ang, ang], axis=-1)
    cosT = np.ascontiguousarray(np.cos(ang).astype(np.float32).T)
    sinT = np.ascontiguousarray(np.sin(ang).astype(np.float32).T)
    ident = np.eye(128, dtype=np.float32)
    ones = np.ones((128, 128), dtype=np.float32)
    perm = np.zeros((128, 128), dtype=np.float32)
    for d in range(64):
        perm[d + 64, d] = -1.0
        perm[d, d + 64] = 1.0
    mk = np.zeros((16, 1024), dtype=np.float32)
    for i in range(1024):
        mk[i // 64, i] = 1.0
    return cosT, sinT, ident, ones, perm, mk


def make_in_maps(cfg, inputs):
    c = cfg
    f32 = np.float32
    x = np.asarray(inputs["x"], dtype=f32)
    cosT, sinT, ident, ones, perm, mk = _tables(c)

    def colT(v, n):
        return np.ascontiguousarray(np.asarray(v, dtype=f32).reshape(n, 128).T)

    shared = {
        "w_in": np.ascontiguousarray(np.asarray(inputs["w_in"], dtype=f32)[0]),
        "w_out": np.ascontiguousarray(np.asarray(inputs["w_out"], dtype=f32)[0]),
        "w_gate": np.ascontiguousarray(np.asarray(inputs["w_gate"], dtype=f32)[0]),
        "w_up": np.ascontiguousarray(np.asarray(inputs["w_up"], dtype=f32)[0]),
        "w_down": np.ascontiguousarray(np.asarray(inputs["w_down"], dtype=f32)[0]),
        "norm1_g": colT(inputs["norm1_g"][0], c.DC),
        "norm2_g": colT(inputs["norm2_g"][0], c.DC),
        "q_norm_g": np.ascontiguousarray(np.asarray(inputs["q_norm_g"], dtype=f32)[0].reshape(128, 1)),
        "k_norm_g": np.ascontiguousarray(np.asarray(inputs["k_norm_g"], dtype=f32)[0].reshape(128, 1)),
        "lambda_q1": np.ascontiguousarray(np.asarray(inputs["lambda_q1"], dtype=f32)[0]),
        "lambda_k1": np.ascontiguousarray(np.asarray(inputs["lambda_k1"], dtype=f32)[0]),
        "lambda_q2": np.ascontiguousarray(np.asarray(inputs["lambda_q2"], dtype=f32)[0]),
        "lambda_k2": np.ascontiguousarray(np.asarray(inputs["lambda_k2"], dtype=f32)[0]),
        "subln_g": colT(inputs["subln_g"][0], 2),
        "conv_w": np.ascontiguousarray(np.asarray(inputs["conv_w"], dtype=f32)[0, :, 0, :].T.reshape(c.CC, 128, CW)
                                       .transpose(1, 0, 2).reshape(128, c.CC * CW)),
        "conv_b": colT(inputs["conv_b"][0], c.CC),
        "conv_ln_g": colT(inputs["conv_ln_g"][0], c.CC),
        "conv_ln_b": colT(inputs["conv_ln_b"][0], c.CC),
        "cosk": cosT, "sink": sinT, "ident": ident, "ones": ones, "perm": perm, "mk": mk,
    }
    maps = []
    for core in range(c.NCORES):
        b, p = core // 2, core % 2
        xo = np.zeros((c.NOWN, 544, c.D), dtype=f32)
        cosq = np.zeros((c.NOWN, 128, 512), dtype=f32)
        sinq = np.zeros((c.NOWN, 128, 512), dtype=f32)
        mq = np.zeros((16, c.NOWN * 512), dtype=f32)
        for j in range(c.NOWN):
            g = own_tile(p, j)
            xo[j, 32:] = x[b, g * 512:(g + 1) * 512]
            if g > 0:
                xo[j, :32] = x[b, g * 512 - 32:g * 512]
            cosq[j] = cosT[:, g * 512:(g + 1) * 512]
            sinq[j] = sinT[:, g * 512:(g + 1) * 512]
            for s in range(2):
                qc = (g - 2 * j) * 8 + s * 4 + np.arange(256) // 64
                for r_ in range(16):
                    mq[r_, j * 512 + s * 256:j * 512 + (s + 1) * 256] = np.where(qc < r_, NEG_BIG, 0.0)
        m = dict(shared)
        m["xf"] = np.ascontiguousarray(x[b])
        m["xo"] = xo
        m["cosq"] = cosq
        m["sinq"] = sinq
        m["mq"] = mq
        maps.append(m)
    return maps


def gather_out(cfg, results, dtype=np.float32):
    c = cfg
    out = np.zeros((c.B, c.S, c.D), dtype=dtype)
    for core in range(c.NCORES):
        b, p = core // 2, core % 2
        y = np.asarray(results[core]["y"])
        for j in range(c.NOWN):
            g = own_tile(p, j)
            out[b, g * 512:(g + 1) * 512] = y[j * 512:(j + 1) * 512]
    return out


_NC_CACHE = {}


def run_cfg(cfg, inputs):
    key = (cfg.D, cfg.H, cfg.F, cfg.S, cfg.B)
    if key not in _NC_CACHE:
        _NC_CACHE[key] = build(cfg)
    nc = _NC_CACHE[key]
    maps = make_in_maps(cfg, inputs)
    res = run_bass_kernel_spmd(nc, maps, core_ids=list(range(cfg.NCORES)))
    return gather_out(cfg, res.results)


def kernel(**inputs):
    cfg = Cfg()
    return run_cfg(cfg, inputs)
```
